# Optimizing a Trainium2 kernel written in Bass

```python
import math
import jax
import jax.numpy as jnp
from jax import lax
import numpy as np

D_MODEL = 1024
BATCH = 4
SEQ = 4096
DEPTH = 4

CHUNK = 64
N_MIXERS = 2
CONV_WIDTH = 4
EPS = 1e-6

D_RNN = D_MODEL
LRU_BLOCKS = 8
LRU_BLOCK_W = D_RNN // LRU_BLOCKS
RG_C = 8.0

GDN_HEAD_DIM = 128
GDN_HEADS = max(4, D_MODEL // GDN_HEAD_DIM)
GDN_DK = GDN_HEAD_DIM
GDN_DV = GDN_HEAD_DIM
GDN_HK = GDN_HEADS * GDN_DK
GDN_HV = GDN_HEADS * GDN_DV
GDN_PROJ = 2 * GDN_HK + 2 * GDN_HV + 2 * GDN_HEADS

kernel_name = "hybrid_rglru_gdn_adaln_trunk"


def rmsnorm(x, g):
    xf = x.astype(jnp.float32)
    y = xf * lax.rsqrt(jnp.mean(xf * xf, axis=-1, keepdims=True) + EPS)
    return (y * g.astype(jnp.float32)).astype(x.dtype)


def l2norm(x):
    return x * lax.rsqrt(jnp.sum(x * x, axis=-1, keepdims=True) + EPS)


def causal_dwconv(x, w):
    k = w.shape[0]
    s = x.shape[1]
    xp = jnp.pad(x, ((0, 0), (k - 1, 0), (0, 0)))
    out = xp[:, 0:s] * w[0]
    for j in range(1, k):
        out = out + xp[:, j:j + s] * w[j]
    return out


def _lin_combine(e1, e2):
    a1, b1 = e1
    a2, b2 = e2
    return a1 * a2, a2 * b1 + b2


def rglru_block(h, in_w, conv_w, conv_b, gate_w, gate_b, lam, out_w):
    bsz, s, _ = h.shape
    proj = h @ in_w
    xb, zg = proj[..., :D_RNN], proj[..., D_RNN:]
    xb = causal_dwconv(xb, conv_w) + conv_b
    xf = xb.astype(jnp.float32)
    xg = xf.reshape(bsz, s, LRU_BLOCKS, LRU_BLOCK_W)
    pre = jnp.einsum('bsni,knij->kbsnj', xg, gate_w.astype(jnp.float32))
    pre = pre.reshape(2, bsz, s, D_RNN) + gate_b.astype(jnp.float32)[:, None, None, :]
    gates = jax.nn.sigmoid(pre)
    r_t, i_t = gates[0], gates[1]
    log_a = -RG_C * r_t * jax.nn.softplus(-lam.astype(jnp.float32))
    a_t = jnp.exp(log_a)
    mult = jnp.sqrt(-jnp.expm1(2.0 * log_a))
    b_t = mult * (i_t * xf)
    _, hseq = lax.associative_scan(_lin_combine, (a_t, b_t), axis=1)
    y = hseq.astype(h.dtype) * jax.nn.silu(zg)
    return y @ out_w


def chunk_gated_delta_rule(q, k, v, g, beta):
    bsz, s, nh, dk = q.shape
    dv = v.shape[-1]
    n = s // CHUNK

    def to_chunks(t):
        return t.reshape(bsz, n, CHUNK, nh, -1).transpose(0, 3, 1, 2, 4)

    q, k, v = to_chunks(q), to_chunks(k), to_chunks(v)
    g = g.reshape(bsz, n, CHUNK, nh).transpose(0, 3, 1, 2)
    beta = beta.reshape(bsz, n, CHUNK, nh).transpose(0, 3, 1, 2)
    g = jnp.cumsum(g, axis=-1)
    idx = jnp.arange(CHUNK)
    causal = idx[:, None] >= idx[None, :]
    strict = idx[:, None] > idx[None, :]
    diff = g[..., :, None] - g[..., None, :]
    decay_mask = jnp.exp(jnp.where(causal, diff, -jnp.inf))
    k_beta = k * beta[..., None]
    v_beta = v * beta[..., None]
    a_mat = jnp.where(strict, jnp.einsum('bhncd,bhnmd->bhncm', k_beta, k) * decay_mask, 0.0)
    eye = jnp.eye(CHUNK, dtype=jnp.float32)
    t_mat = lax.linalg.triangular_solve(a_mat + eye, jnp.broadcast_to(eye, a_mat.shape),
                                        left_side=True, lower=True)
    w = jnp.einsum('bhncm,bhnmd->bhncd', t_mat, k_beta * jnp.exp(g)[..., None])
    u = jnp.einsum('bhncm,bhnmd->bhncd', t_mat, v_beta)
    attn = jnp.where(causal, jnp.einsum('bhncd,bhnmd->bhncm', q, k) * decay_mask, 0.0)
    g_last = g[..., -1]
    q_dec = q * jnp.exp(g)[..., None]
    k_dec = k * jnp.exp(g_last[..., None] - g)[..., None]
    e_last = jnp.exp(g_last)

    def step(state, inp):
        w_c, u_c, attn_c, q_c, k_c, e_c = inp
        v_new = u_c - jnp.einsum('bhcd,bhde->bhce', w_c, state)
        o_c = jnp.einsum('bhcd,bhde->bhce', q_c, state) + jnp.einsum('bhcm,bhme->bhce', attn_c, v_new)
        state = state * e_c[..., None, None] + jnp.einsum('bhcd,bhce->bhde', k_c, v_new)
        return state, o_c

    def lead(t):
        return jnp.moveaxis(t, 2, 0)

    s0 = jnp.zeros((bsz, nh, dk, dv), jnp.float32)
    _, o = lax.scan(step, s0, (lead(w), lead(u), lead(attn), lead(q_dec), lead(k_dec), lead(e_last)))
    return o.transpose(1, 0, 3, 2, 4).reshape(bsz, s, nh, dv)


def gated_deltanet_block(h, in_w, conv_w, a_log, dt_bias, onorm_g, out_w):
    bsz, s, _ = h.shape
    proj = h @ in_w
    o1 = 2 * GDN_HK + GDN_HV
    o2 = o1 + GDN_HV
    qkv = jax.nn.silu(causal_dwconv(proj[..., :o1], conv_w))
    z = proj[..., o1:o2]
    a_in = proj[..., o2:o2 + GDN_HEADS]
    b_in = proj[..., o2 + GDN_HEADS:]
    qkv = qkv.astype(jnp.float32)
    q = qkv[..., :GDN_HK].reshape(bsz, s, GDN_HEADS, GDN_DK)
    k = qkv[..., GDN_HK:2 * GDN_HK].reshape(bsz, s, GDN_HEADS, GDN_DK)
    v = qkv[..., 2 * GDN_HK:].reshape(bsz, s, GDN_HEADS, GDN_DV)
    q = l2norm(q) * (GDN_DK ** -0.5)
    k = l2norm(k)
    beta = jax.nn.sigmoid(b_in.astype(jnp.float32))
    g = -jnp.exp(a_log.astype(jnp.float32)) * jax.nn.softplus(
        a_in.astype(jnp.float32) + dt_bias.astype(jnp.float32))
    o = chunk_gated_delta_rule(q, k, v, g, beta)
    o = o * lax.rsqrt(jnp.mean(o * o, axis=-1, keepdims=True) + EPS) * onorm_g.astype(jnp.float32)
    o = o.astype(h.dtype) * jax.nn.silu(z.reshape(bsz, s, GDN_HEADS, GDN_DV))
    return o.reshape(bsz, s, GDN_HV) @ out_w


def setup_inputs(seed: int = 0) -> dict:
    key = jax.random.key(seed)
    ks = jax.random.split(key, 24)
    n_a = (DEPTH + 1) // 2
    n_b = DEPTH // 2
    f32 = jnp.float32
    nrm = lambda k, shp, sc: jax.random.normal(k, shp, f32) * sc
    x = jax.random.normal(ks[0], (BATCH, SEQ, D_MODEL), f32)
    c = jax.random.normal(ks[1], (BATCH, D_MODEL), f32)
    ada_w = nrm(ks[2], (DEPTH, D_MODEL, 3 * D_MODEL), 0.5 * D_MODEL ** -0.5)
    ada_b = nrm(ks[3], (DEPTH, 3 * D_MODEL), 0.02)
    norm_g = 1.0 + nrm(ks[4], (DEPTH, D_MODEL), 0.02)
    final_g = 1.0 + nrm(ks[5], (D_MODEL,), 0.02)
    lru_in_w = nrm(ks[6], (n_a, D_MODEL, 2 * D_RNN), D_MODEL ** -0.5)
    lru_conv_w = nrm(ks[7], (n_a, CONV_WIDTH, D_RNN), CONV_WIDTH ** -0.5)
    lru_conv_b = nrm(ks[8], (n_a, D_RNN), 0.01)
    lru_gate_w = nrm(ks[9], (n_a, 2, LRU_BLOCKS, LRU_BLOCK_W, LRU_BLOCK_W), LRU_BLOCK_W ** -0.5)
    lru_gate_b = nrm(ks[10], (n_a, 2, D_RNN), 0.01)
    a_target = jax.random.uniform(ks[11], (n_a, D_RNN), f32, 0.9, 0.999)
    s_lam = a_target ** (1.0 / RG_C)
    lru_lambda = jnp.log(s_lam) - jnp.log1p(-s_lam)
    lru_out_w = nrm(ks[12], (n_a, D_RNN, D_MODEL), D_RNN ** -0.5)
    gdn_in_w = nrm(ks[13], (n_b, D_MODEL, GDN_PROJ), D_MODEL ** -0.5)
    gdn_conv_w = nrm(ks[14], (n_b, CONV_WIDTH, 2 * GDN_HK + GDN_HV), CONV_WIDTH ** -0.5)
    gdn_a_log = jnp.log(jax.random.uniform(ks[15], (n_b, GDN_HEADS), f32, 1.0, 16.0))
    dt = jnp.exp(jax.random.uniform(ks[16], (n_b, GDN_HEADS), f32, math.log(1e-3), math.log(1e-1)))
    gdn_dt_bias = dt + jnp.log(-jnp.expm1(-dt))
    gdn_onorm_g = 1.0 + nrm(ks[17], (n_b, GDN_DV), 0.02)
    gdn_out_w = nrm(ks[18], (n_b, GDN_HV, D_MODEL), GDN_HV ** -0.5)
    return {"x": x, "c": c, "ada_w": ada_w, "ada_b": ada_b, "norm_g": norm_g, "final_g": final_g,
            "lru_in_w": lru_in_w, "lru_conv_w": lru_conv_w, "lru_conv_b": lru_conv_b,
            "lru_gate_w": lru_gate_w, "lru_gate_b": lru_gate_b, "lru_lambda": lru_lambda,
            "lru_out_w": lru_out_w,
            "gdn_in_w": gdn_in_w, "gdn_conv_w": gdn_conv_w, "gdn_a_log": gdn_a_log,
            "gdn_dt_bias": gdn_dt_bias, "gdn_onorm_g": gdn_onorm_g, "gdn_out_w": gdn_out_w}


def reference(x, c, ada_w, ada_b, norm_g, final_g,
              lru_in_w, lru_conv_w, lru_conv_b, lru_gate_w, lru_gate_b, lru_lambda, lru_out_w,
              gdn_in_w, gdn_conv_w, gdn_a_log, gdn_dt_bias, gdn_onorm_g, gdn_out_w):
    c_act = jax.nn.silu(c)
    for i in range(DEPTH):
        cond = c_act @ ada_w[i] + ada_b[i]
        shift = cond[:, None, :D_MODEL]
        scale = cond[:, None, D_MODEL:2 * D_MODEL]
        gate = cond[:, None, 2 * D_MODEL:]
        h = rmsnorm(x, norm_g[i]) * (1.0 + scale) + shift
        j = i // N_MIXERS
        if i % N_MIXERS == 0:
            out = rglru_block(h, lru_in_w[j], lru_conv_w[j], lru_conv_b[j], lru_gate_w[j],
                              lru_gate_b[j], lru_lambda[j], lru_out_w[j])
        else:
            out = gated_deltanet_block(h, gdn_in_w[j], gdn_conv_w[j], gdn_a_log[j],
                                       gdn_dt_bias[j], gdn_onorm_g[j], gdn_out_w[j])
        x = x + gate * out
    return rmsnorm(x, final_g)
```

```python
import numpy as np
from contextlib import ExitStack
import concourse.bass as bass
import concourse.mybir as mybir
from concourse.bass_utils import run_bass_kernel_spmd

F32 = mybir.dt.float32
BF16 = mybir.dt.bfloat16
F32R = mybir.dt.float32r
AF = mybir.ActivationFunctionType
ALU = mybir.AluOpType

D = 1024
KC = 8
SEQ = 4096
BATCH = 4
NCORES = 8
EPS = 1e-6
SEM_PER = 30000
NMASK = 10
NBK = 4


class Ctr:
    def __init__(self, nc, es, name, step):
        self.nc, self.es, self.name, self.step = nc, es, name, step
        self.sems = []
        self.n = 0

    def _sem(self, idx):
        while idx >= len(self.sems):
            self.sems.append(self.es.enter_context(self.nc.semaphore(f"{self.name}_{len(self.sems)}")))
        return self.sems[idx]

    def ref(self, n):
        return self._sem((n - 1) // SEM_PER), ((n - 1) % SEM_PER + 1) * self.step

    def next(self):
        self.n += 1
        return self.n


class Buf:
    __slots__ = ("w", "r", "name", "excl")

    def __init__(self, name="", excl=False):
        self.w = None
        self.r = {}
        self.name = name
        self.excl = excl


class Sched:
    ENG = ("sync", "act", "dve", "pool", "pe")

    def __init__(self, nc, es):
        self.nc, self.es = nc, es
        self.streams = {e: [] for e in self.ENG}
        self.ctr = {e: Ctr(nc, es, "c_" + e, 1) for e in self.ENG if e != "sync"}
        self.waited = {e: {} for e in self.ENG}
        self.nops = 0

    def op(self, eng, fn, reads=(), writes=(), dma=None):
        deps = {}

        def add(tok):
            if tok is None:
                return
            c, n = tok
            if deps.get(c, 0) < n:
                deps[c] = n

        ex = [b for b in reads if b.excl]
        if ex:
            reads = [b for b in reads if not b.excl]
            writes = list(writes) + ex
        for b in reads:
            add(b.w)
        for b in writes:
            add(b.w)
            for c, n in b.r.items():
                add((c, n))
        waits = []
        wd = self.waited[eng]
        for c, n in deps.items():
            if eng == "pe" and c is self.ctr["pe"]:
                continue
            if wd.get(c, 0) < n:
                wd[c] = n
                waits.append(c.ref(n))
        if dma is not None:
            c = dma
        else:
            c = self.ctr[eng]
        n = c.next()
        sem, val = c.ref(n)
        inc = c.step
        self.streams[eng].append((waits, fn, sem, inc))
        for b in reads:
            if b.r.get(c, 0) < n:
                b.r[c] = n
        for b in writes:
            b.w = (c, n)
            b.r = {}
        self.nops += 1
        return (c, n)

    def emit(self, final_waits):
        nc = self.nc
        streams = self.streams

        def run(E, name):
            for waits, fn, sem, inc in streams[name]:
                for s, v in waits:
                    E.wait_ge(s, v)
                fn(E).then_inc(sem, inc)

        with nc.Block() as block:
            @block.sync
            def _(E):
                run(E, "sync")
                for c, n in final_waits:
                    s, v = c.ref(n)
                    E.wait_ge(s, v)

            @block.scalar
            def _(E):
                run(E, "act")

            @block.vector
            def _(E):
                run(E, "dve")

            @block.gpsimd
            def _(E):
                run(E, "pool")

            @block.tensor
            def _(E):
                run(E, "pe")


class Prog:
    def __init__(self, S=SEQ, T=1024, layers=(0, 1, 2, 3), final=True, debug=False):
        self.S, self.T, self.layers, self.final = S, T, layers, final
        self.debug = debug
        self.dbg_outs = {}
        self._pipe_counts = {}
        self.NT = S // T
        self.NSUB = T // 512
        self.nc = bass.Bass("TRN2", target_bir_lowering=False)
        self.es = ExitStack()
        self.sch = Sched(self.nc, self.es)
        self._n = 0
        self.build()

    def sb(self, shape, dt=F32, name=None):
        self._n += 1
        return self.es.enter_context(self.nc.sbuf_tensor(f"{name or 't'}{self._n}", list(shape), dt))

    def dram_in(self, name, shape):
        return self.nc.dram_tensor(name, list(shape), F32, kind="ExternalInput").ap()

    def dmactr(self, name):
        self._n += 1
        return Ctr(self.nc, self.es, f"d_{name}{self._n}", 16)

    def dma(self, out, in_, ctr, reads=(), writes=()):
        return self.sch.op("sync", lambda E: E.dma_start(out=out, in_=in_), reads, writes, dma=ctr)

    def act(self, out, in_, func, reads, writes, bias=None, scale=None):
        kw = {}
        if bias is not None:
            kw["bias"] = bias
        if scale is not None:
            kw["scale"] = scale
        return self.sch.op("act", lambda E: E.activation(out=out, in_=in_, func=func, **kw), reads, writes)

    def tt(self, out, in0, in1, op, reads, writes, eng="dve"):
        return self.sch.op(eng, lambda E: E.tensor_tensor(out=out, in0=in0, in1=in1, op=op), reads, writes)

    def ts(self, out, in0, s1, s2, op0, op1, reads, writes, eng="dve"):
        if op1 is None:
            return self.sch.op(eng, lambda E: E.tensor_scalar(out=out, in0=in0, scalar1=s1, scalar2=None, op0=op0),
                               reads, writes)
        return self.sch.op(eng, lambda E: E.tensor_scalar(out=out, in0=in0, scalar1=s1, scalar2=s2, op0=op0, op1=op1),
                           reads, writes)

    def stt(self, out, in0, scalar, in1, op0, op1, reads, writes):
        return self.sch.op("dve", lambda E: E.scalar_tensor_tensor(out=out, in0=in0, scalar=scalar, in1=in1,
                                                                    op0=op0, op1=op1), reads, writes)

    def copy(self, out, in_, reads, writes, eng="pool"):
        if eng == "act":
            return self.act(out, in_, AF.Copy, reads, writes)
        return self.sch.op(eng, lambda E: E.tensor_copy(out=out, in_=in_), reads, writes)

    def mm(self, out, lhsT, rhs, start, stop, reads, writes):
        return self.sch.op("pe", lambda E: E.matmul(out, lhsT, rhs, start=start, stop=stop), reads, writes)

    def tr(self, out, in_, ident, reads, writes):
        return self.sch.op("pe", lambda E: E.transpose(out, in_, ident), reads, writes)

    def memset(self, ap, val, writes, eng="pool"):
        return self.sch.op(eng, lambda E: E.memset(ap, val), (), writes)

    def dbg(self, name, ap, buf, dt=F32):
        if not self.debug:
            return
        t = self.nc.dram_tensor("dbg_" + name, list(ap.shape), dt, kind="ExternalOutput").ap()
        self.dbg_outs[name] = t
        self.dma(t, ap, self.octr, reads=[buf])

    def ps_next(self, ring=None):
        if ring is None:
            i = self._psi
            self._psi = (i + 1) % len(self.psb)
        else:
            base, n = {"F": (0, 2), "B": (2, 3), "B0": (2, 3), "B1": (5, 3),
                       "LF": (0, 2), "LZ": (2, 2), "LB": (4, 4)}[ring]
            key = "B0" if ring == "B" else ring
            j = self._psr.get(key, 0)
            self._psr[key] = (j + 1) % n
            i = base + j
        return self.psb[i], self.psbuf[i]

    def load_w(self, dram_block, eng="dve"):
        i = self._wsi
        self._wsi = (i + 1) % len(self.wst)
        st, stb, ctr = self.wst[i], self.wstbuf[i], self.wstctr[i]
        self.dma(st[:, :, :], dram_block, ctr, writes=[stb])
        j = self._wbi
        self._wbi = (j + 1) % len(self.wbf)
        wb, wbb = self.wbf[j], self.wbfbuf[j]
        self.copy(wb[:, :, :], st[:, :, :], [stb], [wbb], eng=eng)
        return wb, wbb

    def load_small(self, dst_ap, src_ap, buf):
        c = self.dmactr("k")
        self.dma(dst_ap, src_ap, c, writes=[buf])

    def build(self):
        nc, S, T = self.nc, self.S, self.T
        NL = 4
        d = {}
        d["xT"] = self.dram_in("xT", [D, S])
        d["cT"] = self.dram_in("cT", [128, KC])
        d["ada_w"] = self.dram_in("ada_w", [NL, 12, 128, KC, 128])
        d["ada_b"] = self.dram_in("ada_b", [128, NL, 12])
        d["norm_g"] = self.dram_in("norm_g", [128, NL, KC])
        d["final_g"] = self.dram_in("final_g", [128, KC])
        d["lru_in_w"] = self.dram_in("lru_in_w", [2, 2 * NBK, 128, KC, 128])
        d["lru_conv_w"] = self.dram_in("lru_conv_w", [128, 2, NBK, 4])
        d["lru_conv_b"] = self.dram_in("lru_conv_b", [128, 2, NBK])
        d["lru_gate_w"] = self.dram_in("lru_gate_w", [2, 2, NBK, 128, 128])
        d["lru_gate_b"] = self.dram_in("lru_gate_b", [128, 2, 2, NBK])
        d["lru_lambda"] = self.dram_in("lru_lambda", [128, 2, NBK])
        d["lru_out_w"] = self.dram_in("lru_out_w", [2, KC, 128, KC, 128])
        d["gdn_in_w"] = self.dram_in("gdn_in_w", [2, 4 * NBK, 128, KC, 128])
        d["gdn_ab_w"] = self.dram_in("gdn_ab_w", [2, 128, KC, 2 * NBK])
        d["gdn_conv_w"] = self.dram_in("gdn_conv_w", [128, 2, 3 * NBK, 4])
        d["gdn_a_log"] = self.dram_in("gdn_a_log", [128, 2, NBK])
        d["gdn_dt_bias"] = self.dram_in("gdn_dt_bias", [128, 2, NBK])
        d["gdn_onorm_g"] = self.dram_in("gdn_onorm_g", [128, 2])
        d["gdn_out_w"] = self.dram_in("gdn_out_w", [2, KC, 128, KC, 128])
        d["masks"] = self.dram_in("masks", [128, NMASK, 128])
        self.d = d
        self.yT = nc.dram_tensor("yT", [D, S], F32, kind="ExternalOutput").ap()

        self.psb = [self.es.enter_context(nc.psum_tensor(f"ps{i}", [128, 512], F32)) for i in range(8)]
        self.psbuf = [Buf(f"ps{i}", excl=True) for i in range(8)]
        self._psi = 0
        self._psr = {}
        self.wst = [self.sb([128, KC, 128], F32, "wst") for _ in range(3)]
        self.wstbuf = [Buf() for _ in self.wst]
        self.wstctr = [self.dmactr("wst") for _ in self.wst]
        self._wsi = 0
        self.wbf = [self.sb([128, KC, 128], BF16, "wbf") for _ in range(6)]
        self.wbfbuf = [Buf() for _ in self.wbf]
        self._wbi = 0

        self.K = {}
        self.KB = {}

        def const(name, shape, src=None, dt=F32):
            t = self.sb(shape, dt, name)
            b = Buf(name)
            self.K[name], self.KB[name] = t, b
            if src is not None:
                self.load_small(t[:], src, b)
            return t

        const("masks", [128, NMASK, 128], d["masks"][:, :, :])
        const("cT", [128, KC], d["cT"][:, :])
        const("ada_b", [128, NL, 12], d["ada_b"][:, :, :])
        const("condh", [128, NL, 12])
        const("norm_g", [128, NL, KC], d["norm_g"][:, :, :])
        const("final_g", [128, KC], d["final_g"][:, :])
        const("lru_conv_w", [128, 2, NBK, 4], d["lru_conv_w"][:, :, :, :])
        const("lru_conv_b", [128, 2, NBK], d["lru_conv_b"][:, :, :])
        const("lru_gate_b", [128, 2, 2, NBK], d["lru_gate_b"][:, :, :, :])
        const("lru_lambda", [128, 2, NBK], d["lru_lambda"][:, :, :])
        const("gdn_conv_w", [128, 2, 3 * NBK, 4], d["gdn_conv_w"][:, :, :, :])
        const("gdn_a_log", [128, 2, NBK], d["gdn_a_log"][:, :, :])
        const("gdn_dt_bias", [128, 2, NBK], d["gdn_dt_bias"][:, :, :])
        const("gdn_onorm_g", [128, 2], d["gdn_onorm_g"][:, :])
        ones_bf = const("ones_bf", [128, 128], None, BF16)
        self.memset(ones_bf[:], 1.0, [self.KB["ones_bf"]])
        epsk = const("epsk", [128, 3])
        self.memset(epsk[:, 0:1], EPS, [self.KB["epsk"]])
        self.memset(epsk[:, 1:2], 1.0, [self.KB["epsk"]])
        self.memset(epsk[:, 2:3], float(np.log(0.5)), [self.KB["epsk"]])
        self.lnhalf_ap = epsk[:, 2:3]
        const("gbh", [128, 2, 2, NBK])
        const("clamh", [128, 2, NBK])
        self.eps_ap = epsk[:, 0:1]
        self.one_ap = epsk[:, 1:2]
        const("cond", [128, NL, 24])
        const("gs", [128, NL, KC])
        const("cact", [128, KC])
        const("clam", [128, 2, NBK])
        const("clam2", [128, 2, NBK])
        const("lru_state", [128, 2, NBK])
        const("lru_halo", [128, 2, NBK, 3])
        const("nexpalog", [128, 2, NBK])
        const("GA", [128, 6, T // 128, NBK])
        const("gdn_S", [128, 2, NBK, 128])
        const("sbf0", [128, 128], None, BF16)
        const("sbf1", [128, 128], None, BF16)
        const("gdn_halo", [128, 2, 3 * NBK, 3])
        const("dconv", [128, 12, 128], None, BF16)
        const("gsmall", [128, 16])
        const("gsmall1", [128, 16])
        const("sbf2", [128, 128], None, BF16)
        const("sbf3", [128, 128], None, BF16)
        const("wab", [128, KC, 2 * NBK], None, BF16)
        self.memset(self.K["gdn_S"][:], 0.0, [self.KB["gdn_S"]])
        self.memset(self.K["gdn_halo"][:], 0.0, [self.KB["gdn_halo"]])
        self.memset(self.K["lru_state"][:], 0.0, [self.KB["lru_state"]])
        self.memset(self.K["lru_halo"][:], 0.0, [self.KB["lru_halo"]])

        self.xT = self.sb([128, KC, T], F32, "xT")
        self.hT = self.sb([128, KC, T], BF16, "hT")
        self.yTs = self.sb([128, KC, T], BF16, "yTs")
        self.yTl = self.sb([128, NBK, T], BF16, "yTl")
        self.ylb = [[Buf() for _ in range(self.NSUB)] for _ in range(NBK)]
        NX = 2 * NBK
        self.xch_in = [nc.dram_tensor(f"xch_in{i}", [128, T], BF16) for i in range(NX)]
        self.xch_out = [nc.dram_tensor(f"xch_out{i}", [2 * 128, T], BF16) for i in range(NX)]
        self.xch_inb = [Buf() for _ in range(NX)]
        self.xch_outb = [Buf() for _ in range(NX)]
        self.xch_c1 = [self.dmactr("xi") for _ in range(NX)]
        self.xch_c2 = [self.dmactr("xo") for _ in range(NX)]
        self.cc_ctr = Ctr(nc, self.es, "cc", 1)
        self._xchi = 0
        self._op_pre = None
        self._presq = None
        self._xch_pending = []
        self.xb = [[Buf() for _ in range(self.NSUB)] for _ in range(KC)]
        self.hb = [[Buf() for _ in range(self.NSUB)] for _ in range(KC)]
        self.yb = [[Buf() for _ in range(self.NSUB)] for _ in range(KC)]
        self.xctr = [self.dmactr("x") for _ in range(KC)]
        self.octr = self.dmactr("o")

        self.scr = [self.sb([128, 520], F32, "scr") for _ in range(33)]
        self.scrbuf = [Buf() for _ in self.scr]
        self._sci = 0
        self.scrb = [self.sb([128, 520], BF16, "scrb") for _ in range(27)]
        self.scrbbuf = [Buf() for _ in self.scrb]
        self._scbi = 0

        self.prologue()
        last = None
        for ti in range(self.NT):
            self.load_x(ti)
            for li in self.layers:
                if li % 2 == 0:
                    self.lru_layer(li, ti)
                else:
                    self.gdn_layer(li, ti)
            last = self.store_out(ti)
        self.sch.emit([(self.octr, self.octr.n)])

    def scratch(self):
        i = self._sci
        self._sci = (i + 1) % 12
        return self.scr[i], self.scrbuf[i]

    def scratch_bf(self):
        i = self._scbi
        self._scbi = (i + 1) % 4
        return self.scrb[i], self.scrbbuf[i]

    def prologue(self):
        K, KB, d = self.K, self.KB, self.d
        self.act(K["cact"][:], K["cT"][:], AF.Silu, [KB["cT"]], [KB["cact"]])
        ast, astb, astc = self.wst[:2], self.wstbuf[:2], self.wstctr[:2]
        q = 0
        row, rowb = self.scratch()
        for li in range(4):
            banks = [self.ps_next() for _ in range(3)]
            for oc in range(12):
                st, stb, sc = ast[q % 2], astb[q % 2], astc[q % 2]
                q += 1
                src = d["ada_w"][li, oc]
                self.dma(st[:, :, :], src, sc, writes=[stb])
                ps, psb = banks[oc // 4]
                cs = slice((oc % 4) * 128, (oc % 4 + 1) * 128)
                for kc in range(KC):
                    self.mm(ps[0:1, cs], K["cact"][:, kc:kc + 1], st[:, kc, :],
                            kc == 0, kc == KC - 1, [stb, KB["cact"]], [psb])
            psT, psTb = self.ps_next()
            for g in range(3):
                ps, psb = banks[g]
                self.act(row[0:1, 0:512], ps[0:1, :], AF.Copy, [psb], [rowb])
                for o4 in range(4):
                    oc = g * 4 + o4
                    self.mm(psT[:, oc:oc + 1], row[0:1, o4 * 128:(o4 + 1) * 128], self.one_ap[0:1, :],
                            True, True, [rowb, KB["epsk"]], [psTb])
            self.tt(K["condh"][:, li, :], psT[:, 0:12], K["ada_b"][:, li, :], ALU.add,
                    [psTb, KB["ada_b"]], [KB["condh"]])
        cin = self.nc.dram_tensor("cond_in", [128, 48], F32)
        cout = self.nc.dram_tensor("cond_out", [256, 48], F32)
        cinb, coutb = Buf(), Buf()
        c1, c2 = self.dmactr("ci"), self.dmactr("co")
        self.dma(cin.ap(), K["condh"][:].rearrange("p l c -> p (l c)"), c1, reads=[KB["condh"]], writes=[cinb])
        self.sch.op("pool", lambda E: E.collective_compute(
            "AllGather", ALU.bypass, replica_groups=[[0, 1], [2, 3], [4, 5], [6, 7]],
            ins=[cin.ap().opt()], outs=[cout.ap().opt()]), [cinb], [coutb], dma=self.cc_ctr)
        for r_ in range(2):
            self.dma(K["cond"][:, :, r_ * 12:(r_ + 1) * 12],
                     cout.ap()[r_ * 128:(r_ + 1) * 128, :].rearrange("p (l c) -> p l c", l=4), c2,
                     reads=[coutb], writes=[KB["cond"]])
        for li in range(4):
            self.stt(K["gs"][:, li, :], K["cond"][:, li, 8:16], 1.0, K["norm_g"][:, li, :], ALU.add, ALU.mult,
                     [KB["cond"], KB["norm_g"]], [KB["gs"]])
        self.act(K["nexpalog"][:], K["gdn_a_log"][:], AF.Exp, [KB["gdn_a_log"]], [KB["nexpalog"]])
        self.ts(K["nexpalog"][:], K["nexpalog"][:], -1.0, None, ALU.mult, None, [KB["nexpalog"]], [KB["nexpalog"]])
        t, tb = self.scratch()
        self.act(t[:, 0:2 * NBK], K["lru_lambda"][:].rearrange("p a b -> p (a b)"), AF.Exp, [KB["lru_lambda"]], [tb],
                 scale=-1.0)
        self.act(t[:, 16:16 + 2 * NBK], t[:, 0:2 * NBK], AF.Ln, [tb, KB["epsk"]], [tb], bias=self.one_ap)
        self.ts(K["clam"][:].rearrange("p a b -> p (a b)"), t[:, 16:16 + 2 * NBK], -8.0, None, ALU.mult, None, [tb], [KB["clam"]])
        self.ts(K["clam2"][:].rearrange("p a b -> p (a b)"), t[:, 16:16 + 2 * NBK], -16.0, None, ALU.mult, None, [tb],
                [KB["clam2"]])
        self.ts(K["clamh"][:].rearrange("p a b -> p (a b)"), t[:, 16:16 + 2 * NBK], -4.0, None, ALU.mult, None, [tb],
                [KB["clamh"]])
        self.ts(K["gbh"][:].rearrange("p a b c -> p (a b c)"), K["lru_gate_b"][:].rearrange("p a b c -> p (a b c)"),
                0.5, None, ALU.mult, None, [KB["lru_gate_b"]], [KB["gbh"]])

    def load_x(self, ti):
        T = self.T
        for kc in range(KC):
            self.dma(self.xT[:, kc, :], self.d["xT"][kc * 128:(kc + 1) * 128, ti * T:(ti + 1) * T], self.xctr[kc],
                     writes=self.xb[kc])

    def norm_phase(self, gs_ap_fn, shift_ap_fn, out_fn):
        K, KB = self.K, self.KB
        presq = self._presq
        self._presq = None
        for sub in range(self.NSUB):
            sl = slice(sub * 512, (sub + 1) * 512)
            if presq is not None:
                ps, psb = presq[sub]
            else:
                ps, psb = self.ps_next()
                for kc in range(KC):
                    sq, sqb = self.scratch_bf()
                    self.act(sq[:, 0:512], self.xT[:, kc, sl], AF.Square, [self.xb[kc][sub]], [sqb])
                    self.mm(ps[:, :], K["ones_bf"][:], sq[:, 0:512], kc == 0, kc == KC - 1,
                            [KB["ones_bf"], sqb], [psb])
            rs, rsb = self.scratch()
            self.act(rs[:, 0:512], ps[:, :], AF.Ln, [psb, KB["epsk"]], [rsb], bias=self.eps_ap, scale=1.0 / D)
            rstd, rstdb = self.scratch()
            self.act(rstd[:, 0:512], rs[:, 0:512], AF.Exp, [rsb], [rstdb], scale=-0.5)
            for kc in range(KC):
                out_fn(kc, sub, sl, rstd, rstdb)

    def store_out(self, ti):
        K, KB, T = self.K, self.KB, self.T
        if not self.final:
            for kc in range(KC):
                self.dma(self.yT[kc * 128:(kc + 1) * 128, ti * T:(ti + 1) * T], self.xT[:, kc, :], self.octr,
                         reads=self.xb[kc])
            return

        def out_fn(kc, sub, sl, rstd, rstdb):
            o, ob = self.scratch()
            self.stt(o[:, 0:512], self.xT[:, kc, sl], K["final_g"][:, kc:kc + 1], rstd[:, 0:512], ALU.mult, ALU.mult,
                     [self.xb[kc][sub], KB["final_g"], rstdb], [ob])
            self.dma(self.yT[kc * 128:(kc + 1) * 128, ti * T + sub * 512: ti * T + (sub + 1) * 512], o[:, 0:512],
                     self.octr, reads=[ob])

        self.norm_phase(None, None, out_fn)

    def mod_phase(self, li):
        K, KB = self.K, self.KB

        def out_fn(kc, sub, sl, rstd, rstdb):
            t, tb = self.scratch()
            self.stt(t[:, 0:512], self.xT[:, kc, sl], K["gs"][:, li, kc:kc + 1], rstd[:, 0:512], ALU.mult, ALU.mult,
                     [self.xb[kc][sub], KB["gs"], rstdb], [tb])
            self.act(self.hT[:, kc, sl], t[:, 0:512], AF.Identity, [tb, KB["cond"]], [self.hb[kc][sub]],
                     bias=K["cond"][:, li, kc:kc + 1])

        self.norm_phase(None, None, out_fn)

    def exchange_part(self, j):
        i = (self._xchi % 2) * NBK + j
        xin, xout = self.xch_in[i], self.xch_out[i]
        self._xch_flush()
        self.dma(xin.ap(), self.yTl[:, j, :], self.xch_c1[i], reads=self.ylb[j], writes=[self.xch_inb[i]])
        self.sch.op("pool", lambda E: E.collective_compute(
            "AllGather", ALU.bypass, replica_groups=[[0, 1], [2, 3], [4, 5], [6, 7]],
            ins=[xin.ap().opt()], outs=[xout.ap().opt()]),
            [self.xch_inb[i]], [self.xch_outb[i]], dma=self.cc_ctr)
        self._xch_pending.append((i, j))

    def _xch_flush(self):
        for i, j in self._xch_pending:
            xout = self.xch_out[i]
            dst = self.yTs[:, :, :].rearrange("p (r g) t -> p r g t", r=2)[:, :, j, :]
            self.dma(dst, xout.ap().rearrange("(r p) t -> p r t", p=128), self.xch_c2[i],
                     reads=[self.xch_outb[i]], writes=self.yb[j] + self.yb[NBK + j])
        self._xch_pending = []

    def exchange(self):
        self._xch_flush()
        self._xchi += 1

    def outproj_phase(self, li, w_dram):
        K, KB = self.K, self.KB
        late = (NBK - 1, 2 * NBK - 1)
        early = [n for n in range(KC) if n not in late]
        pre = self._op_pre if self._op_pre is not None else [self.load_w(w_dram[j]) for j in range(2)]
        self._op_pre = None
        r_ = 0
        for j in range(KC):
            wb, wbb = pre[j] if j < len(pre) else self.load_w(w_dram[j], eng="pool")
            for sub in range(self.NSUB):
                sl = slice(sub * 512, (sub + 1) * 512)
                ps, psb = self.psb[r_ % 6], self.psbuf[r_ % 6]
                r_ += 1
                for k, n in enumerate(early):
                    self.mm(ps[:, :], wb[:, n, :], self.yTs[:, n, sl], k == 0, k == len(early) - 1,
                            [wbb, self.yb[n][sub]], [psb])
                self.stt(self.xT[:, j, sl], ps[:, :], K["cond"][:, li, 16 + j:17 + j], self.xT[:, j, sl],
                         ALU.mult, ALU.add, [psb, KB["cond"], self.xb[j][sub]], [self.xb[j][sub]])
        wl = [self.load_w(w_dram[:, :, n, :].rearrange("j p c -> p j c"), eng="pool") for n in late]
        self.exchange()
        sqacc = [(self.psb[6 + sub], self.psbuf[6 + sub]) for sub in range(self.NSUB)]
        pend = []

        def flush(keep):
            while len(pend) > keep:
                sq_, sqb_, sub_, j_ = pend.pop(0)
                self.mm(sqacc[sub_][0][:, :], K["ones_bf"][:], sq_[:, 0:512], j_ == 0, j_ == KC - 1,
                        [KB["ones_bf"], sqb_], [sqacc[sub_][1]])

        for j in range(KC):
            for sub in range(self.NSUB):
                sl = slice(sub * 512, (sub + 1) * 512)
                ps, psb = self.psb[r_ % 6], self.psbuf[r_ % 6]
                r_ += 1
                for k, n in enumerate(late):
                    self.mm(ps[:, :], wl[k][0][:, j, :], self.yTs[:, n, sl], k == 0, k == len(late) - 1,
                            [wl[k][1], self.yb[n][sub]], [psb])
                flush(2)
                self.stt(self.xT[:, j, sl], ps[:, :], K["cond"][:, li, 16 + j:17 + j], self.xT[:, j, sl],
                         ALU.mult, ALU.add, [psb, KB["cond"], self.xb[j][sub]], [self.xb[j][sub]])
                sq, sqb = self.scratch_bf()
                self.act(sq[:, 0:512], self.xT[:, j, sl], AF.Square, [self.xb[j][sub]], [sqb])
                pend.append((sq, sqb, sub, j))
        flush(0)
        self._presq = sqacc

    def lru_layer(self, li, ti):
        K, KB, d = self.K, self.KB, self.d
        l = li // 2
        self.mod_phase(li)
        W = slice(0, 512)
        k_ = 0
        Fb = {}
        for nm in ["pre", "r", "it", "a", "m", "hs"]:
            Fb[nm] = (self.scr[k_], self.scrbuf[k_])
            k_ += 1
        H = [{}, {}]
        for par in range(2):
            for nm in ["xc", "sz"]:
                H[par][nm] = (self.scr[k_], self.scrbuf[k_])
                k_ += 1
            H[par]["xcf"] = (self.scrb[2 + par], self.scrbbuf[2 + par])
        cw = K["lru_conv_w"]
        wcache = {}

        def load_block_weights(n):
            wx = self.load_w(d["lru_in_w"][l, n])
            wz = self.load_w(d["lru_in_w"][l, NBK + n])
            gw = []
            for k in range(2):
                i = self._wsi
                self._wsi = (i + 1) % len(self.wst)
                st, stb, ctr = self.wst[i], self.wstbuf[i], self.wstctr[i]
                self.dma(st[:, 0, :], d["lru_gate_w"][l, k, n, :, :], ctr, writes=[stb])
                jj = self._wbi
                self._wbi = (jj + 1) % len(self.wbf)
                wb, wbb = self.wbf[jj], self.wbfbuf[jj]
                self.copy(wb[:, 0, :], st[:, 0, :], [stb], [wbb], eng="pool")
                gw.append((wb, wbb))
            return wx, wz, gw

        def lru_iter(n, sub, par):
            Hh = H[par]
            sl = slice(sub * 512, (sub + 1) * 512)
            if n not in wcache:
                wcache[n] = load_block_weights(n)
            (wx, wxb), (wz, wzb), gw = wcache[n]
            prefetch_next = (sub == self.NSUB - 1 and n + 1 < NBK)
            psx, psxb = self.ps_next("LF")
            for kc in range(KC):
                self.mm(psx[:, :], wx[:, kc, :], self.hT[:, kc, sl], kc == 0, kc == KC - 1,
                        [wxb, self.hb[kc][sub]], [psxb])
                if kc % 4 == 3:
                    yield 0
            psz, pszb = self.ps_next("LZ")
            for kc in range(KC):
                self.mm(psz[:, :], wz[:, kc, :], self.hT[:, kc, sl], kc == 0, kc == KC - 1,
                        [wzb, self.hb[kc][sub]], [pszb])
                if kc % 4 == 3:
                    yield 0
            if prefetch_next:
                wcache[n + 1] = load_block_weights(n + 1)
            elif sub == self.NSUB - 1 and n + 1 == NBK:
                self._op_pre = [self.load_w(d["lru_out_w"][l][j]) for j in range(4)]
            pre, preb = Fb["pre"]
            self.copy(pre[:, 0:3], K["lru_halo"][:, l, n, :], [KB["lru_halo"]], [preb], eng="pool")
            self.act(pre[:, 3:515], psx[:, :], AF.Copy, [psxb], [preb])
            self.copy(K["lru_halo"][:, l, n, :], pre[:, 512:515], [preb], [KB["lru_halo"]], eng="pool")
            yield 0
            xc, xcb = Hh["xc"]
            self.ts(xc[:, W], pre[:, 3:515], cw[:, l, n, 3:4], K["lru_conv_b"][:, l, n:n + 1], ALU.mult, ALU.add,
                    [preb, KB["lru_conv_w"], KB["lru_conv_b"]], [xcb])
            for jx in (2, 1, 0):
                self.stt(xc[:, W], pre[:, jx:jx + 512], cw[:, l, n, jx:jx + 1], xc[:, W], ALU.mult, ALU.add,
                         [preb, KB["lru_conv_w"], xcb], [xcb])
                if jx == 1:
                    yield 0
            yield 0
            xcf, xcfb = Hh["xcf"]
            self.copy(xcf[:, W], xc[:, W], [xcb], [xcfb], eng="pool")
            yield "HANDOFF"
            psr, psrb = self.ps_next("LB")
            self.mm(psr[:, :], gw[0][0][:, 0, :], xcf[:, W], True, True, [gw[0][1], xcfb], [psrb])
            psi, psib = self.ps_next("LB")
            self.mm(psi[:, :], gw[1][0][:, 0, :], xcf[:, W], True, True, [gw[1][1], xcfb], [psib])
            yield 0
            sz, szb = Hh["sz"]
            self.act(sz[:, W], psz[:, :], AF.Silu, [pszb], [szb])
            r, rb = Fb["r"]
            self.act(r[:, W], psr[:, :], AF.Tanh, [psrb, KB["gbh"]], [rb], bias=K["gbh"][:, l, 0, n:n + 1], scale=0.5)
            it, itb = Fb["it"]
            self.act(it[:, W], psi[:, :], AF.Tanh, [psib, KB["gbh"]], [itb], bias=K["gbh"][:, l, 1, n:n + 1], scale=0.5)
            yield 0
            a, ab = Fb["a"]
            self.act(a[:, W], r[:, W], AF.Exp, [rb, KB["clamh"]], [ab], scale=K["clamh"][:, l, n:n + 1],
                     bias=K["clamh"][:, l, n:n + 1])
            m, mb = Fb["m"]
            self.act(m[:, W], r[:, W], AF.Exp, [rb, KB["clam"]], [mb], scale=K["clam"][:, l, n:n + 1],
                     bias=K["clam"][:, l, n:n + 1])
            self.stt(it[:, W], it[:, W], 1.0, xc[:, W], ALU.add, ALU.mult, [itb, xcb], [itb])
            yield 0
            self.act(m[:, W], m[:, W], AF.Ln, [mb, KB["epsk"]], [mb], bias=self.one_ap, scale=-1.0)
            self.act(m[:, W], m[:, W], AF.Exp, [mb, KB["epsk"]], [mb], bias=self.lnhalf_ap, scale=0.5)
            yield 0
            self.tt(m[:, W], m[:, W], it[:, W], ALU.mult, [mb, itb], [mb])
            hs, hsb = Fb["hs"]
            self.sch.op("dve", lambda E, o=hs[:, W], a_=a[:, W], b_=m[:, W], ini=K["lru_state"][:, l, n:n + 1]:
                        E.tensor_tensor_scan(out=o, data0=a_, data1=b_, initial=ini, op0=ALU.mult, op1=ALU.add),
                        [ab, mb, KB["lru_state"]], [hsb])
            yield 0
            self.copy(K["lru_state"][:, l, n:n + 1], hs[:, 511:512], [hsb], [KB["lru_state"]], eng="pool")
            self._xch_flush()
            self.tt(self.yTl[:, n, sl], hs[:, W], sz[:, W], ALU.mult, [hsb, szb], [self.ylb[n][sub]])
            if sub == self.NSUB - 1:
                self.exchange_part(n)

        self.run_pipelined([lru_iter(n, sub, (n * self.NSUB + sub) % 2)
                            for n in range(NBK) for sub in range(self.NSUB)], "lru")
        self.outproj_phase(li, d["lru_out_w"][l])

    def run_pipelined(self, gens, key):
        nf, nb = self._pipe_counts.get(key, (12, 30))
        prev = None

        def step(g):
            try:
                return next(g)
            except StopIteration:
                return None

        for g in gens:
            cf = 0
            doneb = 0
            while True:
                r = step(g)
                cf += 1
                if prev is not None:
                    want = min(nb, (cf * nb + nf - 1) // nf)
                    while doneb < want:
                        if step(prev) is None:
                            prev = None
                            break
                        doneb += 1
                if r == "HANDOFF" or r is None:
                    break
            while prev is not None:
                if step(prev) is None:
                    prev = None
                else:
                    doneb += 1
            if doneb:
                nb = doneb + 1
            nf = cf
            self._pipe_counts[key] = (nf, nb)
            prev = g
        while prev is not None:
            if step(prev) is None:
                prev = None

    def run_pipelined3(self, gens, key):
        nf, nb = self._pipe_counts.get(key, (36, 40))
        backs = []
        DONE = 10 ** 9

        def step(g):
            try:
                return next(g)
            except StopIteration:
                return None

        def advance(entry, want):
            while entry[1] < want:
                if step(entry[0]) is None:
                    entry[1] = DONE
                    return
                entry[1] += 1

        for g in gens:
            cf = 0
            base = [e[1] for e in backs]
            while True:
                r = step(g)
                cf += 1
                for e, b0 in zip(backs, base):
                    if e[1] < DONE:
                        advance(e, b0 + (cf * nb + 2 * nf - 1) // (2 * nf))
                if r == "HANDOFF" or r is None:
                    break
            nf = cf
            backs = [e for e in backs if e[1] < DONE]
            while len(backs) >= 2:
                e = backs.pop(0)
                while step(e[0]) is not None:
                    e[1] += 1
                nb = max(nb, e[1] + 1)
            backs.append([g, 0])
            self._pipe_counts[key] = (nf, nb)
        for e in backs:
            while step(e[0]) is not None:
                pass

    def mm4(self, lhs, lhsb, rhs, rhsb, ring=None, f32r=False):
        ps, psb = self.ps_next(ring)
        for b in range(4):
            cs = slice(b * 128, (b + 1) * 128)
            if f32r:
                self.mm(ps[:, cs], lhs[:, cs].bitcast(F32R), rhs[:, cs].bitcast(F32R), True, True,
                        [lhsb, rhsb], [psb])
            else:
                self.mm(ps[:, cs], lhs[:, cs], rhs[:, cs], True, True, [lhsb, rhsb], [psb])
        return ps, psb

    def tr4(self, src, srcb, ring=None):
        ps, psb = self.ps_next(ring)
        ident = self.K["masks"][:, 0, :]
        for b in range(4):
            cs = slice(b * 128, (b + 1) * 128)
            self.tr(ps[:, cs], src[:, cs], ident, [srcb, self.KB["masks"]], [psb])
        return ps, psb

    def gdn_layer(self, li, ti):
        K, KB, d = self.K, self.KB, self.d
        l = li // 2
        T = self.T
        NB = T // 128
        M, MB = K["masks"], KB["masks"]
        GA, GAB = K["GA"], KB["GA"]
        W = slice(0, 512)

        def mask4(k):
            return M[:, k, :].unsqueeze(1).to_broadcast([128, 4, 128])

        def v4(ap):
            return ap.rearrange("p (b j) -> p b j", b=4)

        self.mod_phase(li)
        i = self._wsi
        self._wsi = (i + 1) % len(self.wst)
        st, stb, ctr = self.wst[i], self.wstbuf[i], self.wstctr[i]
        NH = NBK
        self.dma(st[:, :, 0:2 * NH], d["gdn_ab_w"][l], ctr, writes=[stb])
        wab, wabb = K["wab"], KB["wab"]
        self.copy(wab[:, :, :], st[:, :, 0:2 * NH], [stb], [wabb], eng="pool")
        ps, psb = self.ps_next()
        psv = ps[:, 0:NB * 2 * NH].rearrange("p (b c) -> p b c", b=NB)
        for blk in range(NB):
            sub = blk // 4
            bs = slice(blk * 128, (blk + 1) * 128)
            for kc in range(KC):
                self.mm(psv[:, blk, :], self.hT[:, kc, bs], wab[:, kc, :], kc == 0, kc == KC - 1,
                        [self.hb[kc][sub], wabb], [psb])
        t, tb = self.scratch()
        NQ = NB * NH

        def tv(i):
            return t[:, i * NQ:(i + 1) * NQ].rearrange("p (b h) -> p b h", b=NB)

        def bc(ap):
            return ap.unsqueeze(1).to_broadcast([128, NB, NH])

        self.tt(tv(0), psv[:, :, 0:NH], bc(K["gdn_dt_bias"][:, l, :]), ALU.add, [psb, KB["gdn_dt_bias"]], [tb])
        self.act(tv(1), tv(0), AF.Exp, [tb], [tb])
        self.act(tv(2), tv(1), AF.Ln, [tb, KB["epsk"]], [tb], bias=self.one_ap)
        self.tt(tv(3), tv(2), bc(K["nexpalog"][:, l, :]), ALU.mult, [tb, KB["nexpalog"]], [tb])
        self.act(tv(4), psv[:, :, NH:2 * NH], AF.Exp, [psb], [tb], scale=-1.0)
        self.act(tv(5), tv(4), AF.Ln, [tb, KB["epsk"]], [tb], bias=self.one_ap)
        self.act(GA[:, 4, :, :], tv(5), AF.Exp, [tb], [GAB], scale=-1.0)
        ps2, ps2b = self.ps_next()
        p2v = ps2[:, 0:2 * NQ].rearrange("p (q b h) -> p q b h", q=2, b=NB)
        for blk in range(NB):
            self.mm(p2v[:, 0, blk, :], M[:, 9, :], tv(3)[:, blk, :], True, True, [MB, tb], [ps2b])
            self.mm(p2v[:, 1, blk, :], M[:, 1, :], tv(3)[:, blk, :], True, True, [MB, tb], [ps2b])
        self.copy(GA[:, 0, :, :], p2v[:, 0], [ps2b], [GAB], eng="act")
        self.act(GA[:, 1, :, :], p2v[:, 0], AF.Exp, [ps2b], [GAB])
        self.act(GA[:, 3, :, :], p2v[:, 1], AF.Exp, [ps2b], [GAB])
        self.tt(tv(6), p2v[:, 1], GA[:, 0, :, :], ALU.subtract, [ps2b, GAB], [tb])
        self.act(GA[:, 2, :, :], tv(6), AF.Exp, [tb], [GAB])
        self.tt(GA[:, 5, :, :], GA[:, 4, :, :], GA[:, 1, :, :], ALU.mult, [GAB], [GAB], eng="pool")

        k_ = 0
        F = {}
        for nm in ["pre", "xc", "qs", "ks", "vs", "knT"]:
            F[nm] = (self.scr[k_], self.scrbuf[k_])
            k_ += 1
        F["rs"], F["rinv"], F["diagG"], F["t"], F["tU"], F["tL"], F["t1"] = (
            F["pre"], F["xc"], F["qs"], F["ks"], F["vs"], F["pre"], F["xc"])
        H = [{}, {}, {}]
        for hs_ in range(3):
            for nm in ["szT", "AL"]:
                H[hs_][nm] = (self.scr[k_], self.scrbuf[k_])
                k_ += 1
        assert k_ == 12
        for hs_ in range(3):
            H[hs_]["XU"] = (self.scr[k_], self.scrbuf[k_])
            k_ += 1
        BK = [{}, {}]
        for bs_ in range(2):
            for nm in ["XL", "PU", "XUa", "XUb", "XLa", "XLb", "L", "Y", "EL"]:
                BK[bs_][nm] = (self.scr[k_], self.scrbuf[k_])
                k_ += 1
            b_ = BK[bs_]
            b_["u"], b_["o"], b_["o1a"], b_["o1b"], b_["sq"], b_["on"] = (
                b_["XLb"], b_["XLa"], b_["XUa"], b_["XUb"], b_["Y"], b_["EL"])
        assert k_ <= len(self.scr), k_
        k_ = 0
        Bf = {}
        for nm in ["sqb1", "sqb2", "knTb"]:
            Bf[nm] = (self.scrb[k_], self.scrbbuf[k_])
            k_ += 1
        k_ = 4
        for bs_ in range(2):
            for nm in ["Ubf", "wT", "vn0", "vn1"]:
                BK[bs_][nm] = (self.scrb[k_], self.scrbbuf[k_])
                k_ += 1
        for hs_ in range(3):
            for nm in ["qnT", "kbg", "kdec", "vb", "attnT"]:
                H[hs_][nm] = (self.scrb[k_], self.scrbbuf[k_])
                k_ += 1
        assert k_ <= len(self.scrb), k_
        smalls = [(K["gsmall"], KB["gsmall"]), (K["gsmall1"], KB["gsmall1"])]
        sbfs = [[(K["sbf0"], KB["sbf0"]), (K["sbf1"], KB["sbf1"])], [(K["sbf2"], KB["sbf2"]), (K["sbf3"], KB["sbf3"])]]
        cw = K["gdn_conv_w"]
        wcache = {}

        def prefetch_head(hd):
            wcache[hd] = [self.load_w(d["gdn_in_w"][l, g4 * NH + hd]) for g4 in range(4)]
            dcv, dcvb = K["dconv"], KB["dconv"]
            for idx in range(3):
                for jx in range(4):
                    self.ts(dcv[:, idx * 4 + jx, :], M[:, 0, :], cw[:, l, idx * NH + hd, jx:jx + 1], None,
                            ALU.mult, None, [MB, KB["gdn_conv_w"]], [dcvb], eng="dve")

        def gdn_iter(hd, sub, it_):
            Hh = H[it_ % 3]
            Bk = BK[it_ % 2]
            RB = "B%d" % (it_ % 2)
            small, smallb = smalls[it_ % 2]
            sl = slice(sub * 512, (sub + 1) * 512)
            gb0 = sub * 4

            def gB(q):
                return GA[:, q, gb0:gb0 + 4, hd:hd + 1].to_broadcast([128, 4, 128])

            if hd not in wcache:
                prefetch_head(hd)
            ws = wcache[hd]
            dcv, dcvb = K["dconv"], KB["dconv"]
            pre, preb = F["pre"]
            preh = pre[:, 0:260].bitcast(BF16)

            def inproj(g4):
                ps, psb = self.ps_next("F")
                for kc in range(KC):
                    self.mm(ps[:, :], ws[g4][0][:, kc, :], self.hT[:, kc, sl], kc == 0, kc == KC - 1,
                            [ws[g4][1], self.hb[kc][sub]], [psb])
                    if kc % 4 == 3:
                        yield 0
                return ps, psb

            for idx, dst in enumerate(("qs", "ks", "vs")):
                cb = idx * NH + hd
                psX, psXb = yield from inproj(idx)
                self.copy(preh[:, 0:3], K["gdn_halo"][:, l, cb, :], [KB["gdn_halo"]], [preb], eng="pool")
                self.act(preh[:, 3:515], psX[:, :], AF.Copy, [psXb], [preb])
                self.copy(K["gdn_halo"][:, l, cb, :], preh[:, 512:515], [preb], [KB["gdn_halo"]], eng="pool")
                yield 0
                pc, pcb = self.ps_next("F")
                for jx in range(4):
                    self.mm(pc[:, :], dcv[:, idx * 4 + jx, :], preh[:, jx:jx + 512], jx == 0, jx == 3,
                            [dcvb, preb], [pcb])
                yield 0
                dt_, dtb = F[dst]
                self.act(dt_[:, W], pc[:, :], AF.Silu, [pcb], [dtb])
                yield 0
            psZ, psZb = yield from inproj(3)
            if sub == self.NSUB - 1 and hd + 1 < NH:
                prefetch_head(hd + 1)
                yield 0
            elif sub == self.NSUB - 1:
                self._op_pre = [self.load_w(d["gdn_out_w"][l][j]) for j in range(4)]
                yield 0
            szT, szTb = Hh["szT"]
            self.act(szT[:, W], psZ[:, :], AF.Silu, [psZb], [szTb])
            yield 0
            rs, rsb = F["rs"]
            rinv, rinvb = F["rinv"]
            qnT, qnTb = Hh["qnT"]
            knT, knTb_ = F["knT"]
            knb, knbb = Bf["knTb"]
            for which in ("q", "k"):
                src, srcb = F[which + "s"]
                sq, sqb = Bf["sqb1"] if which == "q" else Bf["sqb2"]
                self.act(sq[:, W], src[:, W], AF.Square, [srcb], [sqb])
                psn, psnb = self.ps_next("F")
                self.mm(psn[:, :], K["ones_bf"][:], sq[:, W], True, True, [KB["ones_bf"], sqb], [psnb])
                yield 0
                self.act(rs[:, W], psn[:, :], AF.Ln, [psnb, KB["epsk"]], [rsb], bias=self.eps_ap)
                self.act(rinv[:, W], rs[:, W], AF.Exp, [rsb], [rinvb], scale=-0.5)
                yield 0
                if which == "q":
                    self.stt(qnT[:, W], src[:, W], 128.0 ** -0.5, rinv[:, W], ALU.mult, ALU.mult,
                             [srcb, rinvb], [qnTb])
                else:
                    self.tt(knT[:, W], src[:, W], rinv[:, W], ALU.mult, [srcb, rinvb], [knTb_])
                    self.act(knb[:, W], knT[:, W], AF.Copy, [knTb_], [knbb])
                yield 0
            vs, vsb = F["vs"]
            kTp, kTpb = self.tr4(knT, knTb_, ring="F")
            vTp, vTpb = self.tr4(vs, vsb, ring="F")
            kbg, kbgb = Hh["kbg"]
            kdec, kdecb = Hh["kdec"]
            vb, vbb = Hh["vb"]
            yield 0
            self.tt(v4(kbg[:, W]), v4(kTp[:, :]), gB(5), ALU.mult, [kTpb, GAB], [kbgb])
            self.tt(v4(kdec[:, W]), v4(kTp[:, :]), gB(2), ALU.mult, [kTpb, GAB], [kdecb])
            self.tt(v4(vb[:, W]), v4(vTp[:, :]), gB(4), ALU.mult, [vTpb, GAB], [vbb])
            yield 0
            dG, dGb = F["diagG"]
            self.tt(v4(dG[:, W]), mask4(0), gB(0), ALU.mult, [MB, GAB], [dGb], eng="pool")
            Gp, Gpb = self.ps_next("F")
            for b in range(4):
                cs = slice(b * 128, (b + 1) * 128)
                self.mm(Gp[:, cs], M[:, 1, :], dG[:, cs], True, True, [MB, dGb], [Gpb])
            yield 0
            t, tb = F["t"]
            self.tt(v4(t[:, W]), v4(Gp[:, :]), gB(0), ALU.subtract, [Gpb, GAB], [tb])
            KKp, KKpb = self.mm4(knb, knbb, knb, knbb, ring="F")
            QKp, QKpb = self.mm4(knb, knbb, qnT, qnTb, ring="F")
            yield 0
            tU, tUb = F["tU"]
            tL, tLb = F["tL"]
            self.tt(v4(tL[:, W]), mask4(3), v4(t[:, W]), ALU.subtract, [tb, MB], [tLb], eng="pool")
            yield 0
            self.act(tL[:, W], tL[:, W], AF.Exp, [tLb], [tLb])
            self.tt(v4(tU[:, W]), v4(t[:, W]), mask4(2), ALU.add, [tb, MB], [tUb], eng="pool")
            self.act(tU[:, W], tU[:, W], AF.Exp, [tUb], [tUb])
            t1, t1b = F["t1"]
            self.tt(v4(t1[:, W]), v4(KKp[:, :]), gB(4), ALU.mult, [KKpb, GAB], [t1b])
            yield 0
            AL, ALb = Hh["AL"]
            self.tt(AL[:, W], t1[:, W], tL[:, W], ALU.mult, [t1b, tLb], [ALb], eng="pool")
            attnT, attnTb = Hh["attnT"]
            self.tt(attnT[:, W], QKp[:, :], tU[:, W], ALU.mult, [QKpb, tUb], [attnTb])
            AUp, AUpb = self.tr4(AL, ALb, ring="F")
            yield 0
            XU, XUb_ = Hh["XU"]
            self.tt(v4(XU[:, W].bitcast(F32R)), v4(AUp[:, :]), mask4(4), ALU.mult, [AUpb, MB], [XUb_])
            yield "HANDOFF"
            XL, XLb_ = Bk["XL"]
            self.tt(v4(XL[:, W].bitcast(F32R)), v4(AL[:, W]), mask4(5), ALU.mult, [ALb, MB], [XLb_])
            PU, PUb = Bk["PU"]
            self.tt(v4(PU[:, W].bitcast(F32R)), v4(XU[:, W]), mask4(0), ALU.add, [XUb_, MB], [PUb], eng="pool")
            curU, curUb = XU, XUb_
            curL, curLb = XL, XLb_
            for s_ in range(4):
                if s_ == 0:
                    pa, pab = self.mm4(curL, curLb, curU, curUb, ring=RB, f32r=True)
                elif s_ < 3:
                    pa, pab = self.mm4(curL, curLb, curU, curUb, ring=RB, f32r=True)
                    pb, pbb = self.mm4(curL, curLb, PU, PUb, ring=RB, f32r=True)
                else:
                    pb, pbb = self.mm4(curL, curLb, PU, PUb, ring=RB, f32r=True)
                yield 0
                if s_ > 0:
                    self.tt(PU[:, W].bitcast(F32R), PU[:, W], pb[:, :], ALU.add, [PUb, pbb], [PUb])
                if s_ == 3:
                    break
                nU, nUb = Bk["XUa"] if s_ % 2 == 0 else Bk["XUb"]
                self.copy(nU[:, W].bitcast(F32R), pa[:, :], [pab], [nUb], eng="act")
                pt, ptb = self.tr4(nU, nUb, ring=RB)
                yield 0
                nL, nLb = Bk["XLa"] if s_ % 2 == 0 else Bk["XLb"]
                self.copy(nL[:, W].bitcast(F32R), pt[:, :], [ptb], [nLb], eng="dve")
                curU, curUb, curL, curLb = nU, nUb, nL, nLb
            Lt, Ltb = Bk["L"]
            pt, ptb = self.tr4(PU, PUb, ring=RB)
            yield 0
            self.copy(Lt[:, W].bitcast(F32R), pt[:, :], [ptb], [Ltb], eng="act")
            Ubf, Ubfb = Bk["Ubf"]
            for k in range(3):
                EL, ELb = Bk["EL"]
                self.tt(v4(EL[:, W].bitcast(F32R)), v4(AL[:, W]), mask4(6 + k), ALU.mult, [ALb, MB], [ELb])
                py, pyb = self.mm4(EL, ELb, PU, PUb, ring=RB, f32r=True)
                yield 0
                Y, Yb = Bk["Y"]
                self.copy(Y[:, W].bitcast(F32R), py[:, :], [pyb], [Yb], eng="act")
                pz, pzb = self.mm4(Lt, Ltb, Y, Yb, ring=RB, f32r=True)
                yield 0
                if k < 2:
                    self.tt(PU[:, W].bitcast(F32R), PU[:, W], pz[:, :], ALU.subtract, [PUb, pzb], [PUb])
                    pt, ptb = self.tr4(PU, PUb, ring=RB)
                    yield 0
                    self.copy(Lt[:, W].bitcast(F32R), pt[:, :], [ptb], [Ltb], eng="act")
                else:
                    self.tt(Ubf[:, W], PU[:, W], pz[:, :], ALU.subtract, [PUb, pzb], [Ubfb])
            pu, pub = self.mm4(Ubf, Ubfb, vb, vbb, ring=RB)
            pw, pwb = self.mm4(kbg, kbgb, Ubf, Ubfb, ring=RB)
            yield 0
            u, ub = Bk["u"]
            self.copy(u[:, W].bitcast(F32R), pu[:, :], [pub], [ub], eng="act")
            wT, wTb = Bk["wT"]
            self.copy(wT[:, W], pw[:, :], [pwb], [wTb], eng="dve")
            S32, S32b = K["gdn_S"][:, l, hd, :], KB["gdn_S"]
            o, ob = Bk["o"]
            sb2 = sbfs[it_ % 2]
            vn2 = [Bk["vn0"], Bk["vn1"]]
            o12 = [Bk["o1a"], Bk["o1b"]]
            self.copy(sb2[0][0][:, :], S32, [S32b], [sb2[0][1]], eng="act")
            for b in range(4):
                cs = slice(b * 128, (b + 1) * 128)
                gb = gb0 + b
                Sbf, Sbfb = sb2[b % 2][0][:, :], sb2[b % 2][1]
                Sbn, Sbnb = sb2[(b + 1) % 2][0][:, :], sb2[(b + 1) % 2][1]
                vnew, vnewb = vn2[b % 2]
                o1s, o1sb = o12[b % 2]
                p1, p1b = self.ps_next(RB)
                self.mm(p1[:, 0:128], wT[:, cs], Sbf, True, True, [wTb, Sbfb], [p1b])
                p1o, p1ob = self.ps_next(RB)
                self.mm(p1o[:, 0:128], qnT[:, cs], Sbf, True, True, [qnTb, Sbfb], [p1ob])
                yield 0
                self.tt(vnew[:, 0:128], u[:, cs], p1[:, 0:128], ALU.subtract, [ub, p1b], [vnewb])
                self.act(o1s[:, 0:128].bitcast(F32R), p1o[:, 0:128], AF.Copy, [p1ob, GAB], [o1sb],
                         scale=GA[:, 1, gb, hd:hd + 1])
                p2, p2b = self.ps_next(RB)
                self.mm(p2[:, 0:128], kdec[:, cs], vnew[:, 0:128], True, True, [kdecb, vnewb], [p2b])
                p2o, p2ob = self.ps_next(RB)
                self.mm(p2o[:, 0:128], attnT[:, cs], vnew[:, 0:128], True, True, [attnTb, vnewb], [p2ob])
                yield 0
                self.stt(S32, S32, GA[:, 3, gb, hd:hd + 1], p2[:, 0:128], ALU.mult, ALU.add,
                         [S32b, GAB, p2b], [S32b])
                self.tt(o[:, cs].bitcast(F32R), p2o[:, 0:128], o1s[:, 0:128], ALU.add, [p2ob, o1sb], [ob])
                if b < 3:
                    self.copy(Sbn, S32, [S32b], [Sbnb], eng="act")
            self._xch_flush()
            sq, sqb_ = Bk["sq"]
            self.tt(sq[:, W].bitcast(F32R), o[:, W], o[:, W], ALU.mult, [ob], [sqb_], eng="pool")
            self.sch.op("dve", lambda E, o_=small[:, 0:4], i_=v4(sq[:, W]):
                        E.tensor_reduce(out=o_, in_=i_, axis=mybir.AxisListType.X, op=ALU.add), [sqb_], [smallb])
            self.act(small[:, 4:8], small[:, 0:4], AF.Ln, [smallb, KB["epsk"]], [smallb], bias=self.eps_ap,
                     scale=1.0 / 128)
            self.act(small[:, 8:12], small[:, 4:8], AF.Exp, [smallb], [smallb], scale=-0.5)
            yield 0
            on, onb = Bk["on"]
            self.tt(v4(on[:, W].bitcast(F32R)), v4(o[:, W]), small[:, 8:12].unsqueeze(2).to_broadcast([128, 4, 128]), ALU.mult,
                    [ob, smallb], [onb])
            po, pob = self.tr4(on, onb, ring=RB)
            yield 0
            szT, szTb = Hh["szT"]
            self.stt(self.yTl[:, hd, sl], po[:, :], K["gdn_onorm_g"][:, l:l + 1], szT[:, W], ALU.mult, ALU.mult,
                     [pob, KB["gdn_onorm_g"], szTb], [self.ylb[hd][sub]])
            if sub == self.NSUB - 1:
                self.exchange_part(hd)

        prefetch_head(0)
        self.run_pipelined3([gdn_iter(hd, sub, hd * self.NSUB + sub)
                             for hd in range(NH) for sub in range(self.NSUB)], "gdn")
        self.outproj_phase(li, d["gdn_out_w"][l])


def _prep_inputs(inp, b, r=0):
    f = np.float32
    c = np.ascontiguousarray
    bl = slice(r * NBK, (r + 1) * NBK)

    def pm(v):
        v = np.asarray(v, f)
        sh = v.shape[:-1]
        v = v.reshape(sh + (KC, 128))
        return c(np.moveaxis(v, -1, 0))

    def tile_w(w):
        w = np.asarray(w, f)
        lead = w.shape[:-2]
        nblk = w.shape[-1] // 128
        w = w.reshape(lead + (KC, 128, nblk, 128))
        nd = len(lead)
        return c(w.transpose(tuple(range(nd)) + (nd + 2, nd + 1, nd, nd + 3)))

    m = {}
    m["xT"] = c(np.asarray(inp["x"][b], f).T)
    m["cT"] = c(np.asarray(inp["c"][b], f).reshape(KC, 128).T)
    m["ada_w"] = tile_w(np.asarray(inp["ada_w"], f)[:, :, r * 1536:(r + 1) * 1536])
    m["ada_b"] = c(np.moveaxis(np.asarray(inp["ada_b"], f).reshape(4, 24, 128), -1, 0)[:, :, r * 12:(r + 1) * 12])
    m["norm_g"] = pm(inp["norm_g"])
    m["final_g"] = pm(inp["final_g"])
    w = np.asarray(inp["lru_in_w"], f).reshape(2, D, 2, KC, 128)
    m["lru_in_w"] = tile_w(w[:, :, :, bl, :].reshape(2, D, 2 * NBK * 128))
    cw = np.asarray(inp["lru_conv_w"], f).reshape(2, 4, KC, 128)
    m["lru_conv_w"] = c(cw.transpose(3, 0, 2, 1)[:, :, bl, :])
    m["lru_conv_b"] = c(pm(inp["lru_conv_b"])[:, :, bl])
    m["lru_gate_w"] = c(np.asarray(inp["lru_gate_w"], f)[:, :, bl])
    m["lru_gate_b"] = c(pm(inp["lru_gate_b"])[:, :, :, bl])
    m["lru_lambda"] = c(pm(inp["lru_lambda"])[:, :, bl])
    m["lru_out_w"] = tile_w(inp["lru_out_w"])
    gw_ = np.asarray(inp["gdn_in_w"], f)
    qkvz = gw_[:, :, :4096].reshape(2, D, 4, KC, 128)[:, :, :, bl, :].reshape(2, D, 4 * NBK * 128)
    a_ = gw_[:, :, 4096:4104][:, :, bl]
    b_ = gw_[:, :, 4104:4112][:, :, bl]
    m["gdn_in_w"] = tile_w(qkvz)
    ab = np.concatenate([a_, b_], axis=2)
    m["gdn_ab_w"] = c(ab.reshape(2, KC, 128, 2 * NBK).transpose(0, 2, 1, 3))
    gc = np.asarray(inp["gdn_conv_w"], f).reshape(2, 4, 3, KC, 128)
    gc = gc.transpose(4, 0, 2, 3, 1)[:, :, :, bl, :]
    m["gdn_conv_w"] = c(gc.reshape(128, 2, 3 * NBK, 4))
    m["gdn_a_log"] = c(np.broadcast_to(np.asarray(inp["gdn_a_log"], f)[None][:, :, bl], (128, 2, NBK)))
    m["gdn_dt_bias"] = c(np.broadcast_to(np.asarray(inp["gdn_dt_bias"], f)[None][:, :, bl], (128, 2, NBK)))
    m["gdn_onorm_g"] = c(np.asarray(inp["gdn_onorm_g"], f).T)
    m["gdn_out_w"] = tile_w(inp["gdn_out_w"])
    m["masks"] = _masks()
    return m


def _masks():
    i = np.arange(128)[:, None]
    j = np.arange(128)[None, :]
    M = np.zeros((128, NMASK, 128), np.float32)
    M[:, 0] = (i == j)
    M[:, 1] = 1.0
    M[:, 2] = np.where(j >= i, 0.0, -30000.0)
    M[:, 3] = np.where(i > j, 0.0, -30000.0)
    same16 = (i // 16) == (j // 16)
    M[:, 4] = -1.0 * ((i < j) & same16)
    M[:, 5] = -1.0 * ((i > j) & same16)
    for k, b in enumerate((16, 32, 64)):
        same_b = (i // b) == (j // b)
        same_2b = (i // (2 * b)) == (j // (2 * b))
        M[:, 6 + k] = (i > j) & same_2b & ~same_b
    M[:, 9] = (i <= j)
    return M


_PROG = {}


def kernel(**inputs):
    key = "full"
    if key not in _PROG:
        _PROG[key] = Prog()
    prog = _PROG[key]
    in_maps = [_prep_inputs(inputs, core // 2, core % 2) for core in range(NCORES)]
    res = run_bass_kernel_spmd(prog.nc, in_maps, core_ids=list(range(NCORES)))
    out = np.stack([np.ascontiguousarray(res.results[2 * b]["yT"].T) for b in range(BATCH)], axis=0)
    return out.astype(np.float32)
```

```python
import numpy as np
from contextlib import ExitStack
import concourse.bass as bass
import concourse.mybir as mybir
from concourse.bass_utils import run_bass_kernel_spmd

F32 = mybir.dt.float32
BF16 = mybir.dt.bfloat16
F32R = mybir.dt.float32r
AF = mybir.ActivationFunctionType
ALU = mybir.AluOpType

D = 1024
KC = 8
SEQ = 4096
BATCH = 4
NCORES = 8
EPS = 1e-6
SEM_PER = 30000
NMASK = 10
NBK = 4


class Ctr:
    def __init__(self, nc, es, name, step):
        self.nc, self.es, self.name, self.step = nc, es, name, step
        self.sems = []
        self.n = 0

    def _sem(self, idx):
        while idx >= len(self.sems):
            self.sems.append(self.es.enter_context(self.nc.semaphore(f"{self.name}_{len(self.sems)}")))
        return self.sems[idx]

    def ref(self, n):
        return self._sem((n - 1) // SEM_PER), ((n - 1) % SEM_PER + 1) * self.step

    def next(self):
        self.n += 1
        return self.n


class Buf:
    __slots__ = ("w", "r", "name", "excl")

    def __init__(self, name="", excl=False):
        self.w = None
        self.r = {}
        self.name = name
        self.excl = excl


class Sched:
    ENG = ("sync", "act", "dve", "pool", "pe")

    def __init__(self, nc, es):
        self.nc, self.es = nc, es
        self.streams = {e: [] for e in self.ENG}
        self.ctr = {e: Ctr(nc, es, "c_" + e, 1) for e in self.ENG if e != "sync"}
        self.waited = {e: {} for e in self.ENG}
        self.nops = 0

    def op(self, eng, fn, reads=(), writes=(), dma=None):
        deps = {}

        def add(tok):
            if tok is None:
                return
            c, n = tok
            if deps.get(c, 0) < n:
                deps[c] = n

        ex = [b for b in reads if b.excl]
        if ex:
            reads = [b for b in reads if not b.excl]
            writes = list(writes) + ex
        for b in reads:
            add(b.w)
        for b in writes:
            add(b.w)
            for c, n in b.r.items():
                add((c, n))
        waits = []
        wd = self.waited[eng]
        for c, n in deps.items():
            if eng == "pe" and c is self.ctr["pe"]:
                continue
            if wd.get(c, 0) < n:
                wd[c] = n
                waits.append(c.ref(n))
        if dma is not None:
            c = dma
        else:
            c = self.ctr[eng]
        n = c.next()
        sem, val = c.ref(n)
        inc = c.step
        self.streams[eng].append((waits, fn, sem, inc))
        for b in reads:
            if b.r.get(c, 0) < n:
                b.r[c] = n
        for b in writes:
            b.w = (c, n)
            b.r = {}
        self.nops += 1
        return (c, n)

    def emit(self, final_waits):
        nc = self.nc
        streams = self.streams

        def run(E, name):
            for waits, fn, sem, inc in streams[name]:
                for s, v in waits:
                    E.wait_ge(s, v)
                fn(E).then_inc(sem, inc)

        with nc.Block() as block:
            @block.sync
            def _(E):
                run(E, "sync")
                for c, n in final_waits:
                    s, v = c.ref(n)
                    E.wait_ge(s, v)

            @block.scalar
            def _(E):
                run(E, "act")

            @block.vector
            def _(E):
                run(E, "dve")

            @block.gpsimd
            def _(E):
                run(E, "pool")

            @block.tensor
            def _(E):
                run(E, "pe")


class Prog:
    def __init__(self, S=SEQ, T=1024, layers=(0, 1, 2, 3), final=True, debug=False):
        self.S, self.T, self.layers, self.final = S, T, layers, final
        self.debug = debug
        self.dbg_outs = {}
        self._pipe_counts = {}
        self.NT = S // T
        self.NSUB = T // 512
        self.nc = bass.Bass("TRN2", target_bir_lowering=False)
        self.es = ExitStack()
        self.sch = Sched(self.nc, self.es)
        self._n = 0
        self.build()

    def sb(self, shape, dt=F32, name=None):
        self._n += 1
        return self.es.enter_context(self.nc.sbuf_tensor(f"{name or 't'}{self._n}", list(shape), dt))

    def dram_in(self, name, shape):
        return self.nc.dram_tensor(name, list(shape), F32, kind="ExternalInput").ap()

    def dmactr(self, name):
        self._n += 1
        return Ctr(self.nc, self.es, f"d_{name}{self._n}", 16)

    def dma(self, out, in_, ctr, reads=(), writes=()):
        return self.sch.op("sync", lambda E: E.dma_start(out=out, in_=in_), reads, writes, dma=ctr)

    def act(self, out, in_, func, reads, writes, bias=None, scale=None):
        kw = {}
        if bias is not None:
            kw["bias"] = bias
        if scale is not None:
            kw["scale"] = scale
        return self.sch.op("act", lambda E: E.activation(out=out, in_=in_, func=func, **kw), reads, writes)

    def tt(self, out, in0, in1, op, reads, writes, eng="dve"):
        return self.sch.op(eng, lambda E: E.tensor_tensor(out=out, in0=in0, in1=in1, op=op), reads, writes)

    def ts(self, out, in0, s1, s2, op0, op1, reads, writes, eng="dve"):
        if op1 is None:
            return self.sch.op(eng, lambda E: E.tensor_scalar(out=out, in0=in0, scalar1=s1, scalar2=None, op0=op0),
                               reads, writes)
        return self.sch.op(eng, lambda E: E.tensor_scalar(out=out, in0=in0, scalar1=s1, scalar2=s2, op0=op0, op1=op1),
                           reads, writes)

    def stt(self, out, in0, scalar, in1, op0, op1, reads, writes):
        return self.sch.op("dve", lambda E: E.scalar_tensor_tensor(out=out, in0=in0, scalar=scalar, in1=in1,
                                                                    op0=op0, op1=op1), reads, writes)

    def copy(self, out, in_, reads, writes, eng="pool"):
        if eng == "act":
            return self.act(out, in_, AF.Copy, reads, writes)
        return self.sch.op(eng, lambda E: E.tensor_copy(out=out, in_=in_), reads, writes)

    def mm(self, out, lhsT, rhs, start, stop, reads, writes):
        return self.sch.op("pe", lambda E: E.matmul(out, lhsT, rhs, start=start, stop=stop), reads, writes)

    def tr(self, out, in_, ident, reads, writes):
        return self.sch.op("pe", lambda E: E.transpose(out, in_, ident), reads, writes)

    def memset(self, ap, val, writes, eng="pool"):
        return self.sch.op(eng, lambda E: E.memset(ap, val), (), writes)

    def dbg(self, name, ap, buf, dt=F32):
        if not self.debug:
            return
        t = self.nc.dram_tensor("dbg_" + name, list(ap.shape), dt, kind="ExternalOutput").ap()
        self.dbg_outs[name] = t
        self.dma(t, ap, self.octr, reads=[buf])

    def ps_next(self, ring=None):
        if ring is None:
            i = self._psi
            self._psi = (i + 1) % len(self.psb)
        else:
            base, n = {"F": (0, 2), "B": (2, 3), "B0": (2, 3), "B1": (5, 3),
                       "LF": (0, 2), "LZ": (2, 2), "LB": (4, 4)}[ring]
            key = "B0" if ring == "B" else ring
            j = self._psr.get(key, 0)
            self._psr[key] = (j + 1) % n
            i = base + j
        return self.psb[i], self.psbuf[i]

    def load_w(self, dram_block):
        i = self._wsi
        self._wsi = (i + 1) % len(self.wst)
        st, stb, ctr = self.wst[i], self.wstbuf[i], self.wstctr[i]
        self.dma(st[:, :, :], dram_block, ctr, writes=[stb])
        j = self._wbi
        self._wbi = (j + 1) % len(self.wbf)
        wb, wbb = self.wbf[j], self.wbfbuf[j]
        self.copy(wb[:, :, :], st[:, :, :], [stb], [wbb], eng="dve")
        return wb, wbb

    def load_small(self, dst_ap, src_ap, buf):
        c = self.dmactr("k")
        self.dma(dst_ap, src_ap, c, writes=[buf])

    def build(self):
        nc, S, T = self.nc, self.S, self.T
        NL = 4
        d = {}
        d["xT"] = self.dram_in("xT", [D, S])
        d["cT"] = self.dram_in("cT", [128, KC])
        d["ada_w"] = self.dram_in("ada_w", [NL, 12, 128, KC, 128])
        d["ada_b"] = self.dram_in("ada_b", [128, NL, 12])
        d["norm_g"] = self.dram_in("norm_g", [128, NL, KC])
        d["final_g"] = self.dram_in("final_g", [128, KC])
        d["lru_in_w"] = self.dram_in("lru_in_w", [2, 2 * NBK, 128, KC, 128])
        d["lru_conv_w"] = self.dram_in("lru_conv_w", [128, 2, NBK, 4])
        d["lru_conv_b"] = self.dram_in("lru_conv_b", [128, 2, NBK])
        d["lru_gate_w"] = self.dram_in("lru_gate_w", [2, 2, NBK, 128, 128])
        d["lru_gate_b"] = self.dram_in("lru_gate_b", [128, 2, 2, NBK])
        d["lru_lambda"] = self.dram_in("lru_lambda", [128, 2, NBK])
        d["lru_out_w"] = self.dram_in("lru_out_w", [2, KC, 128, KC, 128])
        d["gdn_in_w"] = self.dram_in("gdn_in_w", [2, 4 * NBK, 128, KC, 128])
        d["gdn_ab_w"] = self.dram_in("gdn_ab_w", [2, 128, KC, 2 * NBK])
        d["gdn_conv_w"] = self.dram_in("gdn_conv_w", [128, 2, 3 * NBK, 4])
        d["gdn_a_log"] = self.dram_in("gdn_a_log", [128, 2, NBK])
        d["gdn_dt_bias"] = self.dram_in("gdn_dt_bias", [128, 2, NBK])
        d["gdn_onorm_g"] = self.dram_in("gdn_onorm_g", [128, 2])
        d["gdn_out_w"] = self.dram_in("gdn_out_w", [2, KC, 128, KC, 128])
        d["masks"] = self.dram_in("masks", [128, NMASK, 128])
        self.d = d
        self.yT = nc.dram_tensor("yT", [D, S], F32, kind="ExternalOutput").ap()

        self.psb = [self.es.enter_context(nc.psum_tensor(f"ps{i}", [128, 512], F32)) for i in range(8)]
        self.psbuf = [Buf(f"ps{i}", excl=True) for i in range(8)]
        self._psi = 0
        self._psr = {}
        self.wst = [self.sb([128, KC, 128], F32, "wst") for _ in range(3)]
        self.wstbuf = [Buf() for _ in self.wst]
        self.wstctr = [self.dmactr("wst") for _ in self.wst]
        self._wsi = 0
        self.wbf = [self.sb([128, KC, 128], BF16, "wbf") for _ in range(6)]
        self.wbfbuf = [Buf() for _ in self.wbf]
        self._wbi = 0

        self.K = {}
        self.KB = {}

        def const(name, shape, src=None, dt=F32):
            t = self.sb(shape, dt, name)
            b = Buf(name)
            self.K[name], self.KB[name] = t, b
            if src is not None:
                self.load_small(t[:], src, b)
            return t

        const("masks", [128, NMASK, 128], d["masks"][:, :, :])
        const("cT", [128, KC], d["cT"][:, :])
        const("ada_b", [128, NL, 12], d["ada_b"][:, :, :])
        const("condh", [128, NL, 12])
        const("norm_g", [128, NL, KC], d["norm_g"][:, :, :])
        const("final_g", [128, KC], d["final_g"][:, :])
        const("lru_conv_w", [128, 2, NBK, 4], d["lru_conv_w"][:, :, :, :])
        const("lru_conv_b", [128, 2, NBK], d["lru_conv_b"][:, :, :])
        const("lru_gate_b", [128, 2, 2, NBK], d["lru_gate_b"][:, :, :, :])
        const("lru_lambda", [128, 2, NBK], d["lru_lambda"][:, :, :])
        const("gdn_conv_w", [128, 2, 3 * NBK, 4], d["gdn_conv_w"][:, :, :, :])
        const("gdn_a_log", [128, 2, NBK], d["gdn_a_log"][:, :, :])
        const("gdn_dt_bias", [128, 2, NBK], d["gdn_dt_bias"][:, :, :])
        const("gdn_onorm_g", [128, 2], d["gdn_onorm_g"][:, :])
        ones_bf = const("ones_bf", [128, 128], None, BF16)
        self.memset(ones_bf[:], 1.0, [self.KB["ones_bf"]])
        epsk = const("epsk", [128, 3])
        self.memset(epsk[:, 0:1], EPS, [self.KB["epsk"]])
        self.memset(epsk[:, 1:2], 1.0, [self.KB["epsk"]])
        self.memset(epsk[:, 2:3], float(np.log(0.5)), [self.KB["epsk"]])
        self.lnhalf_ap = epsk[:, 2:3]
        const("gbh", [128, 2, 2, NBK])
        const("clamh", [128, 2, NBK])
        self.eps_ap = epsk[:, 0:1]
        self.one_ap = epsk[:, 1:2]
        const("cond", [128, NL, 24])
        const("gs", [128, NL, KC])
        const("cact", [128, KC])
        const("clam", [128, 2, NBK])
        const("clam2", [128, 2, NBK])
        const("lru_state", [128, 2, NBK])
        const("lru_halo", [128, 2, NBK, 3])
        const("nexpalog", [128, 2, NBK])
        const("GA", [128, 6, T // 128, NBK])
        const("gdn_S", [128, 2, NBK, 128])
        const("sbf0", [128, 128], None, BF16)
        const("sbf1", [128, 128], None, BF16)
        const("gdn_halo", [128, 2, 3 * NBK, 3])
        const("dconv", [128, 12, 128], None, BF16)
        const("gsmall", [128, 16])
        const("gsmall1", [128, 16])
        const("sbf2", [128, 128], None, BF16)
        const("sbf3", [128, 128], None, BF16)
        const("wab", [128, KC, 2 * NBK], None, BF16)
        self.memset(self.K["gdn_S"][:], 0.0, [self.KB["gdn_S"]])
        self.memset(self.K["gdn_halo"][:], 0.0, [self.KB["gdn_halo"]])
        self.memset(self.K["lru_state"][:], 0.0, [self.KB["lru_state"]])
        self.memset(self.K["lru_halo"][:], 0.0, [self.KB["lru_halo"]])

        self.xT = self.sb([128, KC, T], F32, "xT")
        self.hT = self.sb([128, KC, T], BF16, "hT")
        self.yTs = self.sb([128, KC, T], BF16, "yTs")
        self.yTl = self.sb([128, NBK, T], BF16, "yTl")
        self.ylb = [[Buf() for _ in range(self.NSUB)] for _ in range(NBK)]
        NX = 2 * NBK
        self.xch_in = [nc.dram_tensor(f"xch_in{i}", [128, T], BF16) for i in range(NX)]
        self.xch_out = [nc.dram_tensor(f"xch_out{i}", [2 * 128, T], BF16) for i in range(NX)]
        self.xch_inb = [Buf() for _ in range(NX)]
        self.xch_outb = [Buf() for _ in range(NX)]
        self.xch_c1 = [self.dmactr("xi") for _ in range(NX)]
        self.xch_c2 = [self.dmactr("xo") for _ in range(NX)]
        self.cc_ctr = Ctr(nc, self.es, "cc", 1)
        self._xchi = 0
        self._op_pre = None
        self._presq = None
        self._xch_pending = []
        self.xb = [[Buf() for _ in range(self.NSUB)] for _ in range(KC)]
        self.hb = [[Buf() for _ in range(self.NSUB)] for _ in range(KC)]
        self.yb = [[Buf() for _ in range(self.NSUB)] for _ in range(KC)]
        self.xctr = [self.dmactr("x") for _ in range(KC)]
        self.octr = self.dmactr("o")

        self.scr = [self.sb([128, 520], F32, "scr") for _ in range(33)]
        self.scrbuf = [Buf() for _ in self.scr]
        self._sci = 0
        self.scrb = [self.sb([128, 520], BF16, "scrb") for _ in range(27)]
        self.scrbbuf = [Buf() for _ in self.scrb]
        self._scbi = 0

        self.prologue()
        last = None
        for ti in range(self.NT):
            self.load_x(ti)
            for li in self.layers:
                if li % 2 == 0:
                    self.lru_layer(li, ti)
                else:
                    self.gdn_layer(li, ti)
            last = self.store_out(ti)
        self.sch.emit([(self.octr, self.octr.n)])

    def scratch(self):
        i = self._sci
        self._sci = (i + 1) % 12
        return self.scr[i], self.scrbuf[i]

    def scratch_bf(self):
        i = self._scbi
        self._scbi = (i + 1) % 4
        return self.scrb[i], self.scrbbuf[i]

    def prologue(self):
        K, KB, d = self.K, self.KB, self.d
        self.act(K["cact"][:], K["cT"][:], AF.Silu, [KB["cT"]], [KB["cact"]])
        ast, astb, astc = self.wst[:2], self.wstbuf[:2], self.wstctr[:2]
        q = 0
        row, rowb = self.scratch()
        for li in range(4):
            banks = [self.ps_next() for _ in range(3)]
            for oc in range(12):
                st, stb, sc = ast[q % 2], astb[q % 2], astc[q % 2]
                q += 1
                src = d["ada_w"][li, oc]
                self.dma(st[:, :, :], src, sc, writes=[stb])
                ps, psb = banks[oc // 4]
                cs = slice((oc % 4) * 128, (oc % 4 + 1) * 128)
                for kc in range(KC):
                    self.mm(ps[0:1, cs], K["cact"][:, kc:kc + 1], st[:, kc, :],
                            kc == 0, kc == KC - 1, [stb, KB["cact"]], [psb])
            psT, psTb = self.ps_next()
            for g in range(3):
                ps, psb = banks[g]
                self.act(row[0:1, 0:512], ps[0:1, :], AF.Copy, [psb], [rowb])
                for o4 in range(4):
                    oc = g * 4 + o4
                    self.mm(psT[:, oc:oc + 1], row[0:1, o4 * 128:(o4 + 1) * 128], self.one_ap[0:1, :],
                            True, True, [rowb, KB["epsk"]], [psTb])
            self.tt(K["condh"][:, li, :], psT[:, 0:12], K["ada_b"][:, li, :], ALU.add,
                    [psTb, KB["ada_b"]], [KB["condh"]])
        cin = self.nc.dram_tensor("cond_in", [128, 48], F32)
        cout = self.nc.dram_tensor("cond_out", [256, 48], F32)
        cinb, coutb = Buf(), Buf()
        c1, c2 = self.dmactr("ci"), self.dmactr("co")
        self.dma(cin.ap(), K["condh"][:].rearrange("p l c -> p (l c)"), c1, reads=[KB["condh"]], writes=[cinb])
        self.sch.op("pool", lambda E: E.collective_compute(
            "AllGather", ALU.bypass, replica_groups=[[0, 1], [2, 3], [4, 5], [6, 7]],
            ins=[cin.ap().opt()], outs=[cout.ap().opt()]), [cinb], [coutb], dma=self.cc_ctr)
        for r_ in range(2):
            self.dma(K["cond"][:, :, r_ * 12:(r_ + 1) * 12],
                     cout.ap()[r_ * 128:(r_ + 1) * 128, :].rearrange("p (l c) -> p l c", l=4), c2,
                     reads=[coutb], writes=[KB["cond"]])
        for li in range(4):
            self.stt(K["gs"][:, li, :], K["cond"][:, li, 8:16], 1.0, K["norm_g"][:, li, :], ALU.add, ALU.mult,
                     [KB["cond"], KB["norm_g"]], [KB["gs"]])
        self.act(K["nexpalog"][:], K["gdn_a_log"][:], AF.Exp, [KB["gdn_a_log"]], [KB["nexpalog"]])
        self.ts(K["nexpalog"][:], K["nexpalog"][:], -1.0, None, ALU.mult, None, [KB["nexpalog"]], [KB["nexpalog"]])
        t, tb = self.scratch()
        self.act(t[:, 0:2 * NBK], K["lru_lambda"][:].rearrange("p a b -> p (a b)"), AF.Exp, [KB["lru_lambda"]], [tb],
                 scale=-1.0)
        self.act(t[:, 16:16 + 2 * NBK], t[:, 0:2 * NBK], AF.Ln, [tb, KB["epsk"]], [tb], bias=self.one_ap)
        self.ts(K["clam"][:].rearrange("p a b -> p (a b)"), t[:, 16:16 + 2 * NBK], -8.0, None, ALU.mult, None, [tb], [KB["clam"]])
        self.ts(K["clam2"][:].rearrange("p a b -> p (a b)"), t[:, 16:16 + 2 * NBK], -16.0, None, ALU.mult, None, [tb],
                [KB["clam2"]])
        self.ts(K["clamh"][:].rearrange("p a b -> p (a b)"), t[:, 16:16 + 2 * NBK], -4.0, None, ALU.mult, None, [tb],
                [KB["clamh"]])
        self.ts(K["gbh"][:].rearrange("p a b c -> p (a b c)"), K["lru_gate_b"][:].rearrange("p a b c -> p (a b c)"),
                0.5, None, ALU.mult, None, [KB["lru_gate_b"]], [KB["gbh"]])

    def load_x(self, ti):
        T = self.T
        for kc in range(KC):
            self.dma(self.xT[:, kc, :], self.d["xT"][kc * 128:(kc + 1) * 128, ti * T:(ti + 1) * T], self.xctr[kc],
                     writes=self.xb[kc])

    def norm_phase(self, gs_ap_fn, shift_ap_fn, out_fn):
        K, KB = self.K, self.KB
        presq = self._presq
        self._presq = None
        for sub in range(self.NSUB):
            sl = slice(sub * 512, (sub + 1) * 512)
            if presq is not None:
                ps, psb = presq[sub]
            else:
                ps, psb = self.ps_next()
                for kc in range(KC):
                    sq, sqb = self.scratch_bf()
                    self.act(sq[:, 0:512], self.xT[:, kc, sl], AF.Square, [self.xb[kc][sub]], [sqb])
                    self.mm(ps[:, :], K["ones_bf"][:], sq[:, 0:512], kc == 0, kc == KC - 1,
                            [KB["ones_bf"], sqb], [psb])
            rs, rsb = self.scratch()
            self.act(rs[:, 0:512], ps[:, :], AF.Ln, [psb, KB["epsk"]], [rsb], bias=self.eps_ap, scale=1.0 / D)
            rstd, rstdb = self.scratch()
            self.act(rstd[:, 0:512], rs[:, 0:512], AF.Exp, [rsb], [rstdb], scale=-0.5)
            for kc in range(KC):
                out_fn(kc, sub, sl, rstd, rstdb)

    def store_out(self, ti):
        K, KB, T = self.K, self.KB, self.T
        if not self.final:
            for kc in range(KC):
                self.dma(self.yT[kc * 128:(kc + 1) * 128, ti * T:(ti + 1) * T], self.xT[:, kc, :], self.octr,
                         reads=self.xb[kc])
            return

        def out_fn(kc, sub, sl, rstd, rstdb):
            o, ob = self.scratch()
            self.stt(o[:, 0:512], self.xT[:, kc, sl], K["final_g"][:, kc:kc + 1], rstd[:, 0:512], ALU.mult, ALU.mult,
                     [self.xb[kc][sub], KB["final_g"], rstdb], [ob])
            self.dma(self.yT[kc * 128:(kc + 1) * 128, ti * T + sub * 512: ti * T + (sub + 1) * 512], o[:, 0:512],
                     self.octr, reads=[ob])

        self.norm_phase(None, None, out_fn)

    def mod_phase(self, li):
        K, KB = self.K, self.KB

        def out_fn(kc, sub, sl, rstd, rstdb):
            t, tb = self.scratch()
            self.stt(t[:, 0:512], self.xT[:, kc, sl], K["gs"][:, li, kc:kc + 1], rstd[:, 0:512], ALU.mult, ALU.mult,
                     [self.xb[kc][sub], KB["gs"], rstdb], [tb])
            self.act(self.hT[:, kc, sl], t[:, 0:512], AF.Identity, [tb, KB["cond"]], [self.hb[kc][sub]],
                     bias=K["cond"][:, li, kc:kc + 1])

        self.norm_phase(None, None, out_fn)

    def exchange_part(self, j):
        i = (self._xchi % 2) * NBK + j
        xin, xout = self.xch_in[i], self.xch_out[i]
        self._xch_flush()
        self.dma(xin.ap(), self.yTl[:, j, :], self.xch_c1[i], reads=self.ylb[j], writes=[self.xch_inb[i]])
        self.sch.op("pool", lambda E: E.collective_compute(
            "AllGather", ALU.bypass, replica_groups=[[0, 1], [2, 3], [4, 5], [6, 7]],
            ins=[xin.ap().opt()], outs=[xout.ap().opt()]),
            [self.xch_inb[i]], [self.xch_outb[i]], dma=self.cc_ctr)
        self._xch_pending.append((i, j))

    def _xch_flush(self):
        for i, j in self._xch_pending:
            xout = self.xch_out[i]
            dst = self.yTs[:, :, :].rearrange("p (r g) t -> p r g t", r=2)[:, :, j, :]
            self.dma(dst, xout.ap().rearrange("(r p) t -> p r t", p=128), self.xch_c2[i],
                     reads=[self.xch_outb[i]], writes=self.yb[j] + self.yb[NBK + j])
        self._xch_pending = []

    def exchange(self):
        self._xch_flush()
        self._xchi += 1

    def outproj_phase(self, li, w_dram):
        K, KB = self.K, self.KB
        late = (NBK - 1, 2 * NBK - 1)
        early = [n for n in range(KC) if n not in late]
        pre = self._op_pre if self._op_pre is not None else [self.load_w(w_dram[j]) for j in range(2)]
        self._op_pre = None
        r_ = 0
        for j in range(KC):
            wb, wbb = pre[j] if j < len(pre) else self.load_w(w_dram[j])
            for sub in range(self.NSUB):
                sl = slice(sub * 512, (sub + 1) * 512)
                ps, psb = self.psb[r_ % 6], self.psbuf[r_ % 6]
                r_ += 1
                for k, n in enumerate(early):
                    self.mm(ps[:, :], wb[:, n, :], self.yTs[:, n, sl], k == 0, k == len(early) - 1,
                            [wbb, self.yb[n][sub]], [psb])
                self.stt(self.xT[:, j, sl], ps[:, :], K["cond"][:, li, 16 + j:17 + j], self.xT[:, j, sl],
                         ALU.mult, ALU.add, [psb, KB["cond"], self.xb[j][sub]], [self.xb[j][sub]])
        wl = [self.load_w(w_dram[:, :, n, :].rearrange("j p c -> p j c")) for n in late]
        self.exchange()
        sqacc = [(self.psb[6 + sub], self.psbuf[6 + sub]) for sub in range(self.NSUB)]
        pend = []

        def flush(keep):
            while len(pend) > keep:
                sq_, sqb_, sub_, j_ = pend.pop(0)
                self.mm(sqacc[sub_][0][:, :], K["ones_bf"][:], sq_[:, 0:512], j_ == 0, j_ == KC - 1,
                        [KB["ones_bf"], sqb_], [sqacc[sub_][1]])

        for j in range(KC):
            for sub in range(self.NSUB):
                sl = slice(sub * 512, (sub + 1) * 512)
                ps, psb = self.psb[r_ % 6], self.psbuf[r_ % 6]
                r_ += 1
                for k, n in enumerate(late):
                    self.mm(ps[:, :], wl[k][0][:, j, :], self.yTs[:, n, sl], k == 0, k == len(late) - 1,
                            [wl[k][1], self.yb[n][sub]], [psb])
                flush(2)
                self.stt(self.xT[:, j, sl], ps[:, :], K["cond"][:, li, 16 + j:17 + j], self.xT[:, j, sl],
                         ALU.mult, ALU.add, [psb, KB["cond"], self.xb[j][sub]], [self.xb[j][sub]])
                sq, sqb = self.scratch_bf()
                self.act(sq[:, 0:512], self.xT[:, j, sl], AF.Square, [self.xb[j][sub]], [sqb])
                pend.append((sq, sqb, sub, j))
        flush(0)
        self._presq = sqacc

    def lru_layer(self, li, ti):
        K, KB, d = self.K, self.KB, self.d
        l = li // 2
        self.mod_phase(li)
        W = slice(0, 512)
        k_ = 0
        Fb = {}
        for nm in ["pre", "r", "it", "a", "m", "hs"]:
            Fb[nm] = (self.scr[k_], self.scrbuf[k_])
            k_ += 1
        H = [{}, {}]
        for par in range(2):
            for nm in ["xc", "sz"]:
                H[par][nm] = (self.scr[k_], self.scrbuf[k_])
                k_ += 1
            H[par]["xcf"] = (self.scrb[2 + par], self.scrbbuf[2 + par])
        cw = K["lru_conv_w"]
        wcache = {}

        def load_block_weights(n):
            wx = self.load_w(d["lru_in_w"][l, n])
            wz = self.load_w(d["lru_in_w"][l, NBK + n])
            gw = []
            for k in range(2):
                i = self._wsi
                self._wsi = (i + 1) % len(self.wst)
                st, stb, ctr = self.wst[i], self.wstbuf[i], self.wstctr[i]
                self.dma(st[:, 0, :], d["lru_gate_w"][l, k, n, :, :], ctr, writes=[stb])
                jj = self._wbi
                self._wbi = (jj + 1) % len(self.wbf)
                wb, wbb = self.wbf[jj], self.wbfbuf[jj]
                self.copy(wb[:, 0, :], st[:, 0, :], [stb], [wbb], eng="pool")
                gw.append((wb, wbb))
            return wx, wz, gw

        def lru_iter(n, sub, par):
            Hh = H[par]
            sl = slice(sub * 512, (sub + 1) * 512)
            if n not in wcache:
                wcache[n] = load_block_weights(n)
            (wx, wxb), (wz, wzb), gw = wcache[n]
            prefetch_next = (sub == self.NSUB - 1 and n + 1 < NBK)
            psx, psxb = self.ps_next("LF")
            for kc in range(KC):
                self.mm(psx[:, :], wx[:, kc, :], self.hT[:, kc, sl], kc == 0, kc == KC - 1,
                        [wxb, self.hb[kc][sub]], [psxb])
                if kc % 4 == 3:
                    yield 0
            psz, pszb = self.ps_next("LZ")
            for kc in range(KC):
                self.mm(psz[:, :], wz[:, kc, :], self.hT[:, kc, sl], kc == 0, kc == KC - 1,
                        [wzb, self.hb[kc][sub]], [pszb])
                if kc % 4 == 3:
                    yield 0
            if prefetch_next:
                wcache[n + 1] = load_block_weights(n + 1)
            elif sub == self.NSUB - 1 and n + 1 == NBK:
                self._op_pre = [self.load_w(d["lru_out_w"][l][j]) for j in range(4)]
            pre, preb = Fb["pre"]
            self.copy(pre[:, 0:3], K["lru_halo"][:, l, n, :], [KB["lru_halo"]], [preb], eng="pool")
            self.act(pre[:, 3:515], psx[:, :], AF.Copy, [psxb], [preb])
            self.copy(K["lru_halo"][:, l, n, :], pre[:, 512:515], [preb], [KB["lru_halo"]], eng="pool")
            yield 0
            xc, xcb = Hh["xc"]
            self.ts(xc[:, W], pre[:, 3:515], cw[:, l, n, 3:4], K["lru_conv_b"][:, l, n:n + 1], ALU.mult, ALU.add,
                    [preb, KB["lru_conv_w"], KB["lru_conv_b"]], [xcb])
            for jx in (2, 1, 0):
                self.stt(xc[:, W], pre[:, jx:jx + 512], cw[:, l, n, jx:jx + 1], xc[:, W], ALU.mult, ALU.add,
                         [preb, KB["lru_conv_w"], xcb], [xcb])
                if jx == 1:
                    yield 0
            yield 0
            xcf, xcfb = Hh["xcf"]
            self.copy(xcf[:, W], xc[:, W], [xcb], [xcfb], eng="pool")
            yield "HANDOFF"
            psr, psrb = self.ps_next("LB")
            self.mm(psr[:, :], gw[0][0][:, 0, :], xcf[:, W], True, True, [gw[0][1], xcfb], [psrb])
            psi, psib = self.ps_next("LB")
            self.mm(psi[:, :], gw[1][0][:, 0, :], xcf[:, W], True, True, [gw[1][1], xcfb], [psib])
            yield 0
            sz, szb = Hh["sz"]
            self.act(sz[:, W], psz[:, :], AF.Silu, [pszb], [szb])
            r, rb = Fb["r"]
            self.act(r[:, W], psr[:, :], AF.Tanh, [psrb, KB["gbh"]], [rb], bias=K["gbh"][:, l, 0, n:n + 1], scale=0.5)
            it, itb = Fb["it"]
            self.act(it[:, W], psi[:, :], AF.Tanh, [psib, KB["gbh"]], [itb], bias=K["gbh"][:, l, 1, n:n + 1], scale=0.5)
            yield 0
            a, ab = Fb["a"]
            self.act(a[:, W], r[:, W], AF.Exp, [rb, KB["clamh"]], [ab], scale=K["clamh"][:, l, n:n + 1],
                     bias=K["clamh"][:, l, n:n + 1])
            m, mb = Fb["m"]
            self.act(m[:, W], r[:, W], AF.Exp, [rb, KB["clam"]], [mb], scale=K["clam"][:, l, n:n + 1],
                     bias=K["clam"][:, l, n:n + 1])
            self.stt(it[:, W], it[:, W], 1.0, xc[:, W], ALU.add, ALU.mult, [itb, xcb], [itb])
            yield 0
            self.act(m[:, W], m[:, W], AF.Ln, [mb, KB["epsk"]], [mb], bias=self.one_ap, scale=-1.0)
            self.act(m[:, W], m[:, W], AF.Exp, [mb, KB["epsk"]], [mb], bias=self.lnhalf_ap, scale=0.5)
            yield 0
            self.tt(m[:, W], m[:, W], it[:, W], ALU.mult, [mb, itb], [mb])
            hs, hsb = Fb["hs"]
            self.sch.op("dve", lambda E, o=hs[:, W], a_=a[:, W], b_=m[:, W], ini=K["lru_state"][:, l, n:n + 1]:
                        E.tensor_tensor_scan(out=o, data0=a_, data1=b_, initial=ini, op0=ALU.mult, op1=ALU.add),
                        [ab, mb, KB["lru_state"]], [hsb])
            yield 0
            self.copy(K["lru_state"][:, l, n:n + 1], hs[:, 511:512], [hsb], [KB["lru_state"]], eng="pool")
            self._xch_flush()
            self.tt(self.yTl[:, n, sl], hs[:, W], sz[:, W], ALU.mult, [hsb, szb], [self.ylb[n][sub]])
            if sub == self.NSUB - 1:
                self.exchange_part(n)

        self.run_pipelined([lru_iter(n, sub, (n * self.NSUB + sub) % 2)
                            for n in range(NBK) for sub in range(self.NSUB)], "lru")
        self.outproj_phase(li, d["lru_out_w"][l])

    def run_pipelined(self, gens, key):
        nf, nb = self._pipe_counts.get(key, (12, 30))
        prev = None

        def step(g):
            try:
                return next(g)
            except StopIteration:
                return None

        for g in gens:
            cf = 0
            doneb = 0
            while True:
                r = step(g)
                cf += 1
                if prev is not None:
                    want = min(nb, (cf * nb + nf - 1) // nf)
                    while doneb < want:
                        if step(prev) is None:
                            prev = None
                            break
                        doneb += 1
                if r == "HANDOFF" or r is None:
                    break
            while prev is not None:
                if step(prev) is None:
                    prev = None
                else:
                    doneb += 1
            if doneb:
                nb = doneb + 1
            nf = cf
            self._pipe_counts[key] = (nf, nb)
            prev = g
        while prev is not None:
            if step(prev) is None:
                prev = None

    def run_pipelined3(self, gens, key):
        nf, nb = self._pipe_counts.get(key, (36, 40))
        backs = []
        DONE = 10 ** 9

        def step(g):
            try:
                return next(g)
            except StopIteration:
                return None

        def advance(entry, want):
            while entry[1] < want:
                if step(entry[0]) is None:
                    entry[1] = DONE
                    return
                entry[1] += 1

        for g in gens:
            cf = 0
            base = [e[1] for e in backs]
            while True:
                r = step(g)
                cf += 1
                for e, b0 in zip(backs, base):
                    if e[1] < DONE:
                        advance(e, b0 + (cf * nb + 2 * nf - 1) // (2 * nf))
                if r == "HANDOFF" or r is None:
                    break
            nf = cf
            backs = [e for e in backs if e[1] < DONE]
            while len(backs) >= 2:
                e = backs.pop(0)
                while step(e[0]) is not None:
                    e[1] += 1
                nb = max(nb, e[1] + 1)
            backs.append([g, 0])
            self._pipe_counts[key] = (nf, nb)
        live = [e[0] for e in backs]
        while live:
            for g_ in list(live):
                if step(g_) is None:
                    live.remove(g_)

    def mm4(self, lhs, lhsb, rhs, rhsb, ring=None, f32r=False):
        ps, psb = self.ps_next(ring)
        for b in range(4):
            cs = slice(b * 128, (b + 1) * 128)
            if f32r:
                self.mm(ps[:, cs], lhs[:, cs].bitcast(F32R), rhs[:, cs].bitcast(F32R), True, True,
                        [lhsb, rhsb], [psb])
            else:
                self.mm(ps[:, cs], lhs[:, cs], rhs[:, cs], True, True, [lhsb, rhsb], [psb])
        return ps, psb

    def tr4(self, src, srcb, ring=None):
        ps, psb = self.ps_next(ring)
        ident = self.K["masks"][:, 0, :]
        for b in range(4):
            cs = slice(b * 128, (b + 1) * 128)
            self.tr(ps[:, cs], src[:, cs], ident, [srcb, self.KB["masks"]], [psb])
        return ps, psb

    def gdn_layer(self, li, ti):
        K, KB, d = self.K, self.KB, self.d
        l = li // 2
        T = self.T
        NB = T // 128
        M, MB = K["masks"], KB["masks"]
        GA, GAB = K["GA"], KB["GA"]
        W = slice(0, 512)

        def mask4(k):
            return M[:, k, :].unsqueeze(1).to_broadcast([128, 4, 128])

        def v4(ap):
            return ap.rearrange("p (b j) -> p b j", b=4)

        self.mod_phase(li)
        i = self._wsi
        self._wsi = (i + 1) % len(self.wst)
        st, stb, ctr = self.wst[i], self.wstbuf[i], self.wstctr[i]
        NH = NBK
        self.dma(st[:, :, 0:2 * NH], d["gdn_ab_w"][l], ctr, writes=[stb])
        wab, wabb = K["wab"], KB["wab"]
        self.copy(wab[:, :, :], st[:, :, 0:2 * NH], [stb], [wabb], eng="pool")
        ps, psb = self.ps_next()
        psv = ps[:, 0:NB * 2 * NH].rearrange("p (b c) -> p b c", b=NB)
        for blk in range(NB):
            sub = blk // 4
            bs = slice(blk * 128, (blk + 1) * 128)
            for kc in range(KC):
                self.mm(psv[:, blk, :], self.hT[:, kc, bs], wab[:, kc, :], kc == 0, kc == KC - 1,
                        [self.hb[kc][sub], wabb], [psb])
        t, tb = self.scratch()
        NQ = NB * NH

        def tv(i):
            return t[:, i * NQ:(i + 1) * NQ].rearrange("p (b h) -> p b h", b=NB)

        def bc(ap):
            return ap.unsqueeze(1).to_broadcast([128, NB, NH])

        self.tt(tv(0), psv[:, :, 0:NH], bc(K["gdn_dt_bias"][:, l, :]), ALU.add, [psb, KB["gdn_dt_bias"]], [tb])
        self.act(tv(1), tv(0), AF.Exp, [tb], [tb])
        self.act(tv(2), tv(1), AF.Ln, [tb, KB["epsk"]], [tb], bias=self.one_ap)
        self.tt(tv(3), tv(2), bc(K["nexpalog"][:, l, :]), ALU.mult, [tb, KB["nexpalog"]], [tb])
        self.act(tv(4), psv[:, :, NH:2 * NH], AF.Exp, [psb], [tb], scale=-1.0)
        self.act(tv(5), tv(4), AF.Ln, [tb, KB["epsk"]], [tb], bias=self.one_ap)
        self.act(GA[:, 4, :, :], tv(5), AF.Exp, [tb], [GAB], scale=-1.0)
        ps2, ps2b = self.ps_next()
        p2v = ps2[:, 0:2 * NQ].rearrange("p (q b h) -> p q b h", q=2, b=NB)
        for blk in range(NB):
            self.mm(p2v[:, 0, blk, :], M[:, 9, :], tv(3)[:, blk, :], True, True, [MB, tb], [ps2b])
            self.mm(p2v[:, 1, blk, :], M[:, 1, :], tv(3)[:, blk, :], True, True, [MB, tb], [ps2b])
        self.copy(GA[:, 0, :, :], p2v[:, 0], [ps2b], [GAB], eng="act")
        self.act(GA[:, 1, :, :], p2v[:, 0], AF.Exp, [ps2b], [GAB])
        self.act(GA[:, 3, :, :], p2v[:, 1], AF.Exp, [ps2b], [GAB])
        self.tt(tv(6), p2v[:, 1], GA[:, 0, :, :], ALU.subtract, [ps2b, GAB], [tb])
        self.act(GA[:, 2, :, :], tv(6), AF.Exp, [tb], [GAB])
        self.tt(GA[:, 5, :, :], GA[:, 4, :, :], GA[:, 1, :, :], ALU.mult, [GAB], [GAB], eng="pool")

        k_ = 0
        F = {}
        for nm in ["pre", "xc", "qs", "ks", "vs", "knT"]:
            F[nm] = (self.scr[k_], self.scrbuf[k_])
            k_ += 1
        F["rs"], F["rinv"], F["diagG"], F["t"], F["tU"], F["tL"], F["t1"] = (
            F["pre"], F["xc"], F["qs"], F["ks"], F["vs"], F["pre"], F["xc"])
        H = [{}, {}, {}]
        for hs_ in range(3):
            for nm in ["szT", "AL"]:
                H[hs_][nm] = (self.scr[k_], self.scrbuf[k_])
                k_ += 1
        assert k_ == 12
        for hs_ in range(3):
            H[hs_]["XU"] = (self.scr[k_], self.scrbuf[k_])
            k_ += 1
        BK = [{}, {}]
        for bs_ in range(2):
            for nm in ["XL", "PU", "XUa", "XUb", "XLa", "XLb", "L", "Y", "EL"]:
                BK[bs_][nm] = (self.scr[k_], self.scrbuf[k_])
                k_ += 1
            b_ = BK[bs_]
            b_["u"], b_["o"], b_["o1a"], b_["o1b"], b_["sq"], b_["on"] = (
                b_["XLb"], b_["XLa"], b_["XUa"], b_["XUb"], b_["Y"], b_["EL"])
        assert k_ <= len(self.scr), k_
        k_ = 0
        Bf = {}
        for nm in ["sqb1", "sqb2", "knTb"]:
            Bf[nm] = (self.scrb[k_], self.scrbbuf[k_])
            k_ += 1
        k_ = 4
        for bs_ in range(2):
            for nm in ["Ubf", "wT", "vn0", "vn1"]:
                BK[bs_][nm] = (self.scrb[k_], self.scrbbuf[k_])
                k_ += 1
        for hs_ in range(3):
            for nm in ["qnT", "kbg", "kdec", "vb", "attnT"]:
                H[hs_][nm] = (self.scrb[k_], self.scrbbuf[k_])
                k_ += 1
        assert k_ <= len(self.scrb), k_
        smalls = [(K["gsmall"], KB["gsmall"]), (K["gsmall1"], KB["gsmall1"])]
        sbfs = [[(K["sbf0"], KB["sbf0"]), (K["sbf1"], KB["sbf1"])], [(K["sbf2"], KB["sbf2"]), (K["sbf3"], KB["sbf3"])]]
        cw = K["gdn_conv_w"]
        wcache = {}

        def prefetch_head(hd):
            wcache[hd] = [self.load_w(d["gdn_in_w"][l, g4 * NH + hd]) for g4 in range(4)]
            dcv, dcvb = K["dconv"], KB["dconv"]
            for idx in range(3):
                for jx in range(4):
                    self.ts(dcv[:, idx * 4 + jx, :], M[:, 0, :], cw[:, l, idx * NH + hd, jx:jx + 1], None,
                            ALU.mult, None, [MB, KB["gdn_conv_w"]], [dcvb], eng="dve")

        def gdn_iter(hd, sub, it_):
            Hh = H[it_ % 3]
            Bk = BK[it_ % 2]
            RB = "B%d" % (it_ % 2)
            small, smallb = smalls[it_ % 2]
            sl = slice(sub * 512, (sub + 1) * 512)
            gb0 = sub * 4

            def gB(q):
                return GA[:, q, gb0:gb0 + 4, hd:hd + 1].to_broadcast([128, 4, 128])

            if hd not in wcache:
                prefetch_head(hd)
            ws = wcache[hd]
            dcv, dcvb = K["dconv"], KB["dconv"]
            pre, preb = F["pre"]
            preh = pre[:, 0:260].bitcast(BF16)

            def inproj(g4):
                ps, psb = self.ps_next("F")
                for kc in range(KC):
                    self.mm(ps[:, :], ws[g4][0][:, kc, :], self.hT[:, kc, sl], kc == 0, kc == KC - 1,
                            [ws[g4][1], self.hb[kc][sub]], [psb])
                    if kc % 4 == 3:
                        yield 0
                return ps, psb

            for idx, dst in enumerate(("qs", "ks", "vs")):
                cb = idx * NH + hd
                psX, psXb = yield from inproj(idx)
                self.copy(preh[:, 0:3], K["gdn_halo"][:, l, cb, :], [KB["gdn_halo"]], [preb], eng="pool")
                self.act(preh[:, 3:515], psX[:, :], AF.Copy, [psXb], [preb])
                self.copy(K["gdn_halo"][:, l, cb, :], preh[:, 512:515], [preb], [KB["gdn_halo"]], eng="pool")
                yield 0
                pc, pcb = self.ps_next("F")
                for jx in range(4):
                    self.mm(pc[:, :], dcv[:, idx * 4 + jx, :], preh[:, jx:jx + 512], jx == 0, jx == 3,
                            [dcvb, preb], [pcb])
                yield 0
                dt_, dtb = F[dst]
                self.act(dt_[:, W], pc[:, :], AF.Silu, [pcb], [dtb])
                yield 0
            psZ, psZb = yield from inproj(3)
            if sub == self.NSUB - 1 and hd + 1 < NH:
                prefetch_head(hd + 1)
                yield 0
            elif sub == self.NSUB - 1:
                self._op_pre = [self.load_w(d["gdn_out_w"][l][j]) for j in range(4)]
                yield 0
            szT, szTb = Hh["szT"]
            self.act(szT[:, W], psZ[:, :], AF.Silu, [psZb], [szTb])
            yield 0
            rs, rsb = F["rs"]
            rinv, rinvb = F["rinv"]
            qnT, qnTb = Hh["qnT"]
            knT, knTb_ = F["knT"]
            knb, knbb = Bf["knTb"]
            for which in ("q", "k"):
                src, srcb = F[which + "s"]
                sq, sqb = Bf["sqb1"] if which == "q" else Bf["sqb2"]
                self.act(sq[:, W], src[:, W], AF.Square, [srcb], [sqb])
                psn, psnb = self.ps_next("F")
                self.mm(psn[:, :], K["ones_bf"][:], sq[:, W], True, True, [KB["ones_bf"], sqb], [psnb])
                yield 0
                self.act(rs[:, W], psn[:, :], AF.Ln, [psnb, KB["epsk"]], [rsb], bias=self.eps_ap)
                self.act(rinv[:, W], rs[:, W], AF.Exp, [rsb], [rinvb], scale=-0.5)
                yield 0
                if which == "q":
                    self.stt(qnT[:, W], src[:, W], 128.0 ** -0.5, rinv[:, W], ALU.mult, ALU.mult,
                             [srcb, rinvb], [qnTb])
                else:
                    self.tt(knT[:, W], src[:, W], rinv[:, W], ALU.mult, [srcb, rinvb], [knTb_])
                    self.act(knb[:, W], knT[:, W], AF.Copy, [knTb_], [knbb])
                yield 0
            vs, vsb = F["vs"]
            kTp, kTpb = self.tr4(knT, knTb_, ring="F")
            vTp, vTpb = self.tr4(vs, vsb, ring="F")
            kbg, kbgb = Hh["kbg"]
            kdec, kdecb = Hh["kdec"]
            vb, vbb = Hh["vb"]
            yield 0
            self.tt(v4(kbg[:, W]), v4(kTp[:, :]), gB(5), ALU.mult, [kTpb, GAB], [kbgb])
            self.tt(v4(kdec[:, W]), v4(kTp[:, :]), gB(2), ALU.mult, [kTpb, GAB], [kdecb])
            self.tt(v4(vb[:, W]), v4(vTp[:, :]), gB(4), ALU.mult, [vTpb, GAB], [vbb])
            yield 0
            dG, dGb = F["diagG"]
            self.tt(v4(dG[:, W]), mask4(0), gB(0), ALU.mult, [MB, GAB], [dGb], eng="pool")
            Gp, Gpb = self.ps_next("F")
            for b in range(4):
                cs = slice(b * 128, (b + 1) * 128)
                self.mm(Gp[:, cs], M[:, 1, :], dG[:, cs], True, True, [MB, dGb], [Gpb])
            yield 0
            t, tb = F["t"]
            self.tt(v4(t[:, W]), v4(Gp[:, :]), gB(0), ALU.subtract, [Gpb, GAB], [tb])
            KKp, KKpb = self.mm4(knb, knbb, knb, knbb, ring="F")
            QKp, QKpb = self.mm4(knb, knbb, qnT, qnTb, ring="F")
            yield 0
            tU, tUb = F["tU"]
            tL, tLb = F["tL"]
            self.tt(v4(tL[:, W]), mask4(3), v4(t[:, W]), ALU.subtract, [tb, MB], [tLb], eng="pool")
            yield 0
            self.act(tL[:, W], tL[:, W], AF.Exp, [tLb], [tLb])
            self.tt(v4(tU[:, W]), v4(t[:, W]), mask4(2), ALU.add, [tb, MB], [tUb], eng="pool")
            self.act(tU[:, W], tU[:, W], AF.Exp, [tUb], [tUb])
            t1, t1b = F["t1"]
            self.tt(v4(t1[:, W]), v4(KKp[:, :]), gB(4), ALU.mult, [KKpb, GAB], [t1b])
            yield 0
            AL, ALb = Hh["AL"]
            self.tt(AL[:, W], t1[:, W], tL[:, W], ALU.mult, [t1b, tLb], [ALb], eng="pool")
            attnT, attnTb = Hh["attnT"]
            self.tt(attnT[:, W], QKp[:, :], tU[:, W], ALU.mult, [QKpb, tUb], [attnTb])
            AUp, AUpb = self.tr4(AL, ALb, ring="F")
            yield 0
            XU, XUb_ = Hh["XU"]
            self.tt(v4(XU[:, W].bitcast(F32R)), v4(AUp[:, :]), mask4(4), ALU.mult, [AUpb, MB], [XUb_])
            yield "HANDOFF"
            XL, XLb_ = Bk["XL"]
            self.tt(v4(XL[:, W].bitcast(F32R)), v4(AL[:, W]), mask4(5), ALU.mult, [ALb, MB], [XLb_])
            PU, PUb = Bk["PU"]
            self.tt(v4(PU[:, W].bitcast(F32R)), v4(XU[:, W]), mask4(0), ALU.add, [XUb_, MB], [PUb], eng="pool")
            curU, curUb = XU, XUb_
            curL, curLb = XL, XLb_
            for s_ in range(4):
                if s_ == 0:
                    pa, pab = self.mm4(curL, curLb, curU, curUb, ring=RB, f32r=True)
                elif s_ < 3:
                    pa, pab = self.mm4(curL, curLb, curU, curUb, ring=RB, f32r=True)
                    pb, pbb = self.mm4(curL, curLb, PU, PUb, ring=RB, f32r=True)
                else:
                    pb, pbb = self.mm4(curL, curLb, PU, PUb, ring=RB, f32r=True)
                yield 0
                if s_ > 0:
                    self.tt(PU[:, W].bitcast(F32R), PU[:, W], pb[:, :], ALU.add, [PUb, pbb], [PUb])
                if s_ == 3:
                    break
                nU, nUb = Bk["XUa"] if s_ % 2 == 0 else Bk["XUb"]
                self.copy(nU[:, W].bitcast(F32R), pa[:, :], [pab], [nUb], eng="act")
                pt, ptb = self.tr4(nU, nUb, ring=RB)
                yield 0
                nL, nLb = Bk["XLa"] if s_ % 2 == 0 else Bk["XLb"]
                self.copy(nL[:, W].bitcast(F32R), pt[:, :], [ptb], [nLb], eng="dve")
                curU, curUb, curL, curLb = nU, nUb, nL, nLb
            Lt, Ltb = Bk["L"]
            pt, ptb = self.tr4(PU, PUb, ring=RB)
            yield 0
            self.copy(Lt[:, W].bitcast(F32R), pt[:, :], [ptb], [Ltb], eng="act")
            Ubf, Ubfb = Bk["Ubf"]
            for k in range(3):
                EL, ELb = Bk["EL"]
                self.tt(v4(EL[:, W].bitcast(F32R)), v4(AL[:, W]), mask4(6 + k), ALU.mult, [ALb, MB], [ELb])
                py, pyb = self.mm4(EL, ELb, PU, PUb, ring=RB, f32r=True)
                yield 0
                Y, Yb = Bk["Y"]
                self.copy(Y[:, W].bitcast(F32R), py[:, :], [pyb], [Yb], eng="act")
                pz, pzb = self.mm4(Lt, Ltb, Y, Yb, ring=RB, f32r=True)
                yield 0
                if k < 2:
                    self.tt(PU[:, W].bitcast(F32R), PU[:, W], pz[:, :], ALU.subtract, [PUb, pzb], [PUb])
                    pt, ptb = self.tr4(PU, PUb, ring=RB)
                    yield 0
                    self.copy(Lt[:, W].bitcast(F32R), pt[:, :], [ptb], [Ltb], eng="act")
                else:
                    self.tt(Ubf[:, W], PU[:, W], pz[:, :], ALU.subtract, [PUb, pzb], [Ubfb])
            pu, pub = self.mm4(Ubf, Ubfb, vb, vbb, ring=RB)
            pw, pwb = self.mm4(kbg, kbgb, Ubf, Ubfb, ring=RB)
            yield 0
            u, ub = Bk["u"]
            self.copy(u[:, W].bitcast(F32R), pu[:, :], [pub], [ub], eng="act")
            wT, wTb = Bk["wT"]
            self.copy(wT[:, W], pw[:, :], [pwb], [wTb], eng="dve")
            S32, S32b = K["gdn_S"][:, l, hd, :], KB["gdn_S"]
            o, ob = Bk["o"]
            sb2 = sbfs[it_ % 2]
            vn2 = [Bk["vn0"], Bk["vn1"]]
            o12 = [Bk["o1a"], Bk["o1b"]]
            self.copy(sb2[0][0][:, :], S32, [S32b], [sb2[0][1]], eng="act")
            for b in range(4):
                cs = slice(b * 128, (b + 1) * 128)
                gb = gb0 + b
                Sbf, Sbfb = sb2[b % 2][0][:, :], sb2[b % 2][1]
                Sbn, Sbnb = sb2[(b + 1) % 2][0][:, :], sb2[(b + 1) % 2][1]
                vnew, vnewb = vn2[b % 2]
                o1s, o1sb = o12[b % 2]
                p1, p1b = self.ps_next(RB)
                self.mm(p1[:, 0:128], wT[:, cs], Sbf, True, True, [wTb, Sbfb], [p1b])
                p1o, p1ob = self.ps_next(RB)
                self.mm(p1o[:, 0:128], qnT[:, cs], Sbf, True, True, [qnTb, Sbfb], [p1ob])
                yield 0
                self.tt(vnew[:, 0:128], u[:, cs], p1[:, 0:128], ALU.subtract, [ub, p1b], [vnewb])
                self.act(o1s[:, 0:128].bitcast(F32R), p1o[:, 0:128], AF.Copy, [p1ob, GAB], [o1sb],
                         scale=GA[:, 1, gb, hd:hd + 1])
                p2, p2b = self.ps_next(RB)
                self.mm(p2[:, 0:128], kdec[:, cs], vnew[:, 0:128], True, True, [kdecb, vnewb], [p2b])
                p2o, p2ob = self.ps_next(RB)
                self.mm(p2o[:, 0:128], attnT[:, cs], vnew[:, 0:128], True, True, [attnTb, vnewb], [p2ob])
                yield 0
                self.stt(S32, S32, GA[:, 3, gb, hd:hd + 1], p2[:, 0:128], ALU.mult, ALU.add,
                         [S32b, GAB, p2b], [S32b])
                self.tt(o[:, cs].bitcast(F32R), p2o[:, 0:128], o1s[:, 0:128], ALU.add, [p2ob, o1sb], [ob])
                if b < 3:
                    self.copy(Sbn, S32, [S32b], [Sbnb], eng="act")
            self._xch_flush()
            sq, sqb_ = Bk["sq"]
            self.tt(sq[:, W].bitcast(F32R), o[:, W], o[:, W], ALU.mult, [ob], [sqb_], eng="pool")
            self.sch.op("dve", lambda E, o_=small[:, 0:4], i_=v4(sq[:, W]):
                        E.tensor_reduce(out=o_, in_=i_, axis=mybir.AxisListType.X, op=ALU.add), [sqb_], [smallb])
            self.act(small[:, 4:8], small[:, 0:4], AF.Ln, [smallb, KB["epsk"]], [smallb], bias=self.eps_ap,
                     scale=1.0 / 128)
            self.act(small[:, 8:12], small[:, 4:8], AF.Exp, [smallb], [smallb], scale=-0.5)
            yield 0
            on, onb = Bk["on"]
            self.tt(v4(on[:, W].bitcast(F32R)), v4(o[:, W]), small[:, 8:12].unsqueeze(2).to_broadcast([128, 4, 128]), ALU.mult,
                    [ob, smallb], [onb])
            po, pob = self.tr4(on, onb, ring=RB)
            yield 0
            szT, szTb = Hh["szT"]
            self.stt(self.yTl[:, hd, sl], po[:, :], K["gdn_onorm_g"][:, l:l + 1], szT[:, W], ALU.mult, ALU.mult,
                     [pob, KB["gdn_onorm_g"], szTb], [self.ylb[hd][sub]])
            if sub == self.NSUB - 1:
                self.exchange_part(hd)

        prefetch_head(0)
        self.run_pipelined3([gdn_iter(hd, sub, hd * self.NSUB + sub)
                             for hd in range(NH) for sub in range(self.NSUB)], "gdn")
        self.outproj_phase(li, d["gdn_out_w"][l])


def _prep_inputs(inp, b, r=0):
    f = np.float32
    c = np.ascontiguousarray
    bl = slice(r * NBK, (r + 1) * NBK)

    def pm(v):
        v = np.asarray(v, f)
        sh = v.shape[:-1]
        v = v.reshape(sh + (KC, 128))
        return c(np.moveaxis(v, -1, 0))

    def tile_w(w):
        w = np.asarray(w, f)
        lead = w.shape[:-2]
        nblk = w.shape[-1] // 128
        w = w.reshape(lead + (KC, 128, nblk, 128))
        nd = len(lead)
        return c(w.transpose(tuple(range(nd)) + (nd + 2, nd + 1, nd, nd + 3)))

    m = {}
    m["xT"] = c(np.asarray(inp["x"][b], f).T)
    m["cT"] = c(np.asarray(inp["c"][b], f).reshape(KC, 128).T)
    m["ada_w"] = tile_w(np.asarray(inp["ada_w"], f)[:, :, r * 1536:(r + 1) * 1536])
    m["ada_b"] = c(np.moveaxis(np.asarray(inp["ada_b"], f).reshape(4, 24, 128), -1, 0)[:, :, r * 12:(r + 1) * 12])
    m["norm_g"] = pm(inp["norm_g"])
    m["final_g"] = pm(inp["final_g"])
    w = np.asarray(inp["lru_in_w"], f).reshape(2, D, 2, KC, 128)
    m["lru_in_w"] = tile_w(w[:, :, :, bl, :].reshape(2, D, 2 * NBK * 128))
    cw = np.asarray(inp["lru_conv_w"], f).reshape(2, 4, KC, 128)
    m["lru_conv_w"] = c(cw.transpose(3, 0, 2, 1)[:, :, bl, :])
    m["lru_conv_b"] = c(pm(inp["lru_conv_b"])[:, :, bl])
    m["lru_gate_w"] = c(np.asarray(inp["lru_gate_w"], f)[:, :, bl])
    m["lru_gate_b"] = c(pm(inp["lru_gate_b"])[:, :, :, bl])
    m["lru_lambda"] = c(pm(inp["lru_lambda"])[:, :, bl])
    m["lru_out_w"] = tile_w(inp["lru_out_w"])
    gw_ = np.asarray(inp["gdn_in_w"], f)
    qkvz = gw_[:, :, :4096].reshape(2, D, 4, KC, 128)[:, :, :, bl, :].reshape(2, D, 4 * NBK * 128)
    a_ = gw_[:, :, 4096:4104][:, :, bl]
    b_ = gw_[:, :, 4104:4112][:, :, bl]
    m["gdn_in_w"] = tile_w(qkvz)
    ab = np.concatenate([a_, b_], axis=2)
    m["gdn_ab_w"] = c(ab.reshape(2, KC, 128, 2 * NBK).transpose(0, 2, 1, 3))
    gc = np.asarray(inp["gdn_conv_w"], f).reshape(2, 4, 3, KC, 128)
    gc = gc.transpose(4, 0, 2, 3, 1)[:, :, :, bl, :]
    m["gdn_conv_w"] = c(gc.reshape(128, 2, 3 * NBK, 4))
    m["gdn_a_log"] = c(np.broadcast_to(np.asarray(inp["gdn_a_log"], f)[None][:, :, bl], (128, 2, NBK)))
    m["gdn_dt_bias"] = c(np.broadcast_to(np.asarray(inp["gdn_dt_bias"], f)[None][:, :, bl], (128, 2, NBK)))
    m["gdn_onorm_g"] = c(np.asarray(inp["gdn_onorm_g"], f).T)
    m["gdn_out_w"] = tile_w(inp["gdn_out_w"])
    m["masks"] = _masks()
    return m


def _masks():
    i = np.arange(128)[:, None]
    j = np.arange(128)[None, :]
    M = np.zeros((128, NMASK, 128), np.float32)
    M[:, 0] = (i == j)
    M[:, 1] = 1.0
    M[:, 2] = np.where(j >= i, 0.0, -30000.0)
    M[:, 3] = np.where(i > j, 0.0, -30000.0)
    same16 = (i // 16) == (j // 16)
    M[:, 4] = -1.0 * ((i < j) & same16)
    M[:, 5] = -1.0 * ((i > j) & same16)
    for k, b in enumerate((16, 32, 64)):
        same_b = (i // b) == (j // b)
        same_2b = (i // (2 * b)) == (j // (2 * b))
        M[:, 6 + k] = (i > j) & same_2b & ~same_b
    M[:, 9] = (i <= j)
    return M


_PROG = {}


def kernel(**inputs):
    key = "full"
    if key not in _PROG:
        _PROG[key] = Prog()
    prog = _PROG[key]
    in_maps = [_prep_inputs(inputs, core // 2, core % 2) for core in range(NCORES)]
    res = run_bass_kernel_spmd(prog.nc, in_maps, core_ids=list(range(NCORES)))
    out = np.stack([np.ascontiguousarray(res.results[2 * b]["yT"].T) for b in range(BATCH)], axis=0)
    return out.astype(np.float32)
```

```python
import numpy as np
from contextlib import ExitStack
import concourse.bass as bass
import concourse.mybir as mybir
from concourse.bass_utils import run_bass_kernel_spmd

F32 = mybir.dt.float32
BF16 = mybir.dt.bfloat16
F32R = mybir.dt.float32r
AF = mybir.ActivationFunctionType
ALU = mybir.AluOpType

D = 1024
KC = 8
SEQ = 4096
BATCH = 4
NCORES = 8
EPS = 1e-6
SEM_PER = 30000
NMASK = 10
NBK = 4


class Ctr:
    def __init__(self, nc, es, name, step):
        self.nc, self.es, self.name, self.step = nc, es, name, step
        self.sems = []
        self.n = 0

    def _sem(self, idx):
        while idx >= len(self.sems):
            self.sems.append(self.es.enter_context(self.nc.semaphore(f"{self.name}_{len(self.sems)}")))
        return self.sems[idx]

    def ref(self, n):
        return self._sem((n - 1) // SEM_PER), ((n - 1) % SEM_PER + 1) * self.step

    def next(self):
        self.n += 1
        return self.n


class Buf:
    __slots__ = ("w", "r", "name", "excl")

    def __init__(self, name="", excl=False):
        self.w = None
        self.r = {}
        self.name = name
        self.excl = excl


class Sched:
    ENG = ("sync", "act", "dve", "pool", "pe")

    def __init__(self, nc, es):
        self.nc, self.es = nc, es
        self.streams = {e: [] for e in self.ENG}
        self.ctr = {e: Ctr(nc, es, "c_" + e, 1) for e in self.ENG if e != "sync"}
        self.waited = {e: {} for e in self.ENG}
        self.nops = 0

    def op(self, eng, fn, reads=(), writes=(), dma=None):
        deps = {}

        def add(tok):
            if tok is None:
                return
            c, n = tok
            if deps.get(c, 0) < n:
                deps[c] = n

        ex = [b for b in reads if b.excl]
        if ex:
            reads = [b for b in reads if not b.excl]
            writes = list(writes) + ex
        for b in reads:
            add(b.w)
        for b in writes:
            add(b.w)
            for c, n in b.r.items():
                add((c, n))
        waits = []
        wd = self.waited[eng]
        for c, n in deps.items():
            if eng == "pe" and c is self.ctr["pe"]:
                continue
            if wd.get(c, 0) < n:
                wd[c] = n
                waits.append(c.ref(n))
        if dma is not None:
            c = dma
        else:
            c = self.ctr[eng]
        n = c.next()
        sem, val = c.ref(n)
        inc = c.step
        self.streams[eng].append((waits, fn, sem, inc))
        for b in reads:
            if b.r.get(c, 0) < n:
                b.r[c] = n
        for b in writes:
            b.w = (c, n)
            b.r = {}
        self.nops += 1
        return (c, n)

    def emit(self, final_waits):
        nc = self.nc
        streams = self.streams

        def run(E, name):
            for waits, fn, sem, inc in streams[name]:
                for s, v in waits:
                    E.wait_ge(s, v)
                fn(E).then_inc(sem, inc)

        with nc.Block() as block:
            @block.sync
            def _(E):
                run(E, "sync")
                for c, n in final_waits:
                    s, v = c.ref(n)
                    E.wait_ge(s, v)

            @block.scalar
            def _(E):
                run(E, "act")

            @block.vector
            def _(E):
                run(E, "dve")

            @block.gpsimd
            def _(E):
                run(E, "pool")

            @block.tensor
            def _(E):
                run(E, "pe")


class Prog:
    def __init__(self, S=SEQ, T=1024, layers=(0, 1, 2, 3), final=True, debug=False):
        self.S, self.T, self.layers, self.final = S, T, layers, final
        self.debug = debug
        self.dbg_outs = {}
        self._pipe_counts = {}
        self.NT = S // T
        self.NSUB = T // 512
        self.nc = bass.Bass("TRN2", target_bir_lowering=False)
        self.es = ExitStack()
        self.sch = Sched(self.nc, self.es)
        self._n = 0
        self.build()

    def sb(self, shape, dt=F32, name=None):
        self._n += 1
        return self.es.enter_context(self.nc.sbuf_tensor(f"{name or 't'}{self._n}", list(shape), dt))

    def dram_in(self, name, shape):
        return self.nc.dram_tensor(name, list(shape), F32, kind="ExternalInput").ap()

    def dmactr(self, name):
        self._n += 1
        return Ctr(self.nc, self.es, f"d_{name}{self._n}", 16)

    def dma(self, out, in_, ctr, reads=(), writes=()):
        return self.sch.op("sync", lambda E: E.dma_start(out=out, in_=in_), reads, writes, dma=ctr)

    def act(self, out, in_, func, reads, writes, bias=None, scale=None):
        kw = {}
        if bias is not None:
            kw["bias"] = bias
        if scale is not None:
            kw["scale"] = scale
        return self.sch.op("act", lambda E: E.activation(out=out, in_=in_, func=func, **kw), reads, writes)

    def tt(self, out, in0, in1, op, reads, writes, eng="dve"):
        return self.sch.op(eng, lambda E: E.tensor_tensor(out=out, in0=in0, in1=in1, op=op), reads, writes)

    def ts(self, out, in0, s1, s2, op0, op1, reads, writes, eng="dve"):
        if op1 is None:
            return self.sch.op(eng, lambda E: E.tensor_scalar(out=out, in0=in0, scalar1=s1, scalar2=None, op0=op0),
                               reads, writes)
        return self.sch.op(eng, lambda E: E.tensor_scalar(out=out, in0=in0, scalar1=s1, scalar2=s2, op0=op0, op1=op1),
                           reads, writes)

    def stt(self, out, in0, scalar, in1, op0, op1, reads, writes):
        return self.sch.op("dve", lambda E: E.scalar_tensor_tensor(out=out, in0=in0, scalar=scalar, in1=in1,
                                                                    op0=op0, op1=op1), reads, writes)

    def copy(self, out, in_, reads, writes, eng="pool"):
        if eng == "act":
            return self.act(out, in_, AF.Copy, reads, writes)
        return self.sch.op(eng, lambda E: E.tensor_copy(out=out, in_=in_), reads, writes)

    def mm(self, out, lhsT, rhs, start, stop, reads, writes):
        return self.sch.op("pe", lambda E: E.matmul(out, lhsT, rhs, start=start, stop=stop), reads, writes)

    def tr(self, out, in_, ident, reads, writes):
        return self.sch.op("pe", lambda E: E.transpose(out, in_, ident), reads, writes)

    def memset(self, ap, val, writes, eng="pool"):
        return self.sch.op(eng, lambda E: E.memset(ap, val), (), writes)

    def dbg(self, name, ap, buf, dt=F32):
        if not self.debug:
            return
        t = self.nc.dram_tensor("dbg_" + name, list(ap.shape), dt, kind="ExternalOutput").ap()
        self.dbg_outs[name] = t
        self.dma(t, ap, self.octr, reads=[buf])

    def ps_next(self, ring=None):
        if ring is None:
            i = self._psi
            self._psi = (i + 1) % len(self.psb)
        else:
            base, n = {"F": (0, 2), "B": (2, 3), "B0": (2, 3), "B1": (5, 3),
                       "LF": (0, 2), "LZ": (2, 2), "LB": (4, 4)}[ring]
            key = "B0" if ring == "B" else ring
            j = self._psr.get(key, 0)
            self._psr[key] = (j + 1) % n
            i = base + j
        return self.psb[i], self.psbuf[i]

    def load_w(self, dram_block):
        i = self._wsi
        self._wsi = (i + 1) % len(self.wst)
        st, stb, ctr = self.wst[i], self.wstbuf[i], self.wstctr[i]
        self.dma(st[:, :, :], dram_block, ctr, writes=[stb])
        j = self._wbi
        self._wbi = (j + 1) % len(self.wbf)
        wb, wbb = self.wbf[j], self.wbfbuf[j]
        self.copy(wb[:, :, :], st[:, :, :], [stb], [wbb], eng="dve")
        return wb, wbb

    def load_small(self, dst_ap, src_ap, buf):
        c = self.dmactr("k")
        self.dma(dst_ap, src_ap, c, writes=[buf])

    def build(self):
        nc, S, T = self.nc, self.S, self.T
        NL = 4
        d = {}
        d["xT"] = self.dram_in("xT", [D, S])
        d["cT"] = self.dram_in("cT", [128, KC])
        d["ada_w"] = self.dram_in("ada_w", [NL, 12, 128, KC, 128])
        d["ada_b"] = self.dram_in("ada_b", [128, NL, 12])
        d["norm_g"] = self.dram_in("norm_g", [128, NL, KC])
        d["final_g"] = self.dram_in("final_g", [128, KC])
        d["lru_in_w"] = self.dram_in("lru_in_w", [2, 2 * NBK, 128, KC, 128])
        d["lru_conv_w"] = self.dram_in("lru_conv_w", [128, 2, NBK, 4])
        d["lru_conv_b"] = self.dram_in("lru_conv_b", [128, 2, NBK])
        d["lru_gate_w"] = self.dram_in("lru_gate_w", [2, 2, NBK, 128, 128])
        d["lru_gate_b"] = self.dram_in("lru_gate_b", [128, 2, 2, NBK])
        d["lru_lambda"] = self.dram_in("lru_lambda", [128, 2, NBK])
        d["lru_out_w"] = self.dram_in("lru_out_w", [2, KC, 128, KC, 128])
        d["gdn_in_w"] = self.dram_in("gdn_in_w", [2, 4 * NBK, 128, KC, 128])
        d["gdn_ab_w"] = self.dram_in("gdn_ab_w", [2, 128, KC, 2 * NBK])
        d["gdn_conv_w"] = self.dram_in("gdn_conv_w", [128, 2, 3 * NBK, 4])
        d["gdn_a_log"] = self.dram_in("gdn_a_log", [128, 2, NBK])
        d["gdn_dt_bias"] = self.dram_in("gdn_dt_bias", [128, 2, NBK])
        d["gdn_onorm_g"] = self.dram_in("gdn_onorm_g", [128, 2])
        d["gdn_out_w"] = self.dram_in("gdn_out_w", [2, KC, 128, KC, 128])
        d["masks"] = self.dram_in("masks", [128, NMASK, 128])
        self.d = d
        self.yT = nc.dram_tensor("yT", [D, S], F32, kind="ExternalOutput").ap()

        self.psb = [self.es.enter_context(nc.psum_tensor(f"ps{i}", [128, 512], F32)) for i in range(8)]
        self.psbuf = [Buf(f"ps{i}", excl=True) for i in range(8)]
        self._psi = 0
        self._psr = {}
        self.wst = [self.sb([128, KC, 128], F32, "wst") for _ in range(3)]
        self.wstbuf = [Buf() for _ in self.wst]
        self.wstctr = [self.dmactr("wst") for _ in self.wst]
        self._wsi = 0
        self.wbf = [self.sb([128, KC, 128], BF16, "wbf") for _ in range(6)]
        self.wbfbuf = [Buf() for _ in self.wbf]
        self._wbi = 0

        self.K = {}
        self.KB = {}

        def const(name, shape, src=None, dt=F32):
            t = self.sb(shape, dt, name)
            b = Buf(name)
            self.K[name], self.KB[name] = t, b
            if src is not None:
                self.load_small(t[:], src, b)
            return t

        const("masks", [128, NMASK, 128], d["masks"][:, :, :])
        const("cT", [128, KC], d["cT"][:, :])
        const("ada_b", [128, NL, 12], d["ada_b"][:, :, :])
        const("condh", [128, NL, 12])
        const("norm_g", [128, NL, KC], d["norm_g"][:, :, :])
        const("final_g", [128, KC], d["final_g"][:, :])
        const("lru_conv_w", [128, 2, NBK, 4], d["lru_conv_w"][:, :, :, :])
        const("lru_conv_b", [128, 2, NBK], d["lru_conv_b"][:, :, :])
        const("lru_gate_b", [128, 2, 2, NBK], d["lru_gate_b"][:, :, :, :])
        const("lru_lambda", [128, 2, NBK], d["lru_lambda"][:, :, :])
        const("gdn_conv_w", [128, 2, 3 * NBK, 4], d["gdn_conv_w"][:, :, :, :])
        const("gdn_a_log", [128, 2, NBK], d["gdn_a_log"][:, :, :])
        const("gdn_dt_bias", [128, 2, NBK], d["gdn_dt_bias"][:, :, :])
        const("gdn_onorm_g", [128, 2], d["gdn_onorm_g"][:, :])
        ones_bf = const("ones_bf", [128, 128], None, BF16)
        self.memset(ones_bf[:], 1.0, [self.KB["ones_bf"]])
        epsk = const("epsk", [128, 3])
        self.memset(epsk[:, 0:1], EPS, [self.KB["epsk"]])
        self.memset(epsk[:, 1:2], 1.0, [self.KB["epsk"]])
        self.memset(epsk[:, 2:3], float(np.log(0.5)), [self.KB["epsk"]])
        self.lnhalf_ap = epsk[:, 2:3]
        const("gbh", [128, 2, 2, NBK])
        const("clamh", [128, 2, NBK])
        self.eps_ap = epsk[:, 0:1]
        self.one_ap = epsk[:, 1:2]
        const("cond", [128, NL, 24])
        const("gs", [128, NL, KC])
        const("cact", [128, KC])
        const("clam", [128, 2, NBK])
        const("clam2", [128, 2, NBK])
        const("lru_state", [128, 2, NBK])
        const("lru_halo", [128, 2, NBK, 3])
        const("nexpalog", [128, 2, NBK])
        const("GA", [128, 6, T // 128, NBK])
        const("gdn_S", [128, 2, NBK, 128])
        const("sbf0", [128, 128], None, BF16)
        const("sbf1", [128, 128], None, BF16)
        const("gdn_halo", [128, 2, 3 * NBK, 3])
        const("dconv", [128, 12, 128], None, BF16)
        const("gsmall", [128, 16])
        const("gsmall1", [128, 16])
        const("sbf2", [128, 128], None, BF16)
        const("sbf3", [128, 128], None, BF16)
        const("wab", [128, KC, 2 * NBK], None, BF16)
        self.memset(self.K["gdn_S"][:], 0.0, [self.KB["gdn_S"]])
        self.memset(self.K["gdn_halo"][:], 0.0, [self.KB["gdn_halo"]])
        self.memset(self.K["lru_state"][:], 0.0, [self.KB["lru_state"]])
        self.memset(self.K["lru_halo"][:], 0.0, [self.KB["lru_halo"]])

        self.xT = self.sb([128, KC, T], F32, "xT")
        self.hT = self.sb([128, KC, T], BF16, "hT")
        self.yTs = self.sb([128, KC, T], BF16, "yTs")
        self.yTl = self.sb([128, NBK, T], BF16, "yTl")
        self.ylb = [[Buf() for _ in range(self.NSUB)] for _ in range(NBK)]
        NX = 2 * NBK
        self.xch_in = [nc.dram_tensor(f"xch_in{i}", [128, T], BF16) for i in range(NX)]
        self.xch_out = [nc.dram_tensor(f"xch_out{i}", [2 * 128, T], BF16) for i in range(NX)]
        self.xch_inb = [Buf() for _ in range(NX)]
        self.xch_outb = [Buf() for _ in range(NX)]
        self.xch_c1 = [self.dmactr("xi") for _ in range(NX)]
        self.xch_c2 = [self.dmactr("xo") for _ in range(NX)]
        self.cc_ctr = Ctr(nc, self.es, "cc", 1)
        self._xchi = 0
        self._op_pre = None
        self._presq = None
        self._xch_pending = []
        self.xb = [[Buf() for _ in range(self.NSUB)] for _ in range(KC)]
        self.hb = [[Buf() for _ in range(self.NSUB)] for _ in range(KC)]
        self.yb = [[Buf() for _ in range(self.NSUB)] for _ in range(KC)]
        self.xctr = [self.dmactr("x") for _ in range(KC)]
        self.octr = self.dmactr("o")

        self.scr = [self.sb([128, 520], F32, "scr") for _ in range(33)]
        self.scrbuf = [Buf() for _ in self.scr]
        self._sci = 0
        self.scrb = [self.sb([128, 520], BF16, "scrb") for _ in range(27)]
        self.scrbbuf = [Buf() for _ in self.scrb]
        self._scbi = 0

        self.prologue()
        last = None
        for ti in range(self.NT):
            self.load_x(ti)
            for li in self.layers:
                if li % 2 == 0:
                    self.lru_layer(li, ti)
                else:
                    self.gdn_layer(li, ti)
            last = self.store_out(ti)
        self.sch.emit([(self.octr, self.octr.n)])

    def scratch(self):
        i = self._sci
        self._sci = (i + 1) % 12
        return self.scr[i], self.scrbuf[i]

    def scratch_bf(self):
        i = self._scbi
        self._scbi = (i + 1) % 4
        return self.scrb[i], self.scrbbuf[i]

    def prologue(self):
        K, KB, d = self.K, self.KB, self.d
        self.act(K["cact"][:], K["cT"][:], AF.Silu, [KB["cT"]], [KB["cact"]])
        ast, astb, astc = self.wst[:2], self.wstbuf[:2], self.wstctr[:2]
        q = 0
        row, rowb = self.scratch()
        for li in range(4):
            banks = [self.ps_next() for _ in range(3)]
            for oc in range(12):
                st, stb, sc = ast[q % 2], astb[q % 2], astc[q % 2]
                q += 1
                src = d["ada_w"][li, oc]
                self.dma(st[:, :, :], src, sc, writes=[stb])
                ps, psb = banks[oc // 4]
                cs = slice((oc % 4) * 128, (oc % 4 + 1) * 128)
                for kc in range(KC):
                    self.mm(ps[0:1, cs], K["cact"][:, kc:kc + 1], st[:, kc, :],
                            kc == 0, kc == KC - 1, [stb, KB["cact"]], [psb])
            psT, psTb = self.ps_next()
            for g in range(3):
                ps, psb = banks[g]
                self.act(row[0:1, 0:512], ps[0:1, :], AF.Copy, [psb], [rowb])
                for o4 in range(4):
                    oc = g * 4 + o4
                    self.mm(psT[:, oc:oc + 1], row[0:1, o4 * 128:(o4 + 1) * 128], self.one_ap[0:1, :],
                            True, True, [rowb, KB["epsk"]], [psTb])
            self.tt(K["condh"][:, li, :], psT[:, 0:12], K["ada_b"][:, li, :], ALU.add,
                    [psTb, KB["ada_b"]], [KB["condh"]])
        cin = self.nc.dram_tensor("cond_in", [128, 48], F32)
        cout = self.nc.dram_tensor("cond_out", [256, 48], F32)
        cinb, coutb = Buf(), Buf()
        c1, c2 = self.dmactr("ci"), self.dmactr("co")
        self.dma(cin.ap(), K["condh"][:].rearrange("p l c -> p (l c)"), c1, reads=[KB["condh"]], writes=[cinb])
        self.sch.op("pool", lambda E: E.collective_compute(
            "AllGather", ALU.bypass, replica_groups=[[0, 1], [2, 3], [4, 5], [6, 7]],
            ins=[cin.ap().opt()], outs=[cout.ap().opt()]), [cinb], [coutb], dma=self.cc_ctr)
        for r_ in range(2):
            self.dma(K["cond"][:, :, r_ * 12:(r_ + 1) * 12],
                     cout.ap()[r_ * 128:(r_ + 1) * 128, :].rearrange("p (l c) -> p l c", l=4), c2,
                     reads=[coutb], writes=[KB["cond"]])
        for li in range(4):
            self.stt(K["gs"][:, li, :], K["cond"][:, li, 8:16], 1.0, K["norm_g"][:, li, :], ALU.add, ALU.mult,
                     [KB["cond"], KB["norm_g"]], [KB["gs"]])
        self.act(K["nexpalog"][:], K["gdn_a_log"][:], AF.Exp, [KB["gdn_a_log"]], [KB["nexpalog"]])
        self.ts(K["nexpalog"][:], K["nexpalog"][:], -1.0, None, ALU.mult, None, [KB["nexpalog"]], [KB["nexpalog"]])
        t, tb = self.scratch()
        self.act(t[:, 0:2 * NBK], K["lru_lambda"][:].rearrange("p a b -> p (a b)"), AF.Exp, [KB["lru_lambda"]], [tb],
                 scale=-1.0)
        self.act(t[:, 16:16 + 2 * NBK], t[:, 0:2 * NBK], AF.Ln, [tb, KB["epsk"]], [tb], bias=self.one_ap)
        self.ts(K["clam"][:].rearrange("p a b -> p (a b)"), t[:, 16:16 + 2 * NBK], -8.0, None, ALU.mult, None, [tb], [KB["clam"]])
        self.ts(K["clam2"][:].rearrange("p a b -> p (a b)"), t[:, 16:16 + 2 * NBK], -16.0, None, ALU.mult, None, [tb],
                [KB["clam2"]])
        self.ts(K["clamh"][:].rearrange("p a b -> p (a b)"), t[:, 16:16 + 2 * NBK], -4.0, None, ALU.mult, None, [tb],
                [KB["clamh"]])
        self.ts(K["gbh"][:].rearrange("p a b c -> p (a b c)"), K["lru_gate_b"][:].rearrange("p a b c -> p (a b c)"),
                0.5, None, ALU.mult, None, [KB["lru_gate_b"]], [KB["gbh"]])

    def load_x(self, ti):
        T = self.T
        for kc in range(KC):
            self.dma(self.xT[:, kc, :], self.d["xT"][kc * 128:(kc + 1) * 128, ti * T:(ti + 1) * T], self.xctr[kc],
                     writes=self.xb[kc])

    def norm_phase(self, gs_ap_fn, shift_ap_fn, out_fn):
        K, KB = self.K, self.KB
        presq = self._presq
        self._presq = None
        for sub in range(self.NSUB):
            sl = slice(sub * 512, (sub + 1) * 512)
            if presq is not None:
                ps, psb = presq[sub]
            else:
                ps, psb = self.ps_next()
                for kc in range(KC):
                    sq, sqb = self.scratch_bf()
                    self.act(sq[:, 0:512], self.xT[:, kc, sl], AF.Square, [self.xb[kc][sub]], [sqb])
                    self.mm(ps[:, :], K["ones_bf"][:], sq[:, 0:512], kc == 0, kc == KC - 1,
                            [KB["ones_bf"], sqb], [psb])
            rs, rsb = self.scratch()
            self.act(rs[:, 0:512], ps[:, :], AF.Ln, [psb, KB["epsk"]], [rsb], bias=self.eps_ap, scale=1.0 / D)
            rstd, rstdb = self.scratch()
            self.act(rstd[:, 0:512], rs[:, 0:512], AF.Exp, [rsb], [rstdb], scale=-0.5)
            for kc in range(KC):
                out_fn(kc, sub, sl, rstd, rstdb)

    def store_out(self, ti):
        K, KB, T = self.K, self.KB, self.T
        if not self.final:
            for kc in range(KC):
                self.dma(self.yT[kc * 128:(kc + 1) * 128, ti * T:(ti + 1) * T], self.xT[:, kc, :], self.octr,
                         reads=self.xb[kc])
            return

        def out_fn(kc, sub, sl, rstd, rstdb):
            o, ob = self.scratch()
            self.stt(o[:, 0:512], self.xT[:, kc, sl], K["final_g"][:, kc:kc + 1], rstd[:, 0:512], ALU.mult, ALU.mult,
                     [self.xb[kc][sub], KB["final_g"], rstdb], [ob])
            self.dma(self.yT[kc * 128:(kc + 1) * 128, ti * T + sub * 512: ti * T + (sub + 1) * 512], o[:, 0:512],
                     self.octr, reads=[ob])

        self.norm_phase(None, None, out_fn)

    def mod_phase(self, li):
        K, KB = self.K, self.KB

        def out_fn(kc, sub, sl, rstd, rstdb):
            t, tb = self.scratch()
            self.stt(t[:, 0:512], self.xT[:, kc, sl], K["gs"][:, li, kc:kc + 1], rstd[:, 0:512], ALU.mult, ALU.mult,
                     [self.xb[kc][sub], KB["gs"], rstdb], [tb])
            self.act(self.hT[:, kc, sl], t[:, 0:512], AF.Identity, [tb, KB["cond"]], [self.hb[kc][sub]],
                     bias=K["cond"][:, li, kc:kc + 1])

        self.norm_phase(None, None, out_fn)

    def exchange_part(self, j):
        i = (self._xchi % 2) * NBK + j
        xin, xout = self.xch_in[i], self.xch_out[i]
        self._xch_flush()
        self.dma(xin.ap(), self.yTl[:, j, :], self.xch_c1[i], reads=self.ylb[j], writes=[self.xch_inb[i]])
        self.sch.op("pool", lambda E: E.collective_compute(
            "AllGather", ALU.bypass, replica_groups=[[0, 1], [2, 3], [4, 5], [6, 7]],
            ins=[xin.ap().opt()], outs=[xout.ap().opt()]),
            [self.xch_inb[i]], [self.xch_outb[i]], dma=self.cc_ctr)
        self._xch_pending.append((i, j))

    def _xch_flush(self):
        for i, j in self._xch_pending:
            xout = self.xch_out[i]
            dst = self.yTs[:, :, :].rearrange("p (r g) t -> p r g t", r=2)[:, :, j, :]
            self.dma(dst, xout.ap().rearrange("(r p) t -> p r t", p=128), self.xch_c2[i],
                     reads=[self.xch_outb[i]], writes=self.yb[j] + self.yb[NBK + j])
        self._xch_pending = []

    def exchange(self):
        self._xch_flush()
        self._xchi += 1

    def outproj_pass1(self, li, w_dram, nbanks):
        K, KB = self.K, self.KB
        late = (NBK - 1, 2 * NBK - 1)
        early = [n for n in range(KC) if n not in late]
        pre = self._op_pre if self._op_pre is not None else [self.load_w(w_dram[j]) for j in range(2)]
        self._op_pre = None
        r_ = 0
        for j in range(KC):
            wb, wbb = pre[j] if j < len(pre) else self.load_w(w_dram[j])
            for sub in range(self.NSUB):
                sl = slice(sub * 512, (sub + 1) * 512)
                ps, psb = self.psb[r_ % nbanks], self.psbuf[r_ % nbanks]
                r_ += 1
                for k, n in enumerate(early):
                    self.mm(ps[:, :], wb[:, n, :], self.yTs[:, n, sl], k == 0, k == len(early) - 1,
                            [wbb, self.yb[n][sub]], [psb])
                yield 0
                self.stt(self.xT[:, j, sl], ps[:, :], K["cond"][:, li, 16 + j:17 + j], self.xT[:, j, sl],
                         ALU.mult, ALU.add, [psb, KB["cond"], self.xb[j][sub]], [self.xb[j][sub]])

    def outproj_phase(self, li, w_dram, pass1_done=False):
        K, KB = self.K, self.KB
        late = (NBK - 1, 2 * NBK - 1)
        if not pass1_done:
            for _ in self.outproj_pass1(li, w_dram, 6):
                pass
        r_ = 0
        wl = [self.load_w(w_dram[:, :, n, :].rearrange("j p c -> p j c")) for n in late]
        self.exchange()
        sqacc = [(self.psb[6 + sub], self.psbuf[6 + sub]) for sub in range(self.NSUB)]
        pend = []

        def flush(keep):
            while len(pend) > keep:
                sq_, sqb_, sub_, j_ = pend.pop(0)
                self.mm(sqacc[sub_][0][:, :], K["ones_bf"][:], sq_[:, 0:512], j_ == 0, j_ == KC - 1,
                        [KB["ones_bf"], sqb_], [sqacc[sub_][1]])

        for j in range(KC):
            for sub in range(self.NSUB):
                sl = slice(sub * 512, (sub + 1) * 512)
                ps, psb = self.psb[r_ % 6], self.psbuf[r_ % 6]
                r_ += 1
                for k, n in enumerate(late):
                    self.mm(ps[:, :], wl[k][0][:, j, :], self.yTs[:, n, sl], k == 0, k == len(late) - 1,
                            [wl[k][1], self.yb[n][sub]], [psb])
                flush(2)
                self.stt(self.xT[:, j, sl], ps[:, :], K["cond"][:, li, 16 + j:17 + j], self.xT[:, j, sl],
                         ALU.mult, ALU.add, [psb, KB["cond"], self.xb[j][sub]], [self.xb[j][sub]])
                sq, sqb = self.scratch_bf()
                self.act(sq[:, 0:512], self.xT[:, j, sl], AF.Square, [self.xb[j][sub]], [sqb])
                pend.append((sq, sqb, sub, j))
        flush(0)
        self._presq = sqacc

    def lru_layer(self, li, ti):
        K, KB, d = self.K, self.KB, self.d
        l = li // 2
        self.mod_phase(li)
        W = slice(0, 512)
        k_ = 0
        Fb = {}
        for nm in ["pre", "r", "it", "a", "m", "hs"]:
            Fb[nm] = (self.scr[k_], self.scrbuf[k_])
            k_ += 1
        H = [{}, {}]
        for par in range(2):
            for nm in ["xc", "sz"]:
                H[par][nm] = (self.scr[k_], self.scrbuf[k_])
                k_ += 1
            H[par]["xcf"] = (self.scrb[2 + par], self.scrbbuf[2 + par])
        cw = K["lru_conv_w"]
        wcache = {}

        def load_block_weights(n):
            wx = self.load_w(d["lru_in_w"][l, n])
            wz = self.load_w(d["lru_in_w"][l, NBK + n])
            gw = []
            for k in range(2):
                i = self._wsi
                self._wsi = (i + 1) % len(self.wst)
                st, stb, ctr = self.wst[i], self.wstbuf[i], self.wstctr[i]
                self.dma(st[:, 0, :], d["lru_gate_w"][l, k, n, :, :], ctr, writes=[stb])
                jj = self._wbi
                self._wbi = (jj + 1) % len(self.wbf)
                wb, wbb = self.wbf[jj], self.wbfbuf[jj]
                self.copy(wb[:, 0, :], st[:, 0, :], [stb], [wbb], eng="pool")
                gw.append((wb, wbb))
            return wx, wz, gw

        def lru_iter(n, sub, par):
            Hh = H[par]
            sl = slice(sub * 512, (sub + 1) * 512)
            if n not in wcache:
                wcache[n] = load_block_weights(n)
            (wx, wxb), (wz, wzb), gw = wcache[n]
            prefetch_next = (sub == self.NSUB - 1 and n + 1 < NBK)
            psx, psxb = self.ps_next("LF")
            for kc in range(KC):
                self.mm(psx[:, :], wx[:, kc, :], self.hT[:, kc, sl], kc == 0, kc == KC - 1,
                        [wxb, self.hb[kc][sub]], [psxb])
                if kc % 4 == 3:
                    yield 0
            psz, pszb = self.ps_next("LZ")
            for kc in range(KC):
                self.mm(psz[:, :], wz[:, kc, :], self.hT[:, kc, sl], kc == 0, kc == KC - 1,
                        [wzb, self.hb[kc][sub]], [pszb])
                if kc % 4 == 3:
                    yield 0
            if prefetch_next:
                wcache[n + 1] = load_block_weights(n + 1)
            elif sub == self.NSUB - 1 and n + 1 == NBK:
                self._op_pre = [self.load_w(d["lru_out_w"][l][j]) for j in range(4)]
            pre, preb = Fb["pre"]
            self.copy(pre[:, 0:3], K["lru_halo"][:, l, n, :], [KB["lru_halo"]], [preb], eng="pool")
            self.act(pre[:, 3:515], psx[:, :], AF.Copy, [psxb], [preb])
            self.copy(K["lru_halo"][:, l, n, :], pre[:, 512:515], [preb], [KB["lru_halo"]], eng="pool")
            yield 0
            xc, xcb = Hh["xc"]
            self.ts(xc[:, W], pre[:, 3:515], cw[:, l, n, 3:4], K["lru_conv_b"][:, l, n:n + 1], ALU.mult, ALU.add,
                    [preb, KB["lru_conv_w"], KB["lru_conv_b"]], [xcb])
            for jx in (2, 1, 0):
                self.stt(xc[:, W], pre[:, jx:jx + 512], cw[:, l, n, jx:jx + 1], xc[:, W], ALU.mult, ALU.add,
                         [preb, KB["lru_conv_w"], xcb], [xcb])
                if jx == 1:
                    yield 0
            yield 0
            xcf, xcfb = Hh["xcf"]
            self.copy(xcf[:, W], xc[:, W], [xcb], [xcfb], eng="pool")
            yield "HANDOFF"
            psr, psrb = self.ps_next("LB")
            self.mm(psr[:, :], gw[0][0][:, 0, :], xcf[:, W], True, True, [gw[0][1], xcfb], [psrb])
            psi, psib = self.ps_next("LB")
            self.mm(psi[:, :], gw[1][0][:, 0, :], xcf[:, W], True, True, [gw[1][1], xcfb], [psib])
            yield 0
            sz, szb = Hh["sz"]
            self.act(sz[:, W], psz[:, :], AF.Silu, [pszb], [szb])
            r, rb = Fb["r"]
            self.act(r[:, W], psr[:, :], AF.Tanh, [psrb, KB["gbh"]], [rb], bias=K["gbh"][:, l, 0, n:n + 1], scale=0.5)
            it, itb = Fb["it"]
            self.act(it[:, W], psi[:, :], AF.Tanh, [psib, KB["gbh"]], [itb], bias=K["gbh"][:, l, 1, n:n + 1], scale=0.5)
            yield 0
            a, ab = Fb["a"]
            self.act(a[:, W], r[:, W], AF.Exp, [rb, KB["clamh"]], [ab], scale=K["clamh"][:, l, n:n + 1],
                     bias=K["clamh"][:, l, n:n + 1])
            m, mb = Fb["m"]
            self.act(m[:, W], r[:, W], AF.Exp, [rb, KB["clam"]], [mb], scale=K["clam"][:, l, n:n + 1],
                     bias=K["clam"][:, l, n:n + 1])
            self.stt(it[:, W], it[:, W], 1.0, xc[:, W], ALU.add, ALU.mult, [itb, xcb], [itb])
            yield 0
            self.act(m[:, W], m[:, W], AF.Ln, [mb, KB["epsk"]], [mb], bias=self.one_ap, scale=-1.0)
            self.act(m[:, W], m[:, W], AF.Exp, [mb, KB["epsk"]], [mb], bias=self.lnhalf_ap, scale=0.5)
            yield 0
            self.tt(m[:, W], m[:, W], it[:, W], ALU.mult, [mb, itb], [mb])
            hs, hsb = Fb["hs"]
            self.sch.op("dve", lambda E, o=hs[:, W], a_=a[:, W], b_=m[:, W], ini=K["lru_state"][:, l, n:n + 1]:
                        E.tensor_tensor_scan(out=o, data0=a_, data1=b_, initial=ini, op0=ALU.mult, op1=ALU.add),
                        [ab, mb, KB["lru_state"]], [hsb])
            yield 0
            self.copy(K["lru_state"][:, l, n:n + 1], hs[:, 511:512], [hsb], [KB["lru_state"]], eng="pool")
            self._xch_flush()
            self.tt(self.yTl[:, n, sl], hs[:, W], sz[:, W], ALU.mult, [hsb, szb], [self.ylb[n][sub]])
            if sub == self.NSUB - 1:
                self.exchange_part(n)

        self.run_pipelined([lru_iter(n, sub, (n * self.NSUB + sub) % 2)
                            for n in range(NBK) for sub in range(self.NSUB)]
                           + [self.outproj_pass1(li, d["lru_out_w"][l], 2)], "lru")
        self.outproj_phase(li, d["lru_out_w"][l], pass1_done=True)

    def run_pipelined(self, gens, key):
        nf, nb = self._pipe_counts.get(key, (12, 30))
        prev = None

        def step(g):
            try:
                return next(g)
            except StopIteration:
                return None

        for g in gens:
            cf = 0
            doneb = 0
            while True:
                r = step(g)
                cf += 1
                if prev is not None:
                    want = min(nb, (cf * nb + nf - 1) // nf)
                    while doneb < want:
                        if step(prev) is None:
                            prev = None
                            break
                        doneb += 1
                if r == "HANDOFF" or r is None:
                    break
            while prev is not None:
                if step(prev) is None:
                    prev = None
                else:
                    doneb += 1
            if doneb:
                nb = doneb + 1
            nf = cf
            self._pipe_counts[key] = (nf, nb)
            prev = g
        while prev is not None:
            if step(prev) is None:
                prev = None

    def run_pipelined3(self, gens, key):
        nf, nb = self._pipe_counts.get(key, (36, 40))
        backs = []
        DONE = 10 ** 9

        def step(g):
            try:
                return next(g)
            except StopIteration:
                return None

        def advance(entry, want):
            while entry[1] < want:
                if step(entry[0]) is None:
                    entry[1] = DONE
                    return
                entry[1] += 1

        for g in gens:
            cf = 0
            base = [e[1] for e in backs]
            while True:
                r = step(g)
                cf += 1
                for e, b0 in zip(backs, base):
                    if e[1] < DONE:
                        advance(e, b0 + (cf * nb + 2 * nf - 1) // (2 * nf))
                if r == "HANDOFF" or r is None:
                    break
            nf = cf
            backs = [e for e in backs if e[1] < DONE]
            while len(backs) >= 2:
                e = backs.pop(0)
                while step(e[0]) is not None:
                    e[1] += 1
                nb = max(nb, e[1] + 1)
            backs.append([g, 0])
            self._pipe_counts[key] = (nf, nb)
        live = [e[0] for e in backs]
        while live:
            for g_ in list(live):
                if step(g_) is None:
                    live.remove(g_)

    def mm4(self, lhs, lhsb, rhs, rhsb, ring=None, f32r=False):
        ps, psb = self.ps_next(ring)
        for b in range(4):
            cs = slice(b * 128, (b + 1) * 128)
            if f32r:
                self.mm(ps[:, cs], lhs[:, cs].bitcast(F32R), rhs[:, cs].bitcast(F32R), True, True,
                        [lhsb, rhsb], [psb])
            else:
                self.mm(ps[:, cs], lhs[:, cs], rhs[:, cs], True, True, [lhsb, rhsb], [psb])
        return ps, psb

    def tr4(self, src, srcb, ring=None):
        ps, psb = self.ps_next(ring)
        ident = self.K["masks"][:, 0, :]
        for b in range(4):
            cs = slice(b * 128, (b + 1) * 128)
            self.tr(ps[:, cs], src[:, cs], ident, [srcb, self.KB["masks"]], [psb])
        return ps, psb

    def gdn_layer(self, li, ti):
        K, KB, d = self.K, self.KB, self.d
        l = li // 2
        T = self.T
        NB = T // 128
        M, MB = K["masks"], KB["masks"]
        GA, GAB = K["GA"], KB["GA"]
        W = slice(0, 512)

        def mask4(k):
            return M[:, k, :].unsqueeze(1).to_broadcast([128, 4, 128])

        def v4(ap):
            return ap.rearrange("p (b j) -> p b j", b=4)

        self.mod_phase(li)
        i = self._wsi
        self._wsi = (i + 1) % len(self.wst)
        st, stb, ctr = self.wst[i], self.wstbuf[i], self.wstctr[i]
        NH = NBK
        self.dma(st[:, :, 0:2 * NH], d["gdn_ab_w"][l], ctr, writes=[stb])
        wab, wabb = K["wab"], KB["wab"]
        self.copy(wab[:, :, :], st[:, :, 0:2 * NH], [stb], [wabb], eng="pool")
        ps, psb = self.ps_next()
        psv = ps[:, 0:NB * 2 * NH].rearrange("p (b c) -> p b c", b=NB)
        for blk in range(NB):
            sub = blk // 4
            bs = slice(blk * 128, (blk + 1) * 128)
            for kc in range(KC):
                self.mm(psv[:, blk, :], self.hT[:, kc, bs], wab[:, kc, :], kc == 0, kc == KC - 1,
                        [self.hb[kc][sub], wabb], [psb])
        t, tb = self.scratch()
        NQ = NB * NH

        def tv(i):
            return t[:, i * NQ:(i + 1) * NQ].rearrange("p (b h) -> p b h", b=NB)

        def bc(ap):
            return ap.unsqueeze(1).to_broadcast([128, NB, NH])

        self.tt(tv(0), psv[:, :, 0:NH], bc(K["gdn_dt_bias"][:, l, :]), ALU.add, [psb, KB["gdn_dt_bias"]], [tb])
        self.act(tv(1), tv(0), AF.Exp, [tb], [tb])
        self.act(tv(2), tv(1), AF.Ln, [tb, KB["epsk"]], [tb], bias=self.one_ap)
        self.tt(tv(3), tv(2), bc(K["nexpalog"][:, l, :]), ALU.mult, [tb, KB["nexpalog"]], [tb])
        self.act(tv(4), psv[:, :, NH:2 * NH], AF.Exp, [psb], [tb], scale=-1.0)
        self.act(tv(5), tv(4), AF.Ln, [tb, KB["epsk"]], [tb], bias=self.one_ap)
        self.act(GA[:, 4, :, :], tv(5), AF.Exp, [tb], [GAB], scale=-1.0)
        ps2, ps2b = self.ps_next()
        p2v = ps2[:, 0:2 * NQ].rearrange("p (q b h) -> p q b h", q=2, b=NB)
        for blk in range(NB):
            self.mm(p2v[:, 0, blk, :], M[:, 9, :], tv(3)[:, blk, :], True, True, [MB, tb], [ps2b])
            self.mm(p2v[:, 1, blk, :], M[:, 1, :], tv(3)[:, blk, :], True, True, [MB, tb], [ps2b])
        self.copy(GA[:, 0, :, :], p2v[:, 0], [ps2b], [GAB], eng="act")
        self.act(GA[:, 1, :, :], p2v[:, 0], AF.Exp, [ps2b], [GAB])
        self.act(GA[:, 3, :, :], p2v[:, 1], AF.Exp, [ps2b], [GAB])
        self.tt(tv(6), p2v[:, 1], GA[:, 0, :, :], ALU.subtract, [ps2b, GAB], [tb])
        self.act(GA[:, 2, :, :], tv(6), AF.Exp, [tb], [GAB])
        self.tt(GA[:, 5, :, :], GA[:, 4, :, :], GA[:, 1, :, :], ALU.mult, [GAB], [GAB], eng="pool")

        k_ = 0
        F = {}
        for nm in ["pre", "xc", "qs", "ks", "vs", "knT"]:
            F[nm] = (self.scr[k_], self.scrbuf[k_])
            k_ += 1
        F["rs"], F["rinv"], F["diagG"], F["t"], F["tU"], F["tL"], F["t1"] = (
            F["pre"], F["xc"], F["qs"], F["ks"], F["vs"], F["pre"], F["xc"])
        H = [{}, {}, {}]
        for hs_ in range(3):
            for nm in ["szT", "AL"]:
                H[hs_][nm] = (self.scr[k_], self.scrbuf[k_])
                k_ += 1
        assert k_ == 12
        for hs_ in range(3):
            H[hs_]["XU"] = (self.scr[k_], self.scrbuf[k_])
            k_ += 1
        BK = [{}, {}]
        for bs_ in range(2):
            for nm in ["XL", "PU", "XUa", "XUb", "XLa", "XLb", "L", "Y", "EL"]:
                BK[bs_][nm] = (self.scr[k_], self.scrbuf[k_])
                k_ += 1
            b_ = BK[bs_]
            b_["u"], b_["o"], b_["o1a"], b_["o1b"], b_["sq"], b_["on"] = (
                b_["XLb"], b_["XLa"], b_["XUa"], b_["XUb"], b_["Y"], b_["EL"])
        assert k_ <= len(self.scr), k_
        k_ = 0
        Bf = {}
        for nm in ["sqb1", "sqb2", "knTb"]:
            Bf[nm] = (self.scrb[k_], self.scrbbuf[k_])
            k_ += 1
        k_ = 4
        for bs_ in range(2):
            for nm in ["Ubf", "wT", "vn0", "vn1"]:
                BK[bs_][nm] = (self.scrb[k_], self.scrbbuf[k_])
                k_ += 1
        for hs_ in range(3):
            for nm in ["qnT", "kbg", "kdec", "vb", "attnT"]:
                H[hs_][nm] = (self.scrb[k_], self.scrbbuf[k_])
                k_ += 1
        assert k_ <= len(self.scrb), k_
        smalls = [(K["gsmall"], KB["gsmall"]), (K["gsmall1"], KB["gsmall1"])]
        sbfs = [[(K["sbf0"], KB["sbf0"]), (K["sbf1"], KB["sbf1"])], [(K["sbf2"], KB["sbf2"]), (K["sbf3"], KB["sbf3"])]]
        cw = K["gdn_conv_w"]
        wcache = {}

        def prefetch_head(hd):
            wcache[hd] = [self.load_w(d["gdn_in_w"][l, g4 * NH + hd]) for g4 in range(4)]
            dcv, dcvb = K["dconv"], KB["dconv"]
            for idx in range(3):
                for jx in range(4):
                    self.ts(dcv[:, idx * 4 + jx, :], M[:, 0, :], cw[:, l, idx * NH + hd, jx:jx + 1], None,
                            ALU.mult, None, [MB, KB["gdn_conv_w"]], [dcvb], eng="dve")

        def gdn_iter(hd, sub, it_):
            Hh = H[it_ % 3]
            Bk = BK[it_ % 2]
            RB = "B%d" % (it_ % 2)
            small, smallb = smalls[it_ % 2]
            sl = slice(sub * 512, (sub + 1) * 512)
            gb0 = sub * 4

            def gB(q):
                return GA[:, q, gb0:gb0 + 4, hd:hd + 1].to_broadcast([128, 4, 128])

            if hd not in wcache:
                prefetch_head(hd)
            ws = wcache[hd]
            dcv, dcvb = K["dconv"], KB["dconv"]
            pre, preb = F["pre"]
            preh = pre[:, 0:260].bitcast(BF16)

            def inproj(g4):
                ps, psb = self.ps_next("F")
                for kc in range(KC):
                    self.mm(ps[:, :], ws[g4][0][:, kc, :], self.hT[:, kc, sl], kc == 0, kc == KC - 1,
                            [ws[g4][1], self.hb[kc][sub]], [psb])
                    if kc % 4 == 3:
                        yield 0
                return ps, psb

            for idx, dst in enumerate(("qs", "ks", "vs")):
                cb = idx * NH + hd
                psX, psXb = yield from inproj(idx)
                self.copy(preh[:, 0:3], K["gdn_halo"][:, l, cb, :], [KB["gdn_halo"]], [preb], eng="pool")
                self.act(preh[:, 3:515], psX[:, :], AF.Copy, [psXb], [preb])
                self.copy(K["gdn_halo"][:, l, cb, :], preh[:, 512:515], [preb], [KB["gdn_halo"]], eng="pool")
                yield 0
                pc, pcb = self.ps_next("F")
                for jx in range(4):
                    self.mm(pc[:, :], dcv[:, idx * 4 + jx, :], preh[:, jx:jx + 512], jx == 0, jx == 3,
                            [dcvb, preb], [pcb])
                yield 0
                dt_, dtb = F[dst]
                self.act(dt_[:, W], pc[:, :], AF.Silu, [pcb], [dtb])
                yield 0
            psZ, psZb = yield from inproj(3)
            if sub == self.NSUB - 1 and hd + 1 < NH:
                prefetch_head(hd + 1)
                yield 0
            elif sub == self.NSUB - 1:
                self._op_pre = [self.load_w(d["gdn_out_w"][l][j]) for j in range(4)]
                yield 0
            szT, szTb = Hh["szT"]
            self.act(szT[:, W], psZ[:, :], AF.Silu, [psZb], [szTb])
            yield 0
            rs, rsb = F["rs"]
            rinv, rinvb = F["rinv"]
            qnT, qnTb = Hh["qnT"]
            knT, knTb_ = F["knT"]
            knb, knbb = Bf["knTb"]
            for which in ("q", "k"):
                src, srcb = F[which + "s"]
                sq, sqb = Bf["sqb1"] if which == "q" else Bf["sqb2"]
                self.act(sq[:, W], src[:, W], AF.Square, [srcb], [sqb])
                psn, psnb = self.ps_next("F")
                self.mm(psn[:, :], K["ones_bf"][:], sq[:, W], True, True, [KB["ones_bf"], sqb], [psnb])
                yield 0
                self.act(rs[:, W], psn[:, :], AF.Ln, [psnb, KB["epsk"]], [rsb], bias=self.eps_ap)
                self.act(rinv[:, W], rs[:, W], AF.Exp, [rsb], [rinvb], scale=-0.5)
                yield 0
                if which == "q":
                    self.stt(qnT[:, W], src[:, W], 128.0 ** -0.5, rinv[:, W], ALU.mult, ALU.mult,
                             [srcb, rinvb], [qnTb])
                else:
                    self.tt(knT[:, W], src[:, W], rinv[:, W], ALU.mult, [srcb, rinvb], [knTb_])
                    self.act(knb[:, W], knT[:, W], AF.Copy, [knTb_], [knbb])
                yield 0
            vs, vsb = F["vs"]
            kTp, kTpb = self.tr4(knT, knTb_, ring="F")
            vTp, vTpb = self.tr4(vs, vsb, ring="F")
            kbg, kbgb = Hh["kbg"]
            kdec, kdecb = Hh["kdec"]
            vb, vbb = Hh["vb"]
            yield 0
            self.tt(v4(kbg[:, W]), v4(kTp[:, :]), gB(5), ALU.mult, [kTpb, GAB], [kbgb])
            self.tt(v4(kdec[:, W]), v4(kTp[:, :]), gB(2), ALU.mult, [kTpb, GAB], [kdecb])
            self.tt(v4(vb[:, W]), v4(vTp[:, :]), gB(4), ALU.mult, [vTpb, GAB], [vbb])
            yield 0
            dG, dGb = F["diagG"]
            self.tt(v4(dG[:, W]), mask4(0), gB(0), ALU.mult, [MB, GAB], [dGb], eng="pool")
            Gp, Gpb = self.ps_next("F")
            for b in range(4):
                cs = slice(b * 128, (b + 1) * 128)
                self.mm(Gp[:, cs], M[:, 1, :], dG[:, cs], True, True, [MB, dGb], [Gpb])
            yield 0
            t, tb = F["t"]
            self.tt(v4(t[:, W]), v4(Gp[:, :]), gB(0), ALU.subtract, [Gpb, GAB], [tb])
            KKp, KKpb = self.mm4(knb, knbb, knb, knbb, ring="F")
            QKp, QKpb = self.mm4(knb, knbb, qnT, qnTb, ring="F")
            yield 0
            tU, tUb = F["tU"]
            tL, tLb = F["tL"]
            self.tt(v4(tL[:, W]), mask4(3), v4(t[:, W]), ALU.subtract, [tb, MB], [tLb], eng="pool")
            yield 0
            self.act(tL[:, W], tL[:, W], AF.Exp, [tLb], [tLb])
            self.tt(v4(tU[:, W]), v4(t[:, W]), mask4(2), ALU.add, [tb, MB], [tUb], eng="pool")
            self.act(tU[:, W], tU[:, W], AF.Exp, [tUb], [tUb])
            t1, t1b = F["t1"]
            self.tt(v4(t1[:, W]), v4(KKp[:, :]), gB(4), ALU.mult, [KKpb, GAB], [t1b])
            yield 0
            AL, ALb = Hh["AL"]
            self.tt(AL[:, W], t1[:, W], tL[:, W], ALU.mult, [t1b, tLb], [ALb], eng="pool")
            attnT, attnTb = Hh["attnT"]
            self.tt(attnT[:, W], QKp[:, :], tU[:, W], ALU.mult, [QKpb, tUb], [attnTb])
            AUp, AUpb = self.tr4(AL, ALb, ring="F")
            yield 0
            XU, XUb_ = Hh["XU"]
            self.tt(v4(XU[:, W].bitcast(F32R)), v4(AUp[:, :]), mask4(4), ALU.mult, [AUpb, MB], [XUb_])
            yield "HANDOFF"
            XL, XLb_ = Bk["XL"]
            self.tt(v4(XL[:, W].bitcast(F32R)), v4(AL[:, W]), mask4(5), ALU.mult, [ALb, MB], [XLb_])
            PU, PUb = Bk["PU"]
            self.tt(v4(PU[:, W].bitcast(F32R)), v4(XU[:, W]), mask4(0), ALU.add, [XUb_, MB], [PUb], eng="pool")
            curU, curUb = XU, XUb_
            curL, curLb = XL, XLb_
            for s_ in range(4):
                if s_ == 0:
                    pa, pab = self.mm4(curL, curLb, curU, curUb, ring=RB, f32r=True)
                elif s_ < 3:
                    pa, pab = self.mm4(curL, curLb, curU, curUb, ring=RB, f32r=True)
                    pb, pbb = self.mm4(curL, curLb, PU, PUb, ring=RB, f32r=True)
                else:
                    pb, pbb = self.mm4(curL, curLb, PU, PUb, ring=RB, f32r=True)
                yield 0
                if s_ > 0:
                    self.tt(PU[:, W].bitcast(F32R), PU[:, W], pb[:, :], ALU.add, [PUb, pbb], [PUb])
                if s_ == 3:
                    break
                nU, nUb = Bk["XUa"] if s_ % 2 == 0 else Bk["XUb"]
                self.copy(nU[:, W].bitcast(F32R), pa[:, :], [pab], [nUb], eng="act")
                pt, ptb = self.tr4(nU, nUb, ring=RB)
                yield 0
                nL, nLb = Bk["XLa"] if s_ % 2 == 0 else Bk["XLb"]
                self.copy(nL[:, W].bitcast(F32R), pt[:, :], [ptb], [nLb], eng="dve")
                curU, curUb, curL, curLb = nU, nUb, nL, nLb
            Lt, Ltb = Bk["L"]
            pt, ptb = self.tr4(PU, PUb, ring=RB)
            yield 0
            self.copy(Lt[:, W].bitcast(F32R), pt[:, :], [ptb], [Ltb], eng="act")
            Ubf, Ubfb = Bk["Ubf"]
            for k in range(3):
                EL, ELb = Bk["EL"]
                self.tt(v4(EL[:, W].bitcast(F32R)), v4(AL[:, W]), mask4(6 + k), ALU.mult, [ALb, MB], [ELb])
                py, pyb = self.mm4(EL, ELb, PU, PUb, ring=RB, f32r=True)
                yield 0
                Y, Yb = Bk["Y"]
                self.copy(Y[:, W].bitcast(F32R), py[:, :], [pyb], [Yb], eng="act")
                pz, pzb = self.mm4(Lt, Ltb, Y, Yb, ring=RB, f32r=True)
                yield 0
                if k < 2:
                    self.tt(PU[:, W].bitcast(F32R), PU[:, W], pz[:, :], ALU.subtract, [PUb, pzb], [PUb])
                    pt, ptb = self.tr4(PU, PUb, ring=RB)
                    yield 0
                    self.copy(Lt[:, W].bitcast(F32R), pt[:, :], [ptb], [Ltb], eng="act")
                else:
                    self.tt(Ubf[:, W], PU[:, W], pz[:, :], ALU.subtract, [PUb, pzb], [Ubfb])
            pu, pub = self.mm4(Ubf, Ubfb, vb, vbb, ring=RB)
            pw, pwb = self.mm4(kbg, kbgb, Ubf, Ubfb, ring=RB)
            yield 0
            u, ub = Bk["u"]
            self.copy(u[:, W].bitcast(F32R), pu[:, :], [pub], [ub], eng="act")
            wT, wTb = Bk["wT"]
            self.copy(wT[:, W], pw[:, :], [pwb], [wTb], eng="dve")
            S32, S32b = K["gdn_S"][:, l, hd, :], KB["gdn_S"]
            o, ob = Bk["o"]
            sb2 = sbfs[it_ % 2]
            vn2 = [Bk["vn0"], Bk["vn1"]]
            o12 = [Bk["o1a"], Bk["o1b"]]
            self.copy(sb2[0][0][:, :], S32, [S32b], [sb2[0][1]], eng="act")
            for b in range(4):
                cs = slice(b * 128, (b + 1) * 128)
                gb = gb0 + b
                Sbf, Sbfb = sb2[b % 2][0][:, :], sb2[b % 2][1]
                Sbn, Sbnb = sb2[(b + 1) % 2][0][:, :], sb2[(b + 1) % 2][1]
                vnew, vnewb = vn2[b % 2]
                o1s, o1sb = o12[b % 2]
                p1, p1b = self.ps_next(RB)
                self.mm(p1[:, 0:128], wT[:, cs], Sbf, True, True, [wTb, Sbfb], [p1b])
                p1o, p1ob = self.ps_next(RB)
                self.mm(p1o[:, 0:128], qnT[:, cs], Sbf, True, True, [qnTb, Sbfb], [p1ob])
                yield 0
                self.tt(vnew[:, 0:128], u[:, cs], p1[:, 0:128], ALU.subtract, [ub, p1b], [vnewb])
                self.act(o1s[:, 0:128].bitcast(F32R), p1o[:, 0:128], AF.Copy, [p1ob, GAB], [o1sb],
                         scale=GA[:, 1, gb, hd:hd + 1])
                p2, p2b = self.ps_next(RB)
                self.mm(p2[:, 0:128], kdec[:, cs], vnew[:, 0:128], True, True, [kdecb, vnewb], [p2b])
                p2o, p2ob = self.ps_next(RB)
                self.mm(p2o[:, 0:128], attnT[:, cs], vnew[:, 0:128], True, True, [attnTb, vnewb], [p2ob])
                yield 0
                self.stt(S32, S32, GA[:, 3, gb, hd:hd + 1], p2[:, 0:128], ALU.mult, ALU.add,
                         [S32b, GAB, p2b], [S32b])
                self.tt(o[:, cs].bitcast(F32R), p2o[:, 0:128], o1s[:, 0:128], ALU.add, [p2ob, o1sb], [ob])
                if b < 3:
                    self.copy(Sbn, S32, [S32b], [Sbnb], eng="act")
            self._xch_flush()
            sq, sqb_ = Bk["sq"]
            self.tt(sq[:, W].bitcast(F32R), o[:, W], o[:, W], ALU.mult, [ob], [sqb_], eng="pool")
            self.sch.op("dve", lambda E, o_=small[:, 0:4], i_=v4(sq[:, W]):
                        E.tensor_reduce(out=o_, in_=i_, axis=mybir.AxisListType.X, op=ALU.add), [sqb_], [smallb])
            self.act(small[:, 4:8], small[:, 0:4], AF.Ln, [smallb, KB["epsk"]], [smallb], bias=self.eps_ap,
                     scale=1.0 / 128)
            self.act(small[:, 8:12], small[:, 4:8], AF.Exp, [smallb], [smallb], scale=-0.5)
            yield 0
            on, onb = Bk["on"]
            self.tt(v4(on[:, W].bitcast(F32R)), v4(o[:, W]), small[:, 8:12].unsqueeze(2).to_broadcast([128, 4, 128]), ALU.mult,
                    [ob, smallb], [onb])
            po, pob = self.tr4(on, onb, ring=RB)
            yield 0
            szT, szTb = Hh["szT"]
            self.stt(self.yTl[:, hd, sl], po[:, :], K["gdn_onorm_g"][:, l:l + 1], szT[:, W], ALU.mult, ALU.mult,
                     [pob, KB["gdn_onorm_g"], szTb], [self.ylb[hd][sub]])
            if sub == self.NSUB - 1:
                self.exchange_part(hd)

        prefetch_head(0)
        self.run_pipelined3([gdn_iter(hd, sub, hd * self.NSUB + sub)
                             for hd in range(NH) for sub in range(self.NSUB)], "gdn")
        self.outproj_phase(li, d["gdn_out_w"][l])


def _prep_inputs(inp, b, r=0):
    f = np.float32
    c = np.ascontiguousarray
    bl = slice(r * NBK, (r + 1) * NBK)

    def pm(v):
        v = np.asarray(v, f)
        sh = v.shape[:-1]
        v = v.reshape(sh + (KC, 128))
        return c(np.moveaxis(v, -1, 0))

    def tile_w(w):
        w = np.asarray(w, f)
        lead = w.shape[:-2]
        nblk = w.shape[-1] // 128
        w = w.reshape(lead + (KC, 128, nblk, 128))
        nd = len(lead)
        return c(w.transpose(tuple(range(nd)) + (nd + 2, nd + 1, nd, nd + 3)))

    m = {}
    m["xT"] = c(np.asarray(inp["x"][b], f).T)
    m["cT"] = c(np.asarray(inp["c"][b], f).reshape(KC, 128).T)
    m["ada_w"] = tile_w(np.asarray(inp["ada_w"], f)[:, :, r * 1536:(r + 1) * 1536])
    m["ada_b"] = c(np.moveaxis(np.asarray(inp["ada_b"], f).reshape(4, 24, 128), -1, 0)[:, :, r * 12:(r + 1) * 12])
    m["norm_g"] = pm(inp["norm_g"])
    m["final_g"] = pm(inp["final_g"])
    w = np.asarray(inp["lru_in_w"], f).reshape(2, D, 2, KC, 128)
    m["lru_in_w"] = tile_w(w[:, :, :, bl, :].reshape(2, D, 2 * NBK * 128))
    cw = np.asarray(inp["lru_conv_w"], f).reshape(2, 4, KC, 128)
    m["lru_conv_w"] = c(cw.transpose(3, 0, 2, 1)[:, :, bl, :])
    m["lru_conv_b"] = c(pm(inp["lru_conv_b"])[:, :, bl])
    m["lru_gate_w"] = c(np.asarray(inp["lru_gate_w"], f)[:, :, bl])
    m["lru_gate_b"] = c(pm(inp["lru_gate_b"])[:, :, :, bl])
    m["lru_lambda"] = c(pm(inp["lru_lambda"])[:, :, bl])
    m["lru_out_w"] = tile_w(inp["lru_out_w"])
    gw_ = np.asarray(inp["gdn_in_w"], f)
    qkvz = gw_[:, :, :4096].reshape(2, D, 4, KC, 128)[:, :, :, bl, :].reshape(2, D, 4 * NBK * 128)
    a_ = gw_[:, :, 4096:4104][:, :, bl]
    b_ = gw_[:, :, 4104:4112][:, :, bl]
    m["gdn_in_w"] = tile_w(qkvz)
    ab = np.concatenate([a_, b_], axis=2)
    m["gdn_ab_w"] = c(ab.reshape(2, KC, 128, 2 * NBK).transpose(0, 2, 1, 3))
    gc = np.asarray(inp["gdn_conv_w"], f).reshape(2, 4, 3, KC, 128)
    gc = gc.transpose(4, 0, 2, 3, 1)[:, :, :, bl, :]
    m["gdn_conv_w"] = c(gc.reshape(128, 2, 3 * NBK, 4))
    m["gdn_a_log"] = c(np.broadcast_to(np.asarray(inp["gdn_a_log"], f)[None][:, :, bl], (128, 2, NBK)))
    m["gdn_dt_bias"] = c(np.broadcast_to(np.asarray(inp["gdn_dt_bias"], f)[None][:, :, bl], (128, 2, NBK)))
    m["gdn_onorm_g"] = c(np.asarray(inp["gdn_onorm_g"], f).T)
    m["gdn_out_w"] = tile_w(inp["gdn_out_w"])
    m["masks"] = _masks()
    return m


def _masks():
    i = np.arange(128)[:, None]
    j = np.arange(128)[None, :]
    M = np.zeros((128, NMASK, 128), np.float32)
    M[:, 0] = (i == j)
    M[:, 1] = 1.0
    M[:, 2] = np.where(j >= i, 0.0, -30000.0)
    M[:, 3] = np.where(i > j, 0.0, -30000.0)
    same16 = (i // 16) == (j // 16)
    M[:, 4] = -1.0 * ((i < j) & same16)
    M[:, 5] = -1.0 * ((i > j) & same16)
    for k, b in enumerate((16, 32, 64)):
        same_b = (i // b) == (j // b)
        same_2b = (i // (2 * b)) == (j // (2 * b))
        M[:, 6 + k] = (i > j) & same_2b & ~same_b
    M[:, 9] = (i <= j)
    return M


_PROG = {}


def kernel(**inputs):
    key = "full"
    if key not in _PROG:
        _PROG[key] = Prog()
    prog = _PROG[key]
    in_maps = [_prep_inputs(inputs, core // 2, core % 2) for core in range(NCORES)]
    res = run_bass_kernel_spmd(prog.nc, in_maps, core_ids=list(range(NCORES)))
    out = np.stack([np.ascontiguousarray(res.results[2 * b]["yT"].T) for b in range(BATCH)], axis=0)
    return out.astype(np.float32)
```
